# Optimizing a Trainium2 kernel written in Bass

```python
import math
import jax, jax.numpy as jnp
from jax import lax
import numpy as np

D_MODEL = 2048
BATCH = 16
SEQ = 256
DEPTH = 4
DEC_BATCH = 4
DEC_SEQ = 1024
PAST_LEN = 512

GRID_W = 64
RG_WIDTH = 1024
RG_HEADS = 8
RG_BLOCK = RG_WIDTH // RG_HEADS
RG_CONV = 4
RG_C = 8.0
S5_WIDTH = 1024
S5_GROUP_CH = 16
S5_GROUPS = S5_WIDTH // S5_GROUP_CH
S5_STATE = 64
NA_HEADS = 8
NA_HEAD_DIM = 128
NA_WIDTH = NA_HEADS * NA_HEAD_DIM
NA_KH = 8
NA_KW = 16
NA_QB_W = 16
NA_KB_W = 32
NA_SCALE = NA_HEAD_DIM ** -0.5
Q_BLOCK = 128
NEG_INF = -1e30
N_BRANCH = 3
FF_HIDDEN = (8 * D_MODEL + 3 * 256 - 1) // (3 * 256) * 256
RMS_EPS = 1e-6
OFF_RG_X = 0
OFF_RG_G = OFF_RG_X + RG_WIDTH
OFF_S5 = OFF_RG_G + RG_WIDTH
OFF_Q = OFF_S5 + S5_WIDTH
OFF_K = OFF_Q + NA_WIDTH
OFF_V = OFF_K + NA_WIDTH
OFF_GATE = OFF_V + NA_WIDTH
IN_WIDTH = OFF_GATE + N_BRANCH * D_MODEL

kernel_name = "hybrid_rglru_s5_natten_prefix_dit_step"


def rms_norm(x, g):
    xf = x.astype(jnp.float32)
    y = xf * lax.rsqrt(jnp.mean(xf * xf, axis=-1, keepdims=True) + RMS_EPS)
    return (y * g.astype(jnp.float32)).astype(x.dtype)


def modulation(cond, w_mod, b_mod):
    m = jax.nn.silu(cond) @ w_mod + b_mod
    return jnp.split(m, 6, axis=-1)


def linear_scan(a, b, h0, reverse):
    if reverse:
        b = b.at[:, -1].add(a[:, -1] * h0)
    else:
        b = b.at[:, 0].add(a[:, 0] * h0)

    def combine(l, r):
        return (l[0] * r[0], r[0] * l[1] + r[1])

    _, h = lax.associative_scan(combine, (a, b), reverse=reverse, axis=1)
    return h


def depthwise_conv_centred(x, w, b):
    k = w.shape[0]
    pad_l = (k - 1) // 2
    y = lax.conv_general_dilated(x, w[:, None, :], window_strides=(1,), padding=[(pad_l, k - 1 - pad_l)],
                                 dimension_numbers=('NWC', 'WIO', 'NWC'), feature_group_count=x.shape[-1])
    return y + b


def rglru_branch(xr, xg, conv_w, conv_b, gate_w, gate_b, lam, h0, return_state):
    f32 = jnp.float32
    x = depthwise_conv_centred(xr, conv_w, conv_b)
    bsz, length, _ = x.shape
    xh = x.reshape(bsz, length, RG_HEADS, RG_BLOCK)
    gl = jnp.einsum('blhi,dghij->dgblhj', xh, gate_w).reshape(2, 2, bsz, length, RG_WIDTH)
    gates = jax.nn.sigmoid((gl + gate_b[:, :, None, None, :]).astype(f32))
    r, i = gates[:, 0], gates[:, 1]
    log_a = -RG_C * r * jax.nn.softplus(-lam.astype(f32))[:, None, None, :]
    a = jnp.exp(log_a)
    bx = jnp.sqrt(-jnp.expm1(2.0 * log_a)) * i * x.astype(f32)[None]
    h_f = linear_scan(a[0], bx[0], h0[:, 0], reverse=False)
    h_b = linear_scan(a[1], bx[1], h0[:, 1], reverse=True)
    y = ((h_f + h_b) * jax.nn.gelu(xg.astype(f32))).astype(xr.dtype)
    if return_state:
        return y, jnp.stack([h_f[:, -1], h_b[:, 0]], axis=1)
    return y


def s5_branch(u, lam_re, lam_im, log_dt, b_re, b_im, c_re, c_im, d, w_glu, h0, return_state):
    f32 = jnp.float32
    bsz, length, _ = u.shape
    uc = u.astype(f32).reshape(bsz, length, S5_GROUPS, S5_GROUP_CH).astype(jnp.complex64)
    lam = lax.complex(lam_re.astype(f32), lam_im.astype(f32))
    dt = jnp.exp(log_dt.astype(f32))[..., None]
    lam_bar = jnp.exp(lam * dt)
    b_bar = ((lam_bar - 1.0) / lam)[..., None] * lax.complex(b_re.astype(f32), b_im.astype(f32))
    bu = jnp.einsum('blgn,dgpn->dblgp', uc, b_bar)
    h_f = linear_scan(jnp.broadcast_to(lam_bar[0], bu.shape[1:]), bu[0], h0[:, 0], reverse=False)
    h_b = linear_scan(jnp.broadcast_to(lam_bar[1], bu.shape[1:]), bu[1], h0[:, 1], reverse=True)
    c_mat = lax.complex(c_re.astype(f32), c_im.astype(f32))
    y = jnp.real(jnp.einsum('blgp,gnp->blgn', h_f, c_mat[0]) + jnp.einsum('blgp,gnp->blgn', h_b, c_mat[1]))
    y = y.reshape(bsz, length, S5_WIDTH) + d.astype(f32) * u.astype(f32)
    z = jax.nn.gelu(y)
    out = (z * jax.nn.sigmoid(z @ w_glu.astype(f32))).astype(u.dtype)
    if return_state:
        return out, jnp.stack([h_f[:, -1], h_b[:, 0]], axis=1)
    return out


def context_attention(q, k, v):
    bsz, length, h, dh = q.shape
    qb = q.reshape(bsz, length // Q_BLOCK, Q_BLOCK, h, dh).transpose(1, 0, 2, 3, 4)

    def one_block(qi):
        s = jnp.einsum('bqhd,bkhd->bhqk', qi, k).astype(jnp.float32)
        p = jax.nn.softmax(s, axis=-1).astype(v.dtype)
        return jnp.einsum('bhqk,bkhd->bqhd', p, v)

    o = lax.map(one_block, qb)
    return o.transpose(1, 0, 2, 3, 4).reshape(bsz, length, h * dh)


def _na_tables(rows):
    kh = min(NA_KH, rows)
    ncb = GRID_W // NA_QB_W
    r = np.arange(rows)
    row_start = np.clip(r - kh // 2, 0, rows - kh)
    key_rows = row_start[:, None] + np.arange(kh)[None, :]
    blk_start = np.clip(np.arange(ncb) * NA_QB_W - NA_KW // 2, 0, GRID_W - NA_KB_W)
    key_cols = blk_start[:, None] + np.arange(NA_KB_W)[None, :]
    q_cols = np.arange(ncb)[:, None] * NA_QB_W + np.arange(NA_QB_W)[None, :]
    win_start = np.clip(q_cols - NA_KW // 2, 0, GRID_W - NA_KW)
    in_win = (key_cols[:, None, :] >= win_start[:, :, None]) & (key_cols[:, None, :] < win_start[:, :, None] + NA_KW)
    key_idx = (key_rows[:, None, :, None] * GRID_W + key_cols[None, :, None, :]).reshape(rows, ncb, kh * NA_KB_W)
    row_part = (key_rows - r[:, None] + NA_KH - 1) * (2 * NA_KW - 1)
    col_part = np.clip(key_cols[:, None, :] - q_cols[:, :, None], -(NA_KW - 1), NA_KW - 1) + NA_KW - 1
    bias_idx = (row_part[:, None, None, :, None] + col_part[None, :, :, None, :]).reshape(rows, ncb, NA_QB_W, kh * NA_KB_W)
    mask = np.broadcast_to(in_win[None, :, :, None, :], (rows, ncb, NA_QB_W, kh, NA_KB_W)).reshape(rows, ncb, NA_QB_W, kh * NA_KB_W)
    return jnp.asarray(key_idx, jnp.int32), jnp.asarray(bias_idx, jnp.int32), jnp.asarray(mask)


def neighbourhood_attention(q, k, v, k_ctx, v_ctx, rpb, rows):
    bsz, length, h, dh = q.shape
    ncb = GRID_W // NA_QB_W
    key_idx, bias_idx, mask = _na_tables(rows)
    rpb_flat = rpb.reshape(h, -1)
    q_rows = q.reshape(bsz, rows, ncb, NA_QB_W, h, dh).transpose(1, 0, 2, 3, 4, 5)

    def row_step(args):
        q_r, kidx, bidx, msk = args
        k_loc = k[:, kidx]
        v_loc = v[:, kidx]
        s_loc = jnp.einsum('bjqhd,bjkhd->bhjqk', q_r, k_loc).astype(jnp.float32)
        s_loc = jnp.where(msk, s_loc + rpb_flat[:, bidx].astype(jnp.float32), NEG_INF)
        s_ctx = jnp.einsum('bjqhd,bchd->bhjqc', q_r, k_ctx).astype(jnp.float32)
        p = jax.nn.softmax(jnp.concatenate([s_loc, s_ctx], axis=-1), axis=-1).astype(v.dtype)
        kb = kidx.shape[-1]
        return (jnp.einsum('bhjqk,bjkhd->bjqhd', p[..., :kb], v_loc)
                + jnp.einsum('bhjqc,bchd->bjqhd', p[..., kb:], v_ctx))

    o = lax.map(row_step, (q_rows, key_idx, bias_idx, mask))
    return o.transpose(1, 0, 2, 3, 4, 5).reshape(bsz, length, h * dh)


def mixer_proj(x, g_pre, shift, scale, w_in):
    proj = (rms_norm(x, g_pre) * (1 + scale) + shift) @ w_in
    bsz, length, _ = proj.shape
    xr = proj[..., OFF_RG_X:OFF_RG_G]
    xg = proj[..., OFF_RG_G:OFF_S5]
    u = proj[..., OFF_S5:OFF_Q]
    q = proj[..., OFF_Q:OFF_K].reshape(bsz, length, NA_HEADS, NA_HEAD_DIM) * NA_SCALE
    k = proj[..., OFF_K:OFF_V].reshape(bsz, length, NA_HEADS, NA_HEAD_DIM)
    v = proj[..., OFF_V:OFF_GATE].reshape(bsz, length, NA_HEADS, NA_HEAD_DIM)
    return xr, xg, u, q, k, v, proj[..., OFF_GATE:]


def merge_branches(x, gate_logits, y_rg, y_s5, y_na, w_rg_out, w_s5_out, w_na_out, w_o, g_post, gate):
    g = jax.nn.sigmoid(gate_logits.astype(jnp.float32)).astype(x.dtype)
    g_rg, g_s5, g_na = jnp.split(g, N_BRANCH, axis=-1)
    m = g_rg * (y_rg @ w_rg_out) + g_s5 * (y_s5 @ w_s5_out) + g_na * (y_na @ w_na_out)
    return x + gate * rms_norm(m @ w_o, g_post)


def ffn_sublayer(x, g_pre, g_post, shift, scale, gate, w_ffn_in, w_ffn_out):
    h = rms_norm(x, g_pre) * (1 + scale) + shift
    a, b = jnp.split(h @ w_ffn_in, 2, axis=-1)
    return x + gate * rms_norm((jax.nn.silu(a) * b) @ w_ffn_out, g_post)


def setup_inputs(seed: int = 0) -> dict:
    key = jax.random.key(seed)
    ks = iter(jax.random.split(key, 48))
    D = D_MODEL

    def nrm(shape, s):
        return jax.random.normal(next(ks), shape, jnp.float32) * s

    x_prompt = nrm((BATCH, SEQ, D), 1.0)
    x_sample = nrm((DEC_BATCH, DEC_SEQ, D), 1.0)
    cache_na_k = nrm((DEC_BATCH, DEPTH, PAST_LEN, NA_HEADS, NA_HEAD_DIM), 1.0)
    cache_na_v = nrm((DEC_BATCH, DEPTH, PAST_LEN, NA_HEADS, NA_HEAD_DIM), 1.0)
    state_rglru = nrm((DEC_BATCH, DEPTH, 2, RG_WIDTH), 1.0)
    state_s5 = nrm((DEC_BATCH, DEPTH, 2, S5_GROUPS, S5_STATE, 2), 0.1)
    c = nrm((DEC_BATCH, D), 1.0)
    c_ctx = nrm((D,), 1.0)
    w_mod = nrm((DEPTH, D, 6 * D), 0.5 * D ** -0.5)
    b_mod = nrm((DEPTH, 6 * D), 0.02)
    g_mix_pre = 1.0 + nrm((DEPTH, D), 0.02)
    g_mix_post = 1.0 + nrm((DEPTH, D), 0.02)
    g_ffn_pre = 1.0 + nrm((DEPTH, D), 0.02)
    g_ffn_post = 1.0 + nrm((DEPTH, D), 0.02)
    w_in = nrm((DEPTH, D, IN_WIDTH), D ** -0.5)
    rg_conv_w = nrm((DEPTH, RG_CONV, RG_WIDTH), RG_CONV ** -0.5)
    rg_conv_b = nrm((DEPTH, RG_WIDTH), 0.02)
    rg_gate_w = nrm((DEPTH, 2, 2, RG_HEADS, RG_BLOCK, RG_BLOCK), RG_BLOCK ** -0.5)
    rg_gate_b = nrm((DEPTH, 2, 2, RG_WIDTH), 0.02)
    a_pow = jax.random.uniform(next(ks), (DEPTH, 2, RG_WIDTH), jnp.float32, 0.9, 0.999)
    a_base = a_pow ** (1.0 / RG_C)
    rg_lambda = jnp.log(a_base) - jnp.log1p(-a_base)
    s5_lambda_re = -0.5 + nrm((DEPTH, 2, S5_GROUPS, S5_STATE), 0.01)
    s5_lambda_im = math.pi * jnp.arange(S5_STATE, dtype=jnp.float32) + nrm((DEPTH, 2, S5_GROUPS, S5_STATE), 0.01)
    s5_log_dt = jax.random.uniform(next(ks), (DEPTH, 2, S5_GROUPS), jnp.float32, math.log(1e-3), math.log(1e-1))
    s5_b_re = nrm((DEPTH, 2, S5_GROUPS, S5_STATE, S5_GROUP_CH), (2 * S5_GROUP_CH) ** -0.5)
    s5_b_im = nrm((DEPTH, 2, S5_GROUPS, S5_STATE, S5_GROUP_CH), (2 * S5_GROUP_CH) ** -0.5)
    s5_c_re = nrm((DEPTH, 2, S5_GROUPS, S5_GROUP_CH, S5_STATE), S5_STATE ** -0.5)
    s5_c_im = nrm((DEPTH, 2, S5_GROUPS, S5_GROUP_CH, S5_STATE), S5_STATE ** -0.5)
    s5_d = nrm((DEPTH, S5_WIDTH), 1.0)
    s5_w_glu = nrm((DEPTH, S5_WIDTH, S5_WIDTH), S5_WIDTH ** -0.5)
    na_rpb = nrm((DEPTH, NA_HEADS, 2 * NA_KH - 1, 2 * NA_KW - 1), 0.02)
    w_rg_out = nrm((DEPTH, RG_WIDTH, D), RG_WIDTH ** -0.5)
    w_s5_out = nrm((DEPTH, S5_WIDTH, D), S5_WIDTH ** -0.5)
    w_na_out = nrm((DEPTH, NA_WIDTH, D), NA_WIDTH ** -0.5)
    w_o = nrm((DEPTH, D, D), D ** -0.5)
    w_ffn_in = nrm((DEPTH, D, 2 * FF_HIDDEN), D ** -0.5)
    w_ffn_out = nrm((DEPTH, FF_HIDDEN, D), FF_HIDDEN ** -0.5)
    return {"x_prompt": x_prompt, "x_sample": x_sample, "cache_na_k": cache_na_k, "cache_na_v": cache_na_v,
            "state_rglru": state_rglru, "state_s5": state_s5, "c": c, "c_ctx": c_ctx,
            "w_mod": w_mod, "b_mod": b_mod, "g_mix_pre": g_mix_pre, "g_mix_post": g_mix_post,
            "g_ffn_pre": g_ffn_pre, "g_ffn_post": g_ffn_post, "w_in": w_in,
            "rg_conv_w": rg_conv_w, "rg_conv_b": rg_conv_b, "rg_gate_w": rg_gate_w, "rg_gate_b": rg_gate_b,
            "rg_lambda": rg_lambda, "s5_lambda_re": s5_lambda_re, "s5_lambda_im": s5_lambda_im,
            "s5_log_dt": s5_log_dt, "s5_b_re": s5_b_re, "s5_b_im": s5_b_im, "s5_c_re": s5_c_re,
            "s5_c_im": s5_c_im, "s5_d": s5_d, "s5_w_glu": s5_w_glu, "na_rpb": na_rpb,
            "w_rg_out": w_rg_out, "w_s5_out": w_s5_out, "w_na_out": w_na_out, "w_o": w_o,
            "w_ffn_in": w_ffn_in, "w_ffn_out": w_ffn_out}


def reference(x_prompt, x_sample, cache_na_k, cache_na_v, state_rglru, state_s5, c, c_ctx,
              w_mod, b_mod, g_mix_pre, g_mix_post, g_ffn_pre, g_ffn_post, w_in,
              rg_conv_w, rg_conv_b, rg_gate_w, rg_gate_b, rg_lambda,
              s5_lambda_re, s5_lambda_im, s5_log_dt, s5_b_re, s5_b_im, s5_c_re, s5_c_im, s5_d, s5_w_glu,
              na_rpb, w_rg_out, w_s5_out, w_na_out, w_o, w_ffn_in, w_ffn_out):
    f32 = jnp.float32
    rows = x_sample.shape[1] // GRID_W
    bp = x_prompt.shape[0]
    s5_cache = lax.complex(state_s5[..., 0].astype(f32), state_s5[..., 1].astype(f32))
    rg_zero = jnp.zeros((bp, 2, RG_WIDTH), f32)
    s5_zero = jnp.zeros((bp, 2, S5_GROUPS, S5_STATE), jnp.complex64)
    xp, xs = x_prompt, x_sample
    ks, vs, rgs, s5s = [], [], [], []
    for l in range(DEPTH):
        rg_p = (rg_conv_w[l], rg_conv_b[l], rg_gate_w[l], rg_gate_b[l], rg_lambda[l])
        s5_p = (s5_lambda_re[l], s5_lambda_im[l], s5_log_dt[l], s5_b_re[l], s5_b_im[l],
                s5_c_re[l], s5_c_im[l], s5_d[l], s5_w_glu[l])
        out_p = (w_rg_out[l], w_s5_out[l], w_na_out[l], w_o[l])

        mp = modulation(c_ctx[None, None, :], w_mod[l], b_mod[l])
        xr, xg, u, q, k, v, gl = mixer_proj(xp, g_mix_pre[l], mp[0], mp[1], w_in[l])
        y_rg, st_rg = rglru_branch(xr, xg, *rg_p, rg_zero, True)
        y_s5, st_s5 = s5_branch(u, *s5_p, s5_zero, True)
        y_na = context_attention(q, k, v)
        xp = merge_branches(xp, gl, y_rg, y_s5, y_na, *out_p, g_mix_post[l], mp[2])
        xp = ffn_sublayer(xp, g_ffn_pre[l], g_ffn_post[l], mp[3], mp[4], mp[5], w_ffn_in[l], w_ffn_out[l])
        ks.append(k)
        vs.append(v)
        rgs.append(st_rg)
        s5s.append(jnp.stack([jnp.real(st_s5), jnp.imag(st_s5)], axis=-1))

        ms = modulation(c[:, None, :], w_mod[l], b_mod[l])
        xr, xg, u, q, k, v, gl = mixer_proj(xs, g_mix_pre[l], ms[0], ms[1], w_in[l])
        y_rg = rglru_branch(xr, xg, *rg_p, state_rglru[:, l], False)
        y_s5 = s5_branch(u, *s5_p, s5_cache[:, l], False)
        y_na = neighbourhood_attention(q, k, v, cache_na_k[:, l], cache_na_v[:, l], na_rpb[l], rows)
        xs = merge_branches(xs, gl, y_rg, y_s5, y_na, *out_p, g_mix_post[l], ms[2])
        xs = ffn_sublayer(xs, g_ffn_pre[l], g_ffn_post[l], ms[3], ms[4], ms[5], w_ffn_in[l], w_ffn_out[l])

    new_na_k = jnp.stack(ks, axis=1)
    new_na_v = jnp.stack(vs, axis=1)
    new_rglru = jnp.stack(rgs, axis=1)
    new_s5 = jnp.stack(s5s, axis=1)
    return (xp, xs, new_na_k, new_na_v, new_rglru, new_s5)
```

```python
import math
import numpy as np
import concourse.bass as bass
import concourse.mybir as mybir
from concourse.bass_utils import run_bass_kernel_spmd

F32 = mybir.dt.float32
BF16 = mybir.dt.bfloat16
AF = mybir.ActivationFunctionType
ALU = mybir.AluOpType
AX = mybir.AxisListType

D = 2048
DEPTH = 4
NT = 1024
FF = 5632
NEG = -1e30
TWO_PI = 2.0 * math.pi


class Sched:
    ENG = ['pe', 'act', 'dve', 'pool', 'sp']

    def __init__(self, nc, n_dma_sems=32):
        self.nc = nc
        self.eng = dict(pe=nc.tensor, act=nc.scalar, dve=nc.vector, pool=nc.gpsimd, sp=nc.sync)
        self.sem = {e: nc.alloc_semaphore("cnt_" + e) for e in self.ENG}
        self.cnt = {e: 0 for e in self.ENG}
        self.seen = {e: {} for e in self.ENG}
        self.last_w = {}
        self.readers = {}
        self.dma_sems = [nc.alloc_semaphore("dma%d" % i) for i in range(n_dma_sems)]
        self.dma_tot = [0] * n_dma_sems
        self.dma_rr = 0
        self.out_tokens = []
        self.fence_tokens = []
        self.n_ins = 0

    def _wait(self, e, tok):
        sem, val = tok
        key = id(sem)
        if self.seen[e].get(key, 0) >= val:
            return
        self.seen[e][key] = val
        self.eng[e].wait_ge(sem, val)
        self.n_ins += 1

    def _deps(self, e, reads, writes):
        toks = []
        for k in reads:
            t = self.last_w.get(k)
            if t is not None:
                toks.append(t)
        for k in writes:
            t = self.last_w.get(k)
            if t is not None:
                toks.append(t)
            toks.extend(self.readers.get(k, ()))
        for t in toks:
            if e == 'pe' and t[0] is self.sem['pe']:
                continue
            self._wait(e, t)

    def _commit(self, tok, reads, writes):
        for k in reads:
            self.readers.setdefault(k, []).append(tok)
        for k in writes:
            self.last_w[k] = tok
            self.readers[k] = []

    def op(self, e, fn, reads=(), writes=()):
        self._deps(e, reads, writes)
        ins = fn(self.eng[e])
        self.cnt[e] += 1
        ins.then_inc(self.sem[e], 1)
        tok = (self.sem[e], self.cnt[e])
        self._commit(tok, reads, writes)
        self.n_ins += 1
        return tok

    def dma(self, q, out, in_, reads=(), writes=(), is_output=False, fence=True, **kw):
        k = self.dma_rr
        self.dma_rr = (self.dma_rr + 1) % len(self.dma_sems)
        sem = self.dma_sems[k]
        if self.dma_tot[k] > 0:
            self._wait(q, (sem, self.dma_tot[k]))
        self._deps(q, reads, writes)
        ins = self.eng[q].dma_start(out=out, in_=in_, **kw)
        self.dma_tot[k] += 16
        ins.then_inc(sem, 16)
        tok = (sem, self.dma_tot[k])
        self._commit(tok, reads, writes)
        if is_output:
            self.out_tokens.append(tok)
        if fence:
            self.fence_tokens.append(tok)
        self.n_ins += 1
        return tok

    def barrier(self):
        for e in self.ENG:
            for f in self.ENG:
                if f != e and self.cnt[f] > 0:
                    self._wait(e, (self.sem[f], self.cnt[f]))
            for t in self.fence_tokens:
                self._wait(e, t)
        self.fence_tokens = []

    def finish(self, e='sp'):
        for t in self.out_tokens:
            self._wait(e, t)
        for f in self.ENG:
            if self.cnt[f] > 0 and f != e:
                self._wait(e, (self.sem[f], self.cnt[f]))


class _Stop(Exception):
    pass


def build(NL=DEPTH, STOP=None, LW=DEPTH, DBG=False):
    nc = bass.Bass("TRN2", target_bir_lowering=False)

    def ckpt(k):
        if STOP is not None and k == STOP:
            raise _Stop()

    def din(name, shape):
        return nc.dram_tensor(name, list(shape), F32, kind="ExternalInput").ap()

    def dout(name, shape):
        return nc.dram_tensor(name, list(shape), F32, kind="ExternalOutput").ap()

    xin = din("xin", [NT, D])
    cond = din("cond", [16, 128])
    w_mod = din("w_mod", [LW, D, 6 * D])
    b_mod = din("b_mod", [LW, 96, 128])
    gvec = din("gvec", [LW, 64, 128])
    w_in = din("w_in", [LW, D, 12288])
    rgv = din("rgv", [LW, 96, 128])
    rg_gate_w = din("rg_gate_w", [LW, 32, 128, 128])
    s5_lam = din("s5_lam", [LW, 128, 128])
    s5_ldt = din("s5_ldt", [LW, 64, 2])
    s5_b = din("s5_b", [LW, 2, 2, 64, 64, 16])
    s5_c = din("s5_c", [LW, 2, 2, 64, 16, 64])
    s5_w_glu = din("s5_w_glu", [LW, 1024, 1024])
    tp = din("tp", [LW, 8, 15, 64, 64])
    w_bo = din("w_bo", [LW, 3, 1024, D])
    w_o = din("w_o", [LW, D, D])
    w_ffn_in = din("w_ffn_in", [LW, D, 2 * FF])
    w_ffn_out = din("w_ffn_out", [LW, FF, D])
    ck = din("ck", [LW, 512, 1024])
    cv = din("cv", [LW, 512, 1024])
    st_rg = din("st_rg", [LW, 16, 128])
    st_s5 = din("st_s5", [LW, 64, 256])
    amask = din("amask", [8, 128, 640])
    flags = din("flags", [128, 4])
    ident_d = din("ident", [128, 128])
    iota_d = din("iota1", [128, NT])

    y_out = dout("y", [NT, D])
    nk_out = dout("nk", [LW, NT, 1024])
    nv_out = dout("nv", [LW, NT, 1024])
    nrg_out = dout("nrg", [LW, 2, 4, 8, 128])
    ns5_out = dout("ns5", [LW, 2, 32, 4, 256])
    xs = nc.dram_tensor("xs", [16, 128, NT], F32).ap()

    s = Sched(nc)
    dbg_seen = set()

    def dbg(name, ap, keys):
        if not DBG or name in dbg_seen:
            return
        dbg_seen.add(name)
        dt_ = nc.dram_tensor("dbg_" + name, list(ap.shape), ap.dtype, kind="ExternalOutput").ap()
        s.dma('sp', dt_, ap, reads=keys, is_output=True)

    TOT = 52400
    big = nc.alloc_sbuf_tensor("big", [128, TOT], F32)
    cur = [0]

    def carve(nwords):
        a = cur[0]
        cur[0] += nwords
        assert cur[0] <= TOT, cur[0]
        return a

    def f32v(off, n):
        return big[:, off:off + n]

    def bf16v(off, nwords):
        return big[:, off:off + nwords].bitcast(BF16)

    oW = [carve(4096), carve(4096)]
    oXN = carve(8192)
    oT = carve(16384)
    oM = carve(8192)
    oYB = carve(4096)
    oMASK = carve(2560)
    oPAR = carve(512)
    oRSTD = carve(1024)
    oONES = carve(128)
    oIOTA = carve(1024)
    oIDF = carve(128)
    oIDB = carve(64)
    oSMALL = carve(512)
    oSTG = carve(256)
    oXST = carve(1024)
    oCOND = carve(16)
    oONESB = carve(64)

    Wb = [bf16v(o, 4096) for o in oW]
    XN = bf16v(oXN, 8192).rearrange("p (c t) -> p c t", t=NT)
    Tf = f32v(oT, 16384)
    Tb = bf16v(oT, 16384)
    Mb = bf16v(oM, 8192).rearrange("p (c t) -> p c t", t=NT)
    HID = bf16v(oT, 16384 + 8192).rearrange("p (c t) -> p c t", t=NT)[:, 0:44, :]
    YB = bf16v(oYB, 4096).rearrange("p (c t) -> p c t", t=NT)
    MASK = bf16v(oMASK, 2560).rearrange("p (i k) -> p i k", k=640)
    PAR = f32v(oPAR, 512)
    RSTD = f32v(oRSTD, 1024)
    ONES = f32v(oONES, 128)
    IOTA = f32v(oIOTA, 1024)
    IDF = f32v(oIDF, 128)
    IDB = bf16v(oIDB, 64)
    SM = f32v(oSMALL, 512)
    GW = Wb[1][:, 0:4096].rearrange("p (a j) -> p a j", j=128)
    STG = f32v(oSTG, 256)
    XST = f32v(oXST, 1024)
    CONDT = f32v(oCOND, 16)

    pa = [nc.alloc_psum_tensor("pa0", [128, 1024], F32), nc.alloc_psum_tensor("pa1", [128, 1024], F32)]
    pb = nc.alloc_psum_tensor("pb", [128, 1024], F32)
    pc = nc.alloc_psum_tensor("pc", [128, 512], F32)
    pd = nc.alloc_psum_tensor("pd", [128, 1024], BF16)

    def V(e, fn, r=(), w=()):
        return s.op(e, fn, reads=r, writes=w)

    def transpose_f32(dst, src, rows, key_src, key_dst, eng='dve'):
        V('pe', lambda e: e.transpose(out=pc[:, 0:rows], in_=src, identity=IDF[0:rows, 0:rows]), r=[key_src, 'idf'], w=['pc'])
        V(eng, lambda e: e.tensor_copy(out=dst, in_=pc[:, 0:rows]), r=['pc'], w=[key_dst])

    wcount = [0]

    def dense(wsrc, nK, cols, rhs_fn, rhs_keys, consume, m_tiles=None):
        wv = wsrc.rearrange("(k p) n -> p k n", p=128)
        kmax = 8192 // 512
        for (c0, ncol) in cols:
            kper = min(nK, 8192 // ncol)
            assert nK % kper == 0
            nkb = nK // kper
            bufs = []
            for kb in range(nkb):
                bi = wcount[0] % 2
                wcount[0] += 1
                wt = Wb[bi][:, 0:kper * ncol].rearrange("p (k n) -> p k n", n=ncol)
                s.dma('pool', wt, wv[:, kb * kper:(kb + 1) * kper, c0:c0 + ncol], writes=['W%d' % bi], fence=False)
                bufs.append((bi, wt))
                if nkb > 1 and kb < nkb - 1:
                    pass
            assert nkb <= 2
            for mt in range(ncol // 128):
                pi = dense.pcount % 2
                dense.pcount += 1
                ps = pa[pi]
                for half in range(2):
                    for kc in range(nK):
                        bi, wt = bufs[kc // kper]
                        V('pe', lambda e: e.matmul(ps[:, half * 512:(half + 1) * 512], lhsT=wt[:, kc % kper, mt * 128:(mt + 1) * 128],
                                                   rhs=rhs_fn(kc, half), start=(kc == 0), stop=(kc == nK - 1)),
                          r=['W%d' % bi] + rhs_keys(kc), w=['pa%d' % pi])
                consume(c0 + mt * 128, ps, 'pa%d' % pi)
    dense.pcount = 0

    def blocks(c0, n, bs=512):
        return [(c0 + i, min(bs, n - i)) for i in range(0, n, bs)]

    s.dma('sp', IDF, ident_d, writes=['idf'])
    s.dma('sp', IOTA, iota_d, writes=['iota'])
    s.dma('sp', SM[:, 0:4], flags, writes=['flags'])
    s.dma('pool', MASK, amask.rearrange("i p k -> p i k"), writes=['mask'])
    V('dve', lambda e: e.memset(ONES, 1.0), w=['ones'])
    V('dve', lambda e: e.tensor_copy(out=IDB, in_=IDF), r=['idf'], w=['idb'])
    FLAG = SM[:, 0:1]
    CB = SM[:, 1:2]
    FLM1 = SM[:, 2:3]
    HPI = SM[:, 4:5]
    V('dve', lambda e: e.memset(HPI, math.pi / 2), w=['hpi'])
    s.dma('sp', STG[0:16, 0:128], cond, writes=['stg'])
    transpose_f32(CONDT, STG[0:16, 0:128], 16, 'stg', 'condt')
    SC = bf16v(oSMALL + 8, 8)
    V('act', lambda e: e.activation(out=SC, in_=CONDT, func=AF.Silu), r=['condt'], w=['sc'])

    xin_v = xin.rearrange("(a p) d -> p a d", p=128)
    for a in range(8):
        for hh in range(2):
            stage = Tf[:, 0:1024]
            s.dma('sp', stage, xin_v[:, a, hh * 1024:(hh + 1) * 1024], writes=['t_stage'])
            for cc in range(8):
                c = hh * 8 + cc
                V('pe', lambda e: e.transpose(out=pb[:, cc * 128:(cc + 1) * 128], in_=stage[:, cc * 128:(cc + 1) * 128], identity=IDF),
                  r=['t_stage', 'idf'], w=['pb'])
            outst = Tf[:, 1024:2048]
            V('act', lambda e: e.activation(out=outst, in_=pb[:, :], func=AF.Identity), r=['pb'], w=['t_out'])
            s.dma('sp', xs[hh * 8:(hh + 1) * 8, :, a * 128:(a + 1) * 128].rearrange("c p t -> p c t"),
                  outst.rearrange("p (c t) -> p c t", t=128), reads=['t_out'], writes=['xs'])
    s.barrier()

    def load_params(l):
        s.dma('sp', STG[0:96, 0:128], b_mod[l], writes=['stg'])
        transpose_f32(PAR[:, 0:96], STG[0:96, 0:128], 96, 'stg', 'par')
        s.dma('sp', STG[0:64, 0:128], gvec[l], writes=['stg'])
        transpose_f32(PAR[:, 96:160], STG[0:64, 0:128], 64, 'stg', 'par')
        s.dma('sp', STG[0:96, 0:128], rgv[l], writes=['stg'])
        transpose_f32(PAR[:, 160:256], STG[0:96, 0:128], 96, 'stg', 'par')

    def modulation(l):
        MODP = PAR[:, 256:352]
        wv = w_mod[l].rearrange("(k p) n -> p k n", p=128)
        for blk in range(24):
            bi = wcount[0] % 2
            wcount[0] += 1
            wt = Wb[bi].rearrange("p (k n) -> p k n", n=512)
            s.dma('pool', wt, wv[:, :, blk * 512:(blk + 1) * 512], writes=['W%d' % bi], fence=False)
            for mt in range(4):
                j = blk * 4 + mt
                for kc in range(16):
                    V('pe', lambda e: e.matmul(pc[:, j:j + 1], lhsT=wt[:, kc, mt * 128:(mt + 1) * 128], rhs=SC[:, kc:kc + 1],
                                               start=(kc == 0), stop=(kc == 15)), r=['W%d' % bi, 'sc'], w=['pc'])
        V('dve', lambda e: e.tensor_tensor(out=MODP, in0=pc[:, 0:96], in1=PAR[:, 0:96], op=ALU.add), r=['pc', 'par'], w=['mod'])
        dbg('par', PAR, ['par', 'mod'])
        V('dve', lambda e: e.scalar_tensor_tensor(out=PAR[:, 352:368], in0=MODP[:, 16:32], scalar=1.0, in1=PAR[:, 96:112], op0=ALU.add, op1=ALU.mult),
          r=['mod', 'par'], w=['mod2'])
        V('dve', lambda e: e.scalar_tensor_tensor(out=PAR[:, 368:384], in0=MODP[:, 64:80], scalar=1.0, in1=PAR[:, 128:144], op0=ALU.add, op1=ALU.mult),
          r=['mod', 'par'], w=['mod2'])
        V('dve', lambda e: e.tensor_tensor(out=PAR[:, 384:400], in0=MODP[:, 32:48], in1=PAR[:, 112:128], op=ALU.mult), r=['mod', 'par'], w=['mod2'])
        V('dve', lambda e: e.tensor_tensor(out=PAR[:, 400:416], in0=MODP[:, 80:96], in1=PAR[:, 144:160], op=ALU.mult), r=['mod', 'par'], w=['mod2'])

    def norm_to_xn(gs_col, shift_col):
        Xr = Tf.rearrange("p (c t) -> p c t", t=NT)
        SQ = Mb
        for c in range(16):
            s.dma('sp', Xr[:, c, :], xs[c], reads=['xs'], writes=['x%d' % c])
            V('act', lambda e: e.activation(out=SQ[:, c, :], in_=Xr[:, c, :], func=AF.Square), r=['x%d' % c], w=['sq%d' % c])
        onesb = bf16v(oONESB, 64)
        V('dve', lambda e: e.tensor_copy(out=onesb, in_=ONES[:, 0:128]), r=['ones'], w=['onesb'])
        for half in range(2):
            for c in range(16):
                V('pe', lambda e: e.matmul(pb[:, half * 512:(half + 1) * 512], lhsT=onesb, rhs=SQ[:, c, half * 512:(half + 1) * 512],
                                           start=(c == 0), stop=(c == 15)), r=['onesb', 'sq%d' % c], w=['pb'])
        V('act', lambda e: e.activation(out=RSTD, in_=pb[:, :], func=AF.Sqrt, scale=1.0 / D, bias=EPS), r=['pb', 'eps'], w=['rstd'])
        V('dve', lambda e: e.reciprocal(out=RSTD, in_=RSTD), r=['rstd'], w=['rstd'])
        for c in range(16):
            V('dve', lambda e: e.tensor_tensor(out=Xr[:, c, :], in0=Xr[:, c, :], in1=RSTD, op=ALU.mult), r=['x%d' % c, 'rstd'], w=['x%d' % c])
            V('act', lambda e: e.activation(out=XN[:, c, :], in_=Xr[:, c, :], func=AF.Identity,
                                            scale=PAR[:, gs_col + c:gs_col + c + 1], bias=PAR[:, shift_col + c:shift_col + c + 1]),
              r=['x%d' % c, 'mod', 'mod2'], w=['xn%d' % c])
        dbg('rstd', RSTD, ['rstd'])
        dbg('xn0', XN[:, 0, :], ['xn0'])
        dbg('par2', PAR, ['par', 'mod', 'mod2'])
        s.barrier()

    EPS = SM[:, 5:6]
    V('dve', lambda e: e.memset(EPS, 1e-6), w=['eps'])

    def post_norm_residual(gg_col):
        SQ = Mb
        for c in range(16):
            V('act', lambda e: e.activation(out=SQ[:, c, :], in_=XN[:, c, :], func=AF.Square), r=['xn%d' % c], w=['sq%d' % c])
        onesb = bf16v(oONESB, 64)
        for half in range(2):
            for c in range(16):
                V('pe', lambda e: e.matmul(pb[:, half * 512:(half + 1) * 512], lhsT=onesb, rhs=SQ[:, c, half * 512:(half + 1) * 512],
                                           start=(c == 0), stop=(c == 15)), r=['onesb', 'sq%d' % c], w=['pb'])
        V('act', lambda e: e.activation(out=RSTD, in_=pb[:, :], func=AF.Sqrt, scale=1.0 / D, bias=EPS), r=['pb', 'eps'], w=['rstd'])
        V('dve', lambda e: e.reciprocal(out=RSTD, in_=RSTD), r=['rstd'], w=['rstd'])
        Xr = Tf.rearrange("p (c t) -> p c t", t=NT)
        for c in range(16):
            s.dma('sp', Xr[:, c, :], xs[c], reads=['xs'], writes=['x%d' % c])
            tmp = XST
            V('dve', lambda e: e.tensor_tensor(out=tmp, in0=XN[:, c, :], in1=RSTD, op=ALU.mult), r=['xn%d' % c, 'rstd'], w=['xst'])
            V('dve', lambda e: e.scalar_tensor_tensor(out=Xr[:, c, :], in0=tmp, scalar=PAR[:, gg_col + c:gg_col + c + 1], in1=Xr[:, c, :],
                                                      op0=ALU.mult, op1=ALU.add), r=['xst', 'x%d' % c, 'mod2'], w=['x%d' % c])
            s.dma('sp', xs[c], Xr[:, c, :], reads=['x%d' % c], writes=['xs'])
        s.barrier()

    def xn_rhs(kc, half):
        return XN[:, kc, half * 512:(half + 1) * 512]

    def xn_keys(kc):
        return ['xn%d' % kc]

    def merge_branch(l, b, act_col_base):
        SG = Tf[:, 0:1024]
        for blk in range(4):
            SGs = Tf[:, 0:4096].rearrange("p (m t) -> p m t", t=NT)

            def cons_gate(col, ps, pk):
                mt = ((col - act_col_base) // 128) % 4
                V('act', lambda e: e.activation(out=SGs[:, mt, :], in_=ps[:, :], func=AF.Sigmoid), r=[pk], w=['sg%d' % mt])
            dense(w_in[l], 16, [(act_col_base + blk * 512, 512)], xn_rhs, xn_keys, cons_gate)

            def cons_proj(col, ps, pk):
                c = col // 128
                mt = c % 4
                if b == 0:
                    V('dve', lambda e: e.tensor_tensor(out=Mb[:, c, :], in0=ps[:, :], in1=SGs[:, mt, :], op=ALU.mult), r=[pk, 'sg%d' % mt], w=['m%d' % c])
                else:
                    V('dve', lambda e: e.tensor_tensor(out=SGs[:, mt, :], in0=ps[:, :], in1=SGs[:, mt, :], op=ALU.mult), r=[pk, 'sg%d' % mt], w=['sg%d' % mt])
                    V('dve', lambda e: e.tensor_tensor(out=Mb[:, c, :], in0=Mb[:, c, :], in1=SGs[:, mt, :], op=ALU.add), r=['m%d' % c, 'sg%d' % mt], w=['m%d' % c])
            dense(w_bo[l, b], 8, [(blk * 512, 512)], lambda kc, half: YB[:, kc, half * 512:(half + 1) * 512], lambda kc: ['yb%d' % kc], cons_proj)
        s.barrier()

    def rg_branch(l):
        OFFX, OFFG = 0, 1024
        XR = Tb[:, 0:8192].rearrange("p (c t) -> p c t", t=NT)
        XG = Tb[:, 8192:16384].rearrange("p (c t) -> p c t", t=NT)
        base = 8192
        def tf(i):
            return Tf[:, base + i * 1024: base + (i + 1) * 1024]
        XC, R_, IG, A_, SQ_, HF, HB = tf(0), tf(1), tf(2), tf(3), tf(4), tf(5), tf(6)
        XCB = Tb[:, 2 * (base + 7 * 1024): 2 * (base + 7 * 1024) + 1024]
        V('act', lambda e: e.activation(out=SM[:, 16:32], in_=PAR[:, 232:248], func=AF.Exp, scale=-1.0), r=['par'], w=['cneg'])
        V('act', lambda e: e.activation(out=SM[:, 16:32], in_=SM[:, 16:32], func=AF.Ln, bias=1.0), r=['cneg'], w=['cneg'])
        V('dve', lambda e: e.tensor_scalar(out=SM[:, 32:48], in0=SM[:, 16:32], scalar1=-16.0, scalar2=None, op0=ALU.mult), r=['cneg'], w=['cneg2'])
        V('dve', lambda e: e.tensor_scalar(out=SM[:, 16:32], in0=SM[:, 16:32], scalar1=-8.0, scalar2=None, op0=ALU.mult), r=['cneg'], w=['cneg'])
        V('dve', lambda e: e.tensor_scalar(out=SM[:, 48:80], in0=PAR[:, 160:192], scalar1=FLM1, scalar2=None, op0=ALU.mult), r=['par', 'flags'], w=['wneg'])
        s.dma('sp', STG[0:16, 0:128], st_rg[l], writes=['stg'])
        transpose_f32(SM[:, 80:96], STG[0:16, 0:128], 16, 'stg', 'h0rg')
        STC = SM[:, 96:160].rearrange("p (d q c) -> p d q c", d=2, q=4)

        def cons_x(col, ps, pk):
            c = (col % 1024) // 128
            dst = XR if col < 1024 else XG
            V('act', lambda e: e.activation(out=dst[:, c, :], in_=ps[:, :], func=AF.Identity), r=[pk], w=['xrg'])
        dense(w_in[l], 16, blocks(OFFX, 2048), xn_rhs, xn_keys, cons_x)
        s.dma('pool', GW, rg_gate_w[l].rearrange("a i j -> i a j"), writes=['W1'], fence=False)

        for c in range(8):
            wc = lambda k: PAR[:, 160 + k * 8 + c:160 + k * 8 + c + 1]
            wn = lambda k: SM[:, 48 + k * 8 + c:48 + k * 8 + c + 1]
            x = XR[:, c, :]
            V('act', lambda e: e.activation(out=XC, in_=x, func=AF.Identity, scale=wc(1), bias=PAR[:, 192 + c:193 + c]), r=['xrg', 'par'], w=['xc'])
            V('dve', lambda e: e.scalar_tensor_tensor(out=XC[:, 1:NT], in0=x[:, 0:NT - 1], scalar=wc(0), in1=XC[:, 1:NT], op0=ALU.mult, op1=ALU.add), r=['xrg', 'xc'], w=['xc'])
            V('dve', lambda e: e.scalar_tensor_tensor(out=XC[:, 0:NT - 1], in0=x[:, 1:NT], scalar=wc(2), in1=XC[:, 0:NT - 1], op0=ALU.mult, op1=ALU.add), r=['xrg', 'xc'], w=['xc'])
            V('dve', lambda e: e.scalar_tensor_tensor(out=XC[:, 0:NT - 2], in0=x[:, 2:NT], scalar=wc(3), in1=XC[:, 0:NT - 2], op0=ALU.mult, op1=ALU.add), r=['xrg', 'xc'], w=['xc'])
            def cols(off):
                return slice(off, off + 513, 256)
            V('dve', lambda e: e.scalar_tensor_tensor(out=XC[:, cols(256)], in0=x[:, cols(255)], scalar=wn(0), in1=XC[:, cols(256)], op0=ALU.mult, op1=ALU.add), r=['xrg', 'xc', 'wneg'], w=['xc'])
            V('dve', lambda e: e.scalar_tensor_tensor(out=XC[:, cols(255)], in0=x[:, cols(256)], scalar=wn(2), in1=XC[:, cols(255)], op0=ALU.mult, op1=ALU.add), r=['xrg', 'xc', 'wneg'], w=['xc'])
            V('dve', lambda e: e.scalar_tensor_tensor(out=XC[:, cols(255)], in0=x[:, cols(257)], scalar=wn(3), in1=XC[:, cols(255)], op0=ALU.mult, op1=ALU.add), r=['xrg', 'xc', 'wneg'], w=['xc'])
            V('dve', lambda e: e.scalar_tensor_tensor(out=XC[:, cols(254)], in0=x[:, cols(256)], scalar=wn(3), in1=XC[:, cols(254)], op0=ALU.mult, op1=ALU.add), r=['xrg', 'xc', 'wneg'], w=['xc'])
            V('act', lambda e: e.activation(out=XCB, in_=XC, func=AF.Identity), r=['xc'], w=['xcb'])
            for d in range(2):
                H = HF if d == 0 else HB
                for g, dst in ((0, R_), (1, IG)):
                    pi = dense.pcount % 2
                    dense.pcount += 1
                    ps = pa[pi]
                    for half in range(2):
                        V('pe', lambda e: e.matmul(ps[:, half * 512:(half + 1) * 512], lhsT=GW[:, (d * 2 + g) * 8 + c, :], rhs=XCB[:, half * 512:(half + 1) * 512],
                                                   start=True, stop=True), r=['W1', 'xcb'], w=['pa%d' % pi])
                    bcol = 200 + (d * 2 + g) * 8 + c
                    V('act', lambda e: e.activation(out=dst, in_=ps[:, :], func=AF.Sigmoid, bias=PAR[:, bcol:bcol + 1]), r=['pa%d' % pi, 'par'], w=['rg_t%d' % g])
                V('act', lambda e: e.activation(out=A_, in_=R_, func=AF.Exp, scale=SM[:, 16 + d * 8 + c:17 + d * 8 + c]), r=['rg_t0', 'cneg'], w=['rg_a'])
                V('act', lambda e: e.activation(out=SQ_, in_=R_, func=AF.Exp, scale=SM[:, 32 + d * 8 + c:33 + d * 8 + c]), r=['rg_t0', 'cneg2'], w=['rg_s'])
                V('act', lambda e: e.activation(out=SQ_, in_=SQ_, func=AF.Sqrt, scale=-1.0, bias=1.0), r=['rg_s'], w=['rg_s'])
                V('dve', lambda e: e.tensor_tensor(out=SQ_, in0=SQ_, in1=IG, op=ALU.mult), r=['rg_s', 'rg_t1'], w=['rg_s'])
                V('dve', lambda e: e.tensor_tensor(out=SQ_, in0=SQ_, in1=XC, op=ALU.mult), r=['rg_s', 'xc'], w=['rg_s'])
                bc = cols(256) if d == 0 else cols(255)
                V('dve', lambda e: e.tensor_scalar(out=A_[:, bc], in0=A_[:, bc], scalar1=FLAG, scalar2=None, op0=ALU.mult), r=['rg_a', 'flags'], w=['rg_a'])
                h0 = SM[:, 80 + d * 8 + c:81 + d * 8 + c]
                if d == 0:
                    V('dve', lambda e: e.tensor_tensor_scan(out=H, data0=A_, data1=SQ_, initial=h0, op0=ALU.mult, op1=ALU.add), r=['rg_a', 'rg_s', 'h0rg'], w=['rg_h%d' % d])
                    V('dve', lambda e: e.tensor_copy(out=STC[:, 0, :, c], in_=H[:, 255:NT:256]), r=['rg_h0'], w=['stc'])
                else:
                    V('dve', lambda e: e.tensor_tensor_scan(out=H[:, ::-1], data0=A_[:, ::-1], data1=SQ_[:, ::-1], initial=h0, op0=ALU.mult, op1=ALU.add),
                      r=['rg_a', 'rg_s', 'h0rg'], w=['rg_h%d' % d])
                    V('dve', lambda e: e.tensor_copy(out=STC[:, 1, :, c], in_=H[:, 0:NT:256]), r=['rg_h1'], w=['stc'])
            V('dve', lambda e: e.tensor_tensor(out=HF, in0=HF, in1=HB, op=ALU.add), r=['rg_h0', 'rg_h1'], w=['rg_h0'])
            V('act', lambda e: e.activation(out=R_, in_=XG[:, c, :], func=AF.Gelu_apprx_tanh), r=['xrg', 'rg_t0'], w=['rg_t0'])
            V('dve', lambda e: e.tensor_tensor(out=YB[:, c, :], in0=HF, in1=R_, op=ALU.mult), r=['rg_h0', 'rg_t0'], w=['yb%d' % c])
        for d in range(2):
            src = SM[:, 96 + d * 32:96 + (d + 1) * 32]
            V('pe', lambda e: e.transpose(out=pc[0:32, 0:128], in_=src, identity=IDF), r=['stc', 'idf'], w=['pc'])
            V('dve', lambda e: e.tensor_copy(out=STG[0:32, 0:128], in_=pc[0:32, 0:128]), r=['pc'], w=['stg'])
            s.dma('sp', nrg_out[l, d].rearrange("q c h -> (q c) h"), STG[0:32, 0:128], reads=['stg'], is_output=True)
        s.barrier()

    def s5_branch(l):
        OFFU = 2048
        U = Tb[:, 0:8192].rearrange("p (c t) -> p c t", t=NT)
        Z = Mb[:, 0:8, :]
        base = 4096
        def tf(i):
            return Tf[:, base + i * 1024: base + (i + 1) * 1024]
        COS, SIN, A_, T1, T2, T3, GR, GI = [tf(i) for i in range(8)]
        HRb = Tb[:, 2 * (base + 8 * 1024): 2 * (base + 8 * 1024) + 1024]
        HIb = Tb[:, 2 * (base + 8 * 1024) + 1024: 2 * (base + 8 * 1024) + 2048]
        YBb = bf16v(oYB, 4096)
        BT = Wb[0][:, 0:4096].rearrange("p (a c m) -> p a c m", a=4, c=8)
        CTfl = YBb[:, 0:4 * 8 * 192]
        CT = CTfl.rearrange("p (a c m) -> p a c m", a=4, c=8)
        CTOFF = [0, 32, 64, 160]
        U3 = Tb[:, 2 * (base + 9 * 1024 + 1536): 2 * (base + 9 * 1024 + 1536) + 1024]
        assert base + 9 * 1024 + 1536 + 512 <= 16384
        M2 = f32v(oM + 4096, 4096)
        BZf = M2[:, 0:1024]
        BZ = BZf.rearrange("p (g m) -> p g m", m=32)
        CZf = M2[:, 1024:2048]
        CZ = CZf.rearrange("p (c m) -> p c m", m=128)
        CTf = M2[:, 2048:4096].rearrange("p (r c m) -> p r c m", r=2, c=8)
        zoff = base + 9 * 1024 - 4096
        s5p = Tf[:, zoff + 4096: zoff + 4096 + 1024]
        LRE, LIM = s5p[:, 0:64], s5p[:, 64:128]
        DT = s5p[:, 128:192]
        RHO, TH = s5p[:, 192:256], s5p[:, 256:320]
        KR, KI = s5p[:, 320:384], s5p[:, 384:448]
        H0R, H0I = s5p[:, 448:512], s5p[:, 512:576]
        TMPA, TMPB, TMPC, TMPD = s5p[:, 576:640], s5p[:, 640:704], s5p[:, 704:768], s5p[:, 768:832]
        HTR, HTI = s5p[:, 832:896], s5p[:, 896:960]
        STRf = Tf[:, zoff + 5120: zoff + 5120 + 256]
        STIf = Tf[:, zoff + 5376: zoff + 5376 + 256]
        STR_ = STRf.rearrange("p (d g q) -> p d g q", d=2, q=4)
        STI_ = STIf.rearrange("p (d g q) -> p d g q", d=2, q=4)
        assert zoff + 5632 <= 16384

        def cons_u(col, ps, pk):
            c = (col - OFFU) // 128
            V('act', lambda e: e.activation(out=U[:, c, :], in_=ps[:, :], func=AF.Identity), r=[pk], w=['u'])
        dense(w_in[l], 16, blocks(OFFU, 1024), xn_rhs, xn_keys, cons_u)
        dbg('u0', U[:, 0, :], ['u'])
        s.barrier()
        V('dve', lambda e: e.memset(CTfl, 0.0), w=['ct'])

        s.dma('sp', STG[:, 0:128], s5_lam[l], writes=['stg'])
        V('pe', lambda e: e.transpose(out=pc[:, 0:128], in_=STG[:, 0:128], identity=IDF), r=['stg', 'idf'], w=['pc'])
        V('dve', lambda e: e.tensor_copy(out=s5p[:, 0:128], in_=pc[:, 0:128]), r=['pc'], w=['s5p'])
        s.dma('sp', STG[0:64, 128:130], s5_ldt[l], writes=['stg2'])
        for gpar in range(2):
            V('dve', lambda e: e.tensor_scalar(out=STG[0:64, gpar * 64:(gpar + 1) * 64], in0=ONES[0:64, 0:64], scalar1=STG[0:64, 128 + gpar:129 + gpar],
                                               scalar2=None, op0=ALU.mult), r=['stg2', 'ones', 'pc'], w=['stg'])
        V('pe', lambda e: e.transpose(out=pc[:, 0:64], in_=STG[0:64, 0:128], identity=IDF[0:64, 0:64]), r=['stg', 'idf'], w=['pc'])
        V('act', lambda e: e.activation(out=DT, in_=pc[:, 0:64], func=AF.Exp), r=['pc'], w=['s5p'])
        V('dve', lambda e: e.tensor_tensor(out=TH, in0=LIM, in1=DT, op=ALU.mult), r=['s5p'], w=['s5p'])
        V('dve', lambda e: e.tensor_tensor(out=TMPA, in0=LRE, in1=DT, op=ALU.mult), r=['s5p'], w=['s5p'])
        V('act', lambda e: e.activation(out=RHO, in_=TMPA, func=AF.Exp), r=['s5p'], w=['s5p'])
        KI64 = s5p[:, 960:1024].bitcast(mybir.dt.int32)
        V('dve', lambda e: e.tensor_scalar(out=KI64, in0=TH, scalar1=1.0 / TWO_PI, scalar2=None, op0=ALU.mult), r=['s5p'], w=['s5p'])
        V('dve', lambda e: e.scalar_tensor_tensor(out=TMPD, in0=KI64, scalar=-TWO_PI, in1=TH, op0=ALU.mult, op1=ALU.add), r=['s5p'], w=['s5p'])
        V('act', lambda e: e.activation(out=TMPB, in_=TMPD, func=AF.Sin, scale=0.99999), r=['s5p'], w=['s5p'])
        V('dve', lambda e: e.scalar_tensor_tensor(out=TMPD, in0=TMPD, scalar=-1.0, in1=TMPD, op0=ALU.mult, op1=ALU.max), r=['s5p'], w=['s5p'])
        V('act', lambda e: e.activation(out=TMPC, in_=TMPD, func=AF.Sin, scale=-0.99999, bias=HPI), r=['s5p', 'hpi'], w=['s5p'])
        V('dve', lambda e: e.tensor_tensor(out=TMPB, in0=TMPB, in1=RHO, op=ALU.mult), r=['s5p'], w=['s5p'])
        V('dve', lambda e: e.tensor_tensor(out=TMPC, in0=TMPC, in1=RHO, op=ALU.mult), r=['s5p'], w=['s5p'])
        V('dve', lambda e: e.tensor_scalar(out=TMPC, in0=TMPC, scalar1=-1.0, scalar2=None, op0=ALU.add), r=['s5p'], w=['s5p'])
        V('dve', lambda e: e.tensor_tensor(out=TMPA, in0=LRE, in1=LRE, op=ALU.mult), r=['s5p'], w=['s5p'])
        V('dve', lambda e: e.tensor_tensor(out=TMPD, in0=LIM, in1=LIM, op=ALU.mult), r=['s5p'], w=['s5p'])
        V('dve', lambda e: e.tensor_tensor(out=TMPA, in0=TMPA, in1=TMPD, op=ALU.add), r=['s5p'], w=['s5p'])
        V('dve', lambda e: e.reciprocal(out=TMPA, in_=TMPA), r=['s5p'], w=['s5p'])
        V('dve', lambda e: e.tensor_tensor(out=KR, in0=TMPC, in1=LRE, op=ALU.mult), r=['s5p'], w=['s5p'])
        V('dve', lambda e: e.tensor_tensor(out=TMPD, in0=TMPB, in1=LIM, op=ALU.mult), r=['s5p'], w=['s5p'])
        V('dve', lambda e: e.tensor_tensor(out=KR, in0=KR, in1=TMPD, op=ALU.add), r=['s5p'], w=['s5p'])
        V('dve', lambda e: e.tensor_tensor(out=KR, in0=KR, in1=TMPA, op=ALU.mult), r=['s5p'], w=['s5p'])
        V('dve', lambda e: e.tensor_tensor(out=KI, in0=TMPB, in1=LRE, op=ALU.mult), r=['s5p'], w=['s5p'])
        V('dve', lambda e: e.tensor_tensor(out=TMPD, in0=TMPC, in1=LIM, op=ALU.mult), r=['s5p'], w=['s5p'])
        V('dve', lambda e: e.tensor_tensor(out=KI, in0=KI, in1=TMPD, op=ALU.subtract), r=['s5p'], w=['s5p'])
        V('dve', lambda e: e.tensor_tensor(out=KI, in0=KI, in1=TMPA, op=ALU.mult), r=['s5p'], w=['s5p'])
        s.dma('sp', STG[0:64, 0:256], st_s5[l], writes=['stg'])
        for ri, dst in ((0, H0R), (1, H0I)):
            V('pe', lambda e: e.transpose(out=pc[:, 0:64], in_=STG[0:64, ri:256:2], identity=IDF[0:64, 0:64]), r=['stg', 'idf'], w=['pc'])
            V('dve', lambda e: e.tensor_copy(out=dst, in_=pc[:, 0:64]), r=['pc'], w=['s5p'])
        V('dve', lambda e: e.tensor_tensor(out=TMPA, in0=KR, in1=KR, op=ALU.mult), r=['s5p'], w=['s5p'])
        V('dve', lambda e: e.tensor_tensor(out=TMPD, in0=KI, in1=KI, op=ALU.mult), r=['s5p'], w=['s5p'])
        V('dve', lambda e: e.tensor_tensor(out=TMPA, in0=TMPA, in1=TMPD, op=ALU.add), r=['s5p'], w=['s5p'])
        V('dve', lambda e: e.reciprocal(out=TMPA, in_=TMPA), r=['s5p'], w=['s5p'])
        V('dve', lambda e: e.tensor_tensor(out=HTR, in0=H0R, in1=KR, op=ALU.mult), r=['s5p'], w=['s5p'])
        V('dve', lambda e: e.tensor_tensor(out=TMPD, in0=H0I, in1=KI, op=ALU.mult), r=['s5p'], w=['s5p'])
        V('dve', lambda e: e.tensor_tensor(out=HTR, in0=HTR, in1=TMPD, op=ALU.add), r=['s5p'], w=['s5p'])
        V('dve', lambda e: e.tensor_tensor(out=HTR, in0=HTR, in1=TMPA, op=ALU.mult), r=['s5p'], w=['s5p'])
        V('dve', lambda e: e.tensor_tensor(out=HTI, in0=H0I, in1=KR, op=ALU.mult), r=['s5p'], w=['s5p'])
        V('dve', lambda e: e.tensor_tensor(out=TMPD, in0=H0R, in1=KI, op=ALU.mult), r=['s5p'], w=['s5p'])
        V('dve', lambda e: e.tensor_tensor(out=HTI, in0=HTI, in1=TMPD, op=ALU.subtract), r=['s5p'], w=['s5p'])
        V('dve', lambda e: e.tensor_tensor(out=HTI, in0=HTI, in1=TMPA, op=ALU.mult), r=['s5p'], w=['s5p'])

        for ri in range(2):
            for d in range(2):
                V('dve', lambda e: e.memset(BZf, 0.0), w=['bz', 'bz0', 'bz1'])
                bsrc = s5_b[l, ri, d].rearrange("(gp gq) p m -> gq p gp m", gq=2)
                for gq in range(2):
                    s.dma('sp', BZ[gq * 64:(gq + 1) * 64, :, gq * 16:(gq + 1) * 16], bsrc[gq], writes=['bz%d' % gq])
                for ch in range(8):
                    V('pe', lambda e: e.transpose(out=pc[:, 0:128], in_=BZf[:, ch * 128:(ch + 1) * 128], identity=IDF), r=['bz', 'bz0', 'bz1', 'idf'], w=['pc'])
                    V('act', lambda e: e.activation(out=BT[:, ri * 2 + d, ch, :], in_=pc[:, 0:128], func=AF.Identity), r=['pc'], w=['bt'])
        for d in range(2):
            for ri in range(2):
                V('dve', lambda e: e.memset(CZf, 0.0), w=['cz'] + ['cz%d' % i for i in range(8)])
                for gl_ in range(4):
                    for gq in range(2):
                        csrc = s5_c[l, ri, d].rearrange("(c g) n p -> g n c p", g=8)[2 * gl_ + gq]
                        r0 = gl_ * 32 + gq * 16
                        s.dma('sp', CZ[r0:r0 + 16, :, gq * 64:(gq + 1) * 64], csrc, writes=['cz%d' % (gl_ * 2 + gq)])
                for ch in range(8):
                    V('pe', lambda e: e.transpose(out=pc[:, 0:128], in_=CZ[:, ch, :], identity=IDF), r=['cz'] + ['cz%d' % i for i in range(8)] + ['idf'], w=['pc'])
                    V('act', lambda e: e.activation(out=CTf[:, ri, ch, :], in_=pc[:, 0:128], func=AF.Identity), r=['pc'], w=['ctf'])
            for ch in range(8):
                for gl_ in range(4):
                    gp = ch * 4 + gl_
                    kr = KR[:, d * 32 + gp:d * 32 + gp + 1]
                    ki = KI[:, d * 32 + gp:d * 32 + gp + 1]
                    cre = CTf[:, 0, ch, gl_ * 32:(gl_ + 1) * 32]
                    cim = CTf[:, 1, ch, gl_ * 32:(gl_ + 1) * 32]
                    t32 = TMPD[:, 0:32]
                    V('dve', lambda e: e.tensor_scalar(out=t32, in0=cim, scalar1=ki, scalar2=None, op0=ALU.mult), r=['ctf', 's5p'], w=['t32'])
                    V('dve', lambda e: e.scalar_tensor_tensor(out=CT[:, 0 + d, ch, CTOFF[gl_]:CTOFF[gl_] + 32], in0=cre, scalar=kr, in1=t32, op0=ALU.mult, op1=ALU.subtract),
                      r=['ctf', 's5p', 't32'], w=['ct'])
                    V('dve', lambda e: e.tensor_scalar(out=t32, in0=cim, scalar1=kr, scalar2=-1.0, op0=ALU.mult, op1=ALU.mult), r=['ctf', 's5p'], w=['t32'])
                    V('dve', lambda e: e.tensor_scalar(out=cre, in0=cre, scalar1=ki, scalar2=None, op0=ALU.mult), r=['ctf', 's5p'], w=['ctf'])
                    V('dve', lambda e: e.tensor_tensor(out=CT[:, 2 + d, ch, CTOFF[gl_]:CTOFF[gl_] + 32], in0=t32, in1=cre, op=ALU.subtract), r=['ctf', 't32'], w=['ct'])

        dbg('s5p', s5p, ['s5p'])
        dbg('bt', BT[:, 0, 0, :], ['bt'])
        dbg('bt2', BT[:, 2, 0, :], ['bt'])
        dbg('ct', CT[:, 0, 0, :], ['ct'])
        dbg('ct2', CT[:, 2, 0, :], ['ct'])
        W1f = f32v(oW[1], 4096)
        W0h = f32v(oW[0] + 2048, 2048)
        SETS = [dict(COS=COS, SIN=SIN, T1=T1, T2=T2, T3=T3, GR=GR),
                dict(COS=W1f[:, 0:1024], SIN=W1f[:, 1024:2048], T1=W1f[:, 2048:3072], T2=W1f[:, 3072:4096], T3=W0h[:, 0:1024], GR=W0h[:, 1024:2048])]
        it = 0
        for ch in range(8):
            V('dve', lambda e: e.tensor_copy(out=U3[64:128, :], in_=U[64:128, ch, :]), r=['u'], w=['u3'])
            V('dve', lambda e: e.memset(U3[64:96, :], 0.0), r=['u3'], w=['u3'])
            for gl_ in range(4):
                gp = ch * 4 + gl_
                rows = slice(gl_ * 32, (gl_ + 1) * 32) if gl_ < 3 else slice(64, 128)
                orows = slice(gl_ * 32, (gl_ + 1) * 32) if gl_ < 2 else slice(64, 128)
                ccols = [slice(0, 32), slice(32, 64), slice(64, 128), slice(128, 192)][gl_]
                for d in range(2):
                    q = it % 2
                    it += 1
                    S_ = SETS[q]
                    cC, cS, c1, c2, c3, cG = S_['COS'], S_['SIN'], S_['T1'], S_['T2'], S_['T3'], S_['GR']
                    kC, kS, k1, k2, k3, kG = 'cos%d' % q, 'sin%d' % q, 't1_%d' % q, 't2_%d' % q, 't3_%d' % q, 'gr%d' % q
                    col = d * 32 + gp
                    for ri in range(2):
                        for half in range(2):
                            V('pe', lambda e: e.matmul(pa[ri][:, half * 512:(half + 1) * 512], lhsT=BT[rows, ri * 2 + d, ch, :],
                                                       rhs=(U[rows, ch, half * 512:(half + 1) * 512] if gl_ < 3 else U3[rows, half * 512:(half + 1) * 512]), start=True, stop=True), r=['bt', 'u', 'u3'], w=['pa%d' % ri])
                    th = TH[:, col:col + 1]
                    KINT = cG.bitcast(mybir.dt.int32)
                    V('pool', lambda e: e.tensor_scalar(out=c1, in0=IOTA, scalar1=th, scalar2=None, op0=ALU.mult), r=['iota', 's5p'], w=[k1])
                    V('pool', lambda e: e.tensor_scalar(out=A_, in0=IOTA, scalar1=0.0, scalar2=RHO[:, col:col + 1], op0=ALU.mult, op1=ALU.add), r=['iota', 's5p'], w=['a5'])
                    V('pool', lambda e: e.tensor_scalar(out=A_[:, 256:NT:256], in0=A_[:, 256:NT:256], scalar1=FLAG, scalar2=None, op0=ALU.mult), r=['a5', 'flags'], w=['a5'])
                    V('dve', lambda e: e.tensor_scalar(out=KINT, in0=c1, scalar1=1.0 / TWO_PI, scalar2=None, op0=ALU.mult), r=[k1], w=[kG])
                    V('dve', lambda e: e.scalar_tensor_tensor(out=c2, in0=KINT, scalar=-TWO_PI, in1=c1, op0=ALU.mult, op1=ALU.add), r=[kG, k1], w=[k2])
                    V('act', lambda e: e.activation(out=cS, in_=c2, func=AF.Sin, scale=0.99999), r=[k2], w=[kS])
                    V('dve', lambda e: e.scalar_tensor_tensor(out=c3, in0=c2, scalar=-1.0, in1=c2, op0=ALU.mult, op1=ALU.max), r=[k2], w=[k3])
                    V('act', lambda e: e.activation(out=cC, in_=c3, func=AF.Sin, scale=-0.99999, bias=HPI), r=[k3, 'hpi'], w=[kC])
                    if d == 0:
                        br, bi = pa[0][:, :], pa[1][:, :]
                    else:
                        br, bi = pa[0][:, ::-1], pa[1][:, ::-1]
                    V('dve', lambda e: e.tensor_tensor(out=c1, in0=cC, in1=br, op=ALU.mult), r=[kC, 'pa0'], w=[k1])
                    V('dve', lambda e: e.tensor_tensor(out=c2, in0=cS, in1=bi, op=ALU.mult), r=[kS, 'pa1'], w=[k2])
                    V('dve', lambda e: e.tensor_tensor(out=c1, in0=c1, in1=c2, op=ALU.add), r=[k1, k2], w=[k1])
                    V('dve', lambda e: e.tensor_tensor(out=c2, in0=cC, in1=bi, op=ALU.mult), r=[kC, 'pa1'], w=[k2])
                    V('dve', lambda e: e.tensor_tensor(out=c3, in0=cS, in1=br, op=ALU.mult), r=[kS, 'pa0'], w=[k3])
                    V('dve', lambda e: e.tensor_tensor(out=c2, in0=c2, in1=c3, op=ALU.subtract), r=[k2, k3], w=[k2])
                    V('dve', lambda e: e.tensor_tensor_scan(out=cG, data0=A_, data1=c1, initial=HTR[:, col:col + 1], op0=ALU.mult, op1=ALU.add), r=['a5', k1, 's5p'], w=[kG])
                    V('dve', lambda e: e.tensor_tensor_scan(out=GI, data0=A_, data1=c2, initial=HTI[:, col:col + 1], op0=ALU.mult, op1=ALU.add), r=['a5', k2, 's5p'], w=['gi'])
                    V('pool', lambda e: e.tensor_tensor(out=c1, in0=cC, in1=cG, op=ALU.mult), r=[kC, kG, k1], w=[k1])
                    V('pool', lambda e: e.tensor_tensor(out=c3, in0=cS, in1=GI, op=ALU.mult), r=[kS, 'gi', k3], w=[k3])
                    V('pool', lambda e: e.tensor_tensor(out=c1, in0=c1, in1=c3, op=ALU.subtract), r=[k1, k3], w=[k1])
                    V('pool', lambda e: e.tensor_tensor(out=c2, in0=cS, in1=cG, op=ALU.mult), r=[kS, kG, k2], w=[k2])
                    V('pool', lambda e: e.tensor_tensor(out=c3, in0=cC, in1=GI, op=ALU.mult), r=[kC, 'gi', k3], w=[k3])
                    V('pool', lambda e: e.tensor_tensor(out=c2, in0=c2, in1=c3, op=ALU.add), r=[k2, k3], w=[k2])
                    if d == 0:
                        V('act', lambda e: e.activation(out=HRb, in_=c1, func=AF.Identity), r=[k1], w=['hrb'])
                        V('act', lambda e: e.activation(out=HIb, in_=c2, func=AF.Identity), r=[k2], w=['hib'])
                    else:
                        V('act', lambda e: e.activation(out=HRb[:, ::-1], in_=c1, func=AF.Identity), r=[k1], w=['hrb'])
                        V('act', lambda e: e.activation(out=HIb[:, ::-1], in_=c2, func=AF.Identity), r=[k2], w=['hib'])
                    kr = KR[:, col:col + 1]
                    ki = KI[:, col:col + 1]
                    e1 = c1[:, 255:NT:256]
                    e2 = c2[:, 255:NT:256]
                    t4 = TMPD[:, 32:36]
                    so = slice(None) if d == 0 else slice(None, None, -1)
                    V('dve', lambda e: e.tensor_scalar(out=t4, in0=e2, scalar1=ki, scalar2=None, op0=ALU.mult), r=[k2, 's5p'], w=['t4'])
                    V('dve', lambda e: e.scalar_tensor_tensor(out=STR_[:, d, gp, so], in0=e1, scalar=kr, in1=t4, op0=ALU.mult, op1=ALU.subtract), r=[k1, 't4', 's5p'], w=['st5'])
                    V('dve', lambda e: e.tensor_scalar(out=t4, in0=e2, scalar1=kr, scalar2=None, op0=ALU.mult), r=[k2, 's5p'], w=['t4'])
                    V('dve', lambda e: e.scalar_tensor_tensor(out=STI_[:, d, gp, so], in0=e1, scalar=ki, in1=t4, op0=ALU.mult, op1=ALU.add), r=[k1, 't4', 's5p'], w=['st5'])
                    for half in range(2):
                        V('pe', lambda e: e.matmul(pb[orows, half * 512:(half + 1) * 512], lhsT=CT[:, 0 + d, ch, ccols], rhs=HRb[:, half * 512:(half + 1) * 512],
                                                   start=(d == 0 and gl_ < 3), stop=False), r=['ct', 'hrb'], w=['pb'])
                        V('pe', lambda e: e.matmul(pb[orows, half * 512:(half + 1) * 512], lhsT=CT[:, 2 + d, ch, ccols], rhs=HIb[:, half * 512:(half + 1) * 512],
                                                   start=False, stop=(d == 1)), r=['ct', 'hib'], w=['pb'])
            dcol = 248 + ch
            V('dve', lambda e: e.scalar_tensor_tensor(out=GI, in0=U[:, ch, :], scalar=PAR[:, dcol:dcol + 1], in1=pb[:, :], op0=ALU.mult, op1=ALU.add), r=['u', 'par', 'pb'], w=['gi'])
            V('act', lambda e: e.activation(out=Z[:, ch, :], in_=GI, func=AF.Gelu_apprx_tanh), r=['gi'], w=['z%d' % ch])
        s.barrier()
        def cons_glu(col, ps, pk):
            c = col // 128
            V('act', lambda e: e.activation(out=T3, in_=ps[:, :], func=AF.Sigmoid), r=[pk], w=['t3_0'])
            V('dve', lambda e: e.tensor_tensor(out=YB[:, c, :], in0=T3, in1=Z[:, c, :], op=ALU.mult), r=['t3_0', 'z%d' % c], w=['yb%d' % c])
        dense(s5_w_glu[l], 8, blocks(0, 1024), lambda kc, half: Z[:, kc, half * 512:(half + 1) * 512], lambda kc: ['z%d' % kc], cons_glu)
        dbg('strf', STRf, ['st5'])
        for d in range(2):
            OUTT = Tf[:, base: base + 256].rearrange("p (x r) -> p x r", r=2)
            for ri, src in ((0, STRf), (1, STIf)):
                V('pe', lambda e: e.transpose(out=pc[:, 0:128], in_=src[:, d * 128:(d + 1) * 128], identity=IDF), r=['st5', 'idf'], w=['pc'])
                V('dve', lambda e: e.tensor_copy(out=OUTT[:, :, ri], in_=pc[:, 0:128]), r=['pc'], w=['outt'])
            dbg('outt', OUTT.rearrange("p x r -> p (x r)"), ['outt'])
            s.dma('sp', ns5_out[l, d].rearrange("g q x -> (g q) x"), OUTT.rearrange("p x r -> p (x r)"), reads=['outt'], is_output=True)
        s.barrier()

    def na_branch(l):
        OFFQ, OFFK, OFFV = 3072, 4096, 5120
        QT = Tb[:, 0:8192].rearrange("p (h t) -> p h t", t=NT)
        KT = Tb[:, 8192:16384].rearrange("p (h t) -> p h t", t=NT)
        VT = Tb[:, 16384:24576].rearrange("p (a c) -> p a c", c=1024)
        base = 12288
        BIAS = Tf[:, base: base + 640]
        SP = Tf[:, base + 640: base + 1280]
        Pb = Tb[:, 2 * (base + 1280): 2 * (base + 1280) + 1152]
        PT = Tb[:, 2 * (base + 1856): 2 * (base + 1856) + 1152]
        KCT = Tb[:, 2 * (base + 2432): 2 * (base + 2432) + 512]
        CKh = Tb[:, 2 * (base + 2688): 2 * (base + 2688) + 512].rearrange("p (a c) -> p a c", c=128)
        CVh = Tb[:, 2 * (base + 2944): 2 * (base + 2944) + 512].rearrange("p (a c) -> p a c", c=128)
        STAT = Tf[:, base + 3200: base + 3216]
        OST = Tf[:, base + 3216: base + 3216 + 512]
        assert base + 3728 <= 16384

        def cons_q(col, ps, pk):
            h = (col - OFFQ) // 128
            V('act', lambda e: e.activation(out=QT[:, h, :], in_=ps[:, :], func=AF.Identity, scale=128.0 ** -0.5), r=[pk], w=['qt'])

        def cons_k(col, ps, pk):
            h = (col - OFFK) // 128
            V('act', lambda e: e.activation(out=KT[:, h, :], in_=ps[:, :], func=AF.Identity), r=[pk], w=['kt'])
        dense(w_in[l], 16, blocks(OFFQ, 1024), xn_rhs, xn_keys, cons_q)
        dense(w_in[l], 16, blocks(OFFK, 1024), xn_rhs, xn_keys, cons_k)
        ckpt(7.1)
        wv = w_in[l].rearrange("(k p) n -> p k n", p=128)
        for which, off in ((0, OFFK), (1, OFFV)):
            for cb_ in range(2):
                bi = wcount[0] % 2
                wcount[0] += 1
                wt = Wb[bi].rearrange("p (k n) -> p k n", n=512)
                s.dma('pool', wt, wv[:, :, off + cb_ * 512: off + (cb_ + 1) * 512], writes=['W%d' % bi], fence=False)
                for a in range(8):
                    for kc in range(16):
                        V('pe', lambda e: e.matmul(pc[:, :], lhsT=XN[:, kc, a * 128:(a + 1) * 128], rhs=wt[:, kc, :], start=(kc == 0), stop=(kc == 15)),
                          r=['W%d' % bi, 'xn%d' % kc], w=['pc'])
                    V('act', lambda e: e.activation(out=OST, in_=pc[:, :], func=AF.Identity), r=['pc'], w=['ost'])
                    if which == 1:
                        V('dve', lambda e: e.tensor_copy(out=VT[:, a, cb_ * 512:(cb_ + 1) * 512], in_=OST), r=['ost'], w=['vt'])
                    dst = (nk_out if which == 0 else nv_out)[l, a * 128:(a + 1) * 128, cb_ * 512:(cb_ + 1) * 512]
                    s.dma('sp', dst, OST, reads=['ost'], is_output=True)
                    ckpt(7.15)
                    if which == 0 and cb_ == 1 and a == 7:
                        ckpt(7.16)
                    if which == 0 and cb_ == 0 and a == 3:
                        ckpt(7.155)
        ckpt(7.2)
        V('dve', lambda e: e.memset(BIAS, 0.0), w=['bias0', 'bias1'])
        ckv = ck[l].rearrange("(a p) c -> p a c", p=128)
        cvv = cv[l].rearrange("(a p) c -> p a c", p=128)
        for h in range(8):
            s.dma('pool', CKh, ckv[:, :, h * 128:(h + 1) * 128], writes=['ckh'])
            s.dma('pool', CVh, cvv[:, :, h * 128:(h + 1) * 128], writes=['cvh'])
            for a in range(4):
                V('pe', lambda e: e.transpose(out=pd[:, a * 128:(a + 1) * 128], in_=CKh[:, a, :], identity=IDB), r=['ckh', 'idb'], w=['pd'])
            V('act', lambda e: e.activation(out=KCT, in_=pd[:, 0:512], func=AF.Identity), r=['pd'], w=['kct'])
            ckpt(7.3)
            for i in range(8):
                lo = min(max(i - 2, 0), 3)
                qs = slice(i * 128, (i + 1) * 128)
                kr0, qr0 = 2 * lo, 2 * i
                for ql in range(2):
                    r_first = kr0 - qr0 - ql + 7
                    rl = max(0, -r_first)
                    rh = min(10, 15 - r_first)
                    src = tp[l, h, r_first + rl: r_first + rh].rearrange("r q k -> q r k")
                    s.dma('sp', BIAS[ql * 64:(ql + 1) * 64, rl * 64: rh * 64].rearrange("p (r k) -> p r k", k=64), src, writes=['bias%d' % ql])
                V('pe', lambda e: e.matmul(pb[:, 0:512], lhsT=QT[:, h, qs], rhs=KT[:, h, lo * 128: lo * 128 + 512], start=True, stop=True), r=['qt', 'kt'], w=['pb'])
                V('pe', lambda e: e.matmul(pb[:, 512:640], lhsT=QT[:, h, qs], rhs=KT[:, h, lo * 128 + 512: lo * 128 + 640], start=True, stop=True), r=['qt', 'kt'], w=['pb'])
                V('pe', lambda e: e.matmul(pc[:, :], lhsT=QT[:, h, qs], rhs=KCT, start=True, stop=True), r=['qt', 'kct'], w=['pc'])
                ckpt(7.4)
                V('dve', lambda e: e.tensor_tensor(out=SP, in0=pb[:, 0:640], in1=BIAS, op=ALU.add), r=['pb', 'bias0', 'bias1'], w=['sp'])
                V('dve', lambda e: e.tensor_tensor(out=SP, in0=SP, in1=MASK[:, i, :], op=ALU.add), r=['sp', 'mask'], w=['sp'])
                V('dve', lambda e: e.tensor_reduce(out=STAT[:, 0:1], in_=SP, axis=AX.X, op=ALU.max), r=['sp'], w=['stat'])
                V('dve', lambda e: e.tensor_reduce(out=STAT[:, 1:2], in_=pc[:, :], axis=AX.X, op=ALU.max), r=['pc'], w=['stat'])
                V('dve', lambda e: e.scalar_tensor_tensor(out=STAT[:, 2:3], in0=STAT[:, 1:2], scalar=CB, in1=STAT[:, 0:1], op0=ALU.add, op1=ALU.max), r=['stat', 'flags'], w=['stat'])
                V('dve', lambda e: e.tensor_scalar(out=STAT[:, 3:4], in0=STAT[:, 2:3], scalar1=-1.0, scalar2=None, op0=ALU.mult), r=['stat'], w=['stat'])
                V('dve', lambda e: e.tensor_tensor(out=STAT[:, 4:5], in0=STAT[:, 3:4], in1=CB, op=ALU.add), r=['stat', 'flags'], w=['stat'])
                V('act', lambda e: e.activation(out=Pb[:, 0:640], in_=SP, func=AF.Exp, bias=STAT[:, 3:4], accum_out=STAT[:, 5:6]), r=['sp', 'stat'], w=['p', 'stat'])
                V('act', lambda e: e.activation(out=Pb[:, 640:1152], in_=pc[:, :], func=AF.Exp, bias=STAT[:, 4:5], accum_out=STAT[:, 6:7]), r=['pc', 'stat'], w=['p', 'stat'])
                V('dve', lambda e: e.tensor_tensor(out=STAT[:, 7:8], in0=STAT[:, 5:6], in1=STAT[:, 6:7], op=ALU.add), r=['stat'], w=['stat'])
                V('dve', lambda e: e.reciprocal(out=STAT[:, 7:8], in_=STAT[:, 7:8]), r=['stat'], w=['stat'])
                V('dve', lambda e: e.tensor_scalar(out=Pb, in0=Pb, scalar1=STAT[:, 7:8], scalar2=None, op0=ALU.mult), r=['p', 'stat'], w=['p'])
                ckpt(7.5)
                for j0, nj in ((0, 5), (5, 4)):
                    for j in range(nj):
                        V('pe', lambda e: e.transpose(out=pd[:, j * 128:(j + 1) * 128], in_=Pb[:, (j0 + j) * 128:(j0 + j + 1) * 128], identity=IDB), r=['p', 'idb'], w=['pd'])
                    V('act', lambda e: e.activation(out=PT[:, j0 * 128:(j0 + nj) * 128], in_=pd[:, 0:nj * 128], func=AF.Identity), r=['pd'], w=['pt'])
                for j in range(9):
                    if j < 5:
                        lhs = VT[:, lo + j, h * 128:(h + 1) * 128]
                        rk = ['vt']
                    else:
                        lhs = CVh[:, j - 5, :]
                        rk = ['cvh']
                    V('pe', lambda e: e.matmul(pa[0][:, 0:128], lhsT=lhs, rhs=PT[:, j * 128:(j + 1) * 128], start=(j == 0), stop=(j == 8)), r=rk + ['pt'], w=['pa0'])
                V('act', lambda e: e.activation(out=YB[:, h, qs], in_=pa[0][:, 0:128], func=AF.Identity), r=['pa0'], w=['yb%d' % h])
                ckpt(7.6)
        s.barrier()

    def ffn(l):
        def cons_a(col, ps, pk):
            c = col // 128
            V('act', lambda e: e.activation(out=HID[:, c, :], in_=ps[:, :], func=AF.Silu), r=[pk], w=['hid%d' % c])

        def cons_b(col, ps, pk):
            c = (col - FF) // 128
            V('dve', lambda e: e.tensor_tensor(out=HID[:, c, :], in0=ps[:, :], in1=HID[:, c, :], op=ALU.mult), r=[pk, 'hid%d' % c], w=['hid%d' % c])
        for blk in range(11):
            dense(w_ffn_in[l], 16, [(blk * 512, 512)], xn_rhs, xn_keys, cons_a)
            dense(w_ffn_in[l], 16, [(FF + blk * 512, 512)], xn_rhs, xn_keys, cons_b)
        s.barrier()
        wv = w_ffn_out[l].rearrange("(k p) n -> p k n", p=128)
        for c in range(16):
            bi = wcount[0] % 2
            wcount[0] += 1
            wt = Wb[bi][:, 0:44 * 128].rearrange("p (k n) -> p k n", n=128)
            s.dma('pool', wt, wv[:, :, c * 128:(c + 1) * 128], writes=['W%d' % bi], fence=False)
            pi = dense.pcount % 2
            dense.pcount += 1
            ps = pa[pi]
            for half in range(2):
                for kc in range(44):
                    V('pe', lambda e: e.matmul(ps[:, half * 512:(half + 1) * 512], lhsT=wt[:, kc, :],
                                               rhs=HID[:, kc, half * 512:(half + 1) * 512], start=(kc == 0), stop=(kc == 43)),
                      r=['W%d' % bi, 'hid%d' % kc], w=['pa%d' % pi])
            V('act', lambda e: e.activation(out=XN[:, c, :], in_=ps[:, :], func=AF.Identity), r=['pa%d' % pi], w=['xn%d' % c])
        s.barrier()

    try:
      ckpt(0)
      for l in range(NL):
        load_params(l)
        ckpt(1)
        modulation(l)
        s.barrier()
        ckpt(2)
        norm_to_xn(352, 256)
        ckpt(3)
        s5_branch(l)
        ckpt(4)
        merge_branch(l, 0, 6144 + 1 * 2048)
        ckpt(5)
        rg_branch(l)
        ckpt(6)
        merge_branch(l, 1, 6144 + 0 * 2048)
        ckpt(7)
        na_branch(l)
        ckpt(8)
        merge_branch(l, 2, 6144 + 2 * 2048)
        ckpt(9)
        def cons_o(col, ps, pk):
            c = col // 128
            V('act', lambda e: e.activation(out=XN[:, c, :], in_=ps[:, :], func=AF.Identity), r=[pk], w=['xn%d' % c])
        dense(w_o[l], 16, blocks(0, D), lambda kc, half: Mb[:, kc, half * 512:(half + 1) * 512], lambda kc: ['m%d' % kc], cons_o)
        s.barrier()
        ckpt(10)
        post_norm_residual(384)
        ckpt(11)
        norm_to_xn(368, 256 + 48)
        ffn(l)
        ckpt(12)
        post_norm_residual(400)
    except _Stop:
        pass

    for a in range(8):
        for hh in range(2):
            stage = Tf[:, 0:1024].rearrange("p (c t) -> p c t", t=128)
            s.dma('sp', stage, xs[hh * 8:(hh + 1) * 8, :, a * 128:(a + 1) * 128].rearrange("c p t -> p c t"), reads=['xs'], writes=['t_stage'])
            for cc in range(8):
                V('pe', lambda e: e.transpose(out=pb[:, cc * 128:(cc + 1) * 128], in_=stage[:, cc, :], identity=IDF), r=['t_stage', 'idf'], w=['pb'])
            outst = Tf[:, 1024:2048]
            V('act', lambda e: e.activation(out=outst, in_=pb[:, :], func=AF.Identity), r=['pb'], w=['t_out'])
            s.dma('sp', y_out[a * 128:(a + 1) * 128, hh * 1024:(hh + 1) * 1024], outst, reads=['t_out'], is_output=True)
    s.finish()
    return nc, s


def _na_mask_sample():
    m = np.full((8, 128, 640), NEG, np.float32)
    for i in range(8):
        lo = min(max(i - 2, 0), 3)
        for ql in range(2):
            qr = 2 * i + ql
            rs = min(max(qr - 4, 0), 8)
            for qc in range(64):
                ws = min(max(qc - 8, 0), 48)
                for krel in range(10):
                    kr = 2 * lo + krel
                    if rs <= kr < rs + 8:
                        m[i, ql * 64 + qc, krel * 64 + ws: krel * 64 + ws + 16] = 0.0
    return m


def _na_mask_prompt():
    m = np.full((8, 128, 640), NEG, np.float32)
    for i in range(8):
        lo = min(max(i - 2, 0), 3)
        seq = i // 2
        for j in range(5):
            if (lo + j) // 2 == seq:
                m[i, :, j * 128:(j + 1) * 128] = 0.0
    return m


def make_in_maps(inp, cores=range(8), L=DEPTH):
    f = lambda a: np.ascontiguousarray(np.asarray(a, dtype=np.float32))
    I = {k: f(v) for k, v in inp.items()}
    shared = {}
    shared["w_mod"] = I["w_mod"]
    shared["b_mod"] = I["b_mod"].reshape(L, 96, 128)
    shared["gvec"] = f(np.concatenate([I["g_mix_pre"].reshape(L, 16, 128), I["g_mix_post"].reshape(L, 16, 128),
                                       I["g_ffn_pre"].reshape(L, 16, 128), I["g_ffn_post"].reshape(L, 16, 128)], axis=1))
    shared["w_in"] = I["w_in"]
    shared["rgv"] = f(np.concatenate([I["rg_conv_w"].reshape(L, 32, 128), I["rg_conv_b"].reshape(L, 8, 128),
                                      I["rg_gate_b"].reshape(L, 32, 128), I["rg_lambda"].reshape(L, 16, 128),
                                      I["s5_d"].reshape(L, 8, 128)], axis=1))
    shared["rg_gate_w"] = I["rg_gate_w"].reshape(L, 32, 128, 128)
    shared["s5_lam"] = f(np.concatenate([I["s5_lambda_re"].reshape(L, 64, 128), I["s5_lambda_im"].reshape(L, 64, 128)], axis=1))
    shared["s5_ldt"] = I["s5_log_dt"].reshape(L, 64, 2)
    shared["s5_b"] = f(np.stack([I["s5_b_re"], I["s5_b_im"]], axis=1))
    shared["s5_c"] = f(np.stack([I["s5_c_re"], I["s5_c_im"]], axis=1))
    shared["s5_w_glu"] = I["s5_w_glu"]
    shared["w_bo"] = f(np.stack([I["w_s5_out"], I["w_rg_out"], I["w_na_out"]], axis=1))
    shared["w_o"] = I["w_o"]
    shared["w_ffn_in"] = I["w_ffn_in"]
    shared["w_ffn_out"] = I["w_ffn_out"]
    shared["ident"] = np.eye(128, dtype=np.float32)
    shared["iota1"] = f(np.broadcast_to(np.arange(1, NT + 1, dtype=np.float32), (128, NT)))
    rpb = I["na_rpb"]
    qc = np.arange(64)[:, None]
    kc = np.arange(64)[None, :]
    cidx = np.clip(kc - qc, -15, 15) + 15
    tp_s = f(rpb[:, :, :, cidx])
    tp_p = np.zeros_like(tp_s)
    mask_s = _na_mask_sample()
    mask_p = _na_mask_prompt()
    maps = []
    for core in cores:
        m = dict(shared)
        if core < 4:
            m["xin"] = f(I["x_prompt"][4 * core:4 * core + 4].reshape(NT, D))
            m["cond"] = I["c_ctx"].reshape(16, 128)
            m["tp"] = tp_p
            m["ck"] = np.zeros((L, 512, 1024), np.float32)
            m["cv"] = np.zeros((L, 512, 1024), np.float32)
            m["st_rg"] = np.zeros((L, 16, 128), np.float32)
            m["st_s5"] = np.zeros((L, 64, 256), np.float32)
            m["amask"] = mask_p
            fl = np.zeros((128, 4), np.float32)
            fl[:, 1] = NEG
            fl[:, 2] = -1.0
        else:
            b = core - 4
            m["xin"] = f(I["x_sample"][b])
            m["cond"] = I["c"][b].reshape(16, 128)
            m["tp"] = tp_s
            m["ck"] = f(I["cache_na_k"][b].reshape(L, 512, 1024))
            m["cv"] = f(I["cache_na_v"][b].reshape(L, 512, 1024))
            m["st_rg"] = f(I["state_rglru"][b].reshape(L, 16, 128))
            m["st_s5"] = f(I["state_s5"][b].reshape(L, 64, 256))
            m["amask"] = mask_s
            fl = np.zeros((128, 4), np.float32)
            fl[:, 0] = 1.0
        m["flags"] = fl
        maps.append(m)
    return maps


_CACHE = {}


def kernel(**inputs):
    if "nc" not in _CACHE:
        _CACHE["nc"] = build(DEPTH)[0]
    nc = _CACHE["nc"]
    maps = make_in_maps(inputs)
    res = run_bass_kernel_spmd(nc, maps, core_ids=list(range(8)))
    R = res.results
    L = DEPTH
    y_prompt = np.concatenate([R[c]["y"].reshape(4, 256, D) for c in range(4)], axis=0)
    y_sample = np.stack([R[c]["y"] for c in range(4, 8)], axis=0)
    nk = np.concatenate([R[c]["nk"].reshape(L, 4, 256, 8, 128).transpose(1, 0, 2, 3, 4) for c in range(4)], axis=0)
    nv = np.concatenate([R[c]["nv"].reshape(L, 4, 256, 8, 128).transpose(1, 0, 2, 3, 4) for c in range(4)], axis=0)
    nrg = np.concatenate([R[c]["nrg"].reshape(L, 2, 4, 1024).transpose(2, 0, 1, 3) for c in range(4)], axis=0)
    ns5 = np.concatenate([R[c]["ns5"].reshape(L, 2, 32, 4, 2, 64, 2).transpose(3, 0, 1, 2, 4, 5, 6).reshape(4, L, 2, 64, 64, 2)
                          for c in range(4)], axis=0)
    f = lambda a: np.ascontiguousarray(a, dtype=np.float32)
    return (f(y_prompt), f(y_sample), f(nk), f(nv), f(nrg), f(ns5))
```

```python
import math
import numpy as np
import concourse.bass as bass
import concourse.mybir as mybir
from concourse.bass_utils import run_bass_kernel_spmd

F32 = mybir.dt.float32
BF16 = mybir.dt.bfloat16
AF = mybir.ActivationFunctionType
ALU = mybir.AluOpType
AX = mybir.AxisListType

D = 2048
DEPTH = 4
NT = 1024
FF = 5632
NEG = -1e30
TWO_PI = 2.0 * math.pi


class Sched:
    ENG = ['pe', 'act', 'dve', 'pool', 'sp']

    def __init__(self, nc, n_dma_sems=32):
        self.nc = nc
        self.eng = dict(pe=nc.tensor, act=nc.scalar, dve=nc.vector, pool=nc.gpsimd, sp=nc.sync)
        self.sem = {e: nc.alloc_semaphore("cnt_" + e) for e in self.ENG}
        self.cnt = {e: 0 for e in self.ENG}
        self.seen = {e: {} for e in self.ENG}
        self.last_w = {}
        self.readers = {}
        self.dma_sems = [nc.alloc_semaphore("dma%d" % i) for i in range(n_dma_sems)]
        self.dma_tot = [0] * n_dma_sems
        self.dma_rr = 0
        self.out_tokens = []
        self.fence_tokens = []
        self.n_ins = 0

    def _wait(self, e, tok):
        sem, val = tok
        key = id(sem)
        if self.seen[e].get(key, 0) >= val:
            return
        self.seen[e][key] = val
        self.eng[e].wait_ge(sem, val)
        self.n_ins += 1

    def _deps(self, e, reads, writes):
        toks = []
        for k in reads:
            t = self.last_w.get(k)
            if t is not None:
                toks.append(t)
        for k in writes:
            t = self.last_w.get(k)
            if t is not None:
                toks.append(t)
            toks.extend(self.readers.get(k, ()))
        for t in toks:
            if e == 'pe' and t[0] is self.sem['pe']:
                continue
            self._wait(e, t)

    def _commit(self, tok, reads, writes):
        for k in reads:
            self.readers.setdefault(k, []).append(tok)
        for k in writes:
            self.last_w[k] = tok
            self.readers[k] = []

    def op(self, e, fn, reads=(), writes=()):
        self._deps(e, reads, writes)
        ins = fn(self.eng[e])
        self.cnt[e] += 1
        ins.then_inc(self.sem[e], 1)
        tok = (self.sem[e], self.cnt[e])
        self._commit(tok, reads, writes)
        self.n_ins += 1
        return tok

    def dma(self, q, out, in_, reads=(), writes=(), is_output=False, fence=True, **kw):
        k = self.dma_rr
        self.dma_rr = (self.dma_rr + 1) % len(self.dma_sems)
        sem = self.dma_sems[k]
        if self.dma_tot[k] > 0:
            self._wait(q, (sem, self.dma_tot[k]))
        self._deps(q, reads, writes)
        ins = self.eng[q].dma_start(out=out, in_=in_, **kw)
        self.dma_tot[k] += 16
        ins.then_inc(sem, 16)
        tok = (sem, self.dma_tot[k])
        self._commit(tok, reads, writes)
        if is_output:
            self.out_tokens.append(tok)
        if fence:
            self.fence_tokens.append(tok)
        self.n_ins += 1
        return tok

    def barrier(self):
        for e in self.ENG:
            for f in self.ENG:
                if f != e and self.cnt[f] > 0:
                    self._wait(e, (self.sem[f], self.cnt[f]))
            for t in self.fence_tokens:
                self._wait(e, t)
        self.fence_tokens = []

    def finish(self, e='sp'):
        for t in self.out_tokens:
            self._wait(e, t)
        for f in self.ENG:
            if self.cnt[f] > 0 and f != e:
                self._wait(e, (self.sem[f], self.cnt[f]))


class _Stop(Exception):
    pass


def build(NL=DEPTH, STOP=None, LW=DEPTH, DBG=False):
    nc = bass.Bass("TRN2", target_bir_lowering=False)

    def ckpt(k):
        if STOP is not None and k == STOP:
            raise _Stop()

    def din(name, shape):
        return nc.dram_tensor(name, list(shape), F32, kind="ExternalInput").ap()

    def dout(name, shape):
        return nc.dram_tensor(name, list(shape), F32, kind="ExternalOutput").ap()

    xin = din("xin", [NT, D])
    cond = din("cond", [16, 128])
    w_mod = din("w_mod", [LW, D, 6 * D])
    b_mod = din("b_mod", [LW, 96, 128])
    gvec = din("gvec", [LW, 64, 128])
    w_in = din("w_in", [LW, D, 12288])
    rgv = din("rgv", [LW, 96, 128])
    rg_gate_w = din("rg_gate_w", [LW, 32, 128, 128])
    s5_lam = din("s5_lam", [LW, 128, 128])
    s5_ldt = din("s5_ldt", [LW, 64, 2])
    s5_b = din("s5_b", [LW, 2, 2, 64, 64, 16])
    s5_c = din("s5_c", [LW, 2, 2, 64, 16, 64])
    s5_w_glu = din("s5_w_glu", [LW, 1024, 1024])
    tp = din("tp", [LW, 8, 15, 64, 64])
    w_bo = din("w_bo", [LW, 3, 1024, D])
    w_o = din("w_o", [LW, D, D])
    w_ffn_in = din("w_ffn_in", [LW, D, 2 * FF])
    w_ffn_out = din("w_ffn_out", [LW, FF, D])
    ck = din("ck", [LW, 512, 1024])
    cv = din("cv", [LW, 512, 1024])
    st_rg = din("st_rg", [LW, 16, 128])
    st_s5 = din("st_s5", [LW, 64, 256])
    amask = din("amask", [8, 128, 640])
    flags = din("flags", [128, 4])
    ident_d = din("ident", [128, 128])
    iota_d = din("iota1", [128, NT])

    y_out = dout("y", [NT, D])
    nk_out = dout("nk", [LW, NT, 1024])
    nv_out = dout("nv", [LW, NT, 1024])
    nrg_out = dout("nrg", [LW, 2, 4, 8, 128])
    ns5_out = dout("ns5", [LW, 2, 32, 4, 256])
    xs = nc.dram_tensor("xs", [16, 128, NT], F32).ap()

    s = Sched(nc)
    dbg_seen = set()

    def dbg(name, ap, keys):
        if not DBG or name in dbg_seen:
            return
        dbg_seen.add(name)
        dt_ = nc.dram_tensor("dbg_" + name, list(ap.shape), ap.dtype, kind="ExternalOutput").ap()
        s.dma('sp', dt_, ap, reads=keys, is_output=True)

    TOT = 52400
    big = nc.alloc_sbuf_tensor("big", [128, TOT], F32)
    cur = [0]

    def carve(nwords):
        a = cur[0]
        cur[0] += nwords
        assert cur[0] <= TOT, cur[0]
        return a

    def f32v(off, n):
        return big[:, off:off + n]

    def bf16v(off, nwords):
        return big[:, off:off + nwords].bitcast(BF16)

    oW = [carve(4096), carve(4096)]
    oXN = carve(8192)
    oT = carve(16384)
    oM = carve(8192)
    oYB = carve(4096)
    oMASK = carve(2560)
    oPAR = carve(512)
    oRSTD = carve(1024)
    oONES = carve(128)
    oIOTA = carve(1024)
    oIDF = carve(128)
    oIDB = carve(64)
    oSMALL = carve(512)
    oSTG = carve(256)
    oXST = carve(1024)
    oCOND = carve(16)
    oONESB = carve(64)

    Wb = [bf16v(o, 4096) for o in oW]
    XN = bf16v(oXN, 8192).rearrange("p (c t) -> p c t", t=NT)
    Tf = f32v(oT, 16384)
    Tb = bf16v(oT, 16384)
    Mb = bf16v(oM, 8192).rearrange("p (c t) -> p c t", t=NT)
    HID = bf16v(oT, 16384 + 8192).rearrange("p (c t) -> p c t", t=NT)[:, 0:44, :]
    YB = bf16v(oYB, 4096).rearrange("p (c t) -> p c t", t=NT)
    MASK = bf16v(oMASK, 2560).rearrange("p (i k) -> p i k", k=640)
    PAR = f32v(oPAR, 512)
    RSTD = f32v(oRSTD, 1024)
    ONES = f32v(oONES, 128)
    IOTA = f32v(oIOTA, 1024)
    IDF = f32v(oIDF, 128)
    IDB = bf16v(oIDB, 64)
    SM = f32v(oSMALL, 512)
    GW = Wb[1][:, 0:4096].rearrange("p (a j) -> p a j", j=128)
    STG = f32v(oSTG, 256)
    XST = f32v(oXST, 1024)
    CONDT = f32v(oCOND, 16)

    pa = [nc.alloc_psum_tensor("pa0", [128, 1024], F32), nc.alloc_psum_tensor("pa1", [128, 1024], F32)]
    pb = nc.alloc_psum_tensor("pb", [128, 1024], F32)
    pc = nc.alloc_psum_tensor("pc", [128, 512], F32)
    pd = nc.alloc_psum_tensor("pd", [128, 1024], BF16)

    def V(e, fn, r=(), w=()):
        return s.op(e, fn, reads=r, writes=w)

    def transpose_f32(dst, src, rows, key_src, key_dst, eng='dve'):
        V('pe', lambda e: e.transpose(out=pc[:, 0:rows], in_=src, identity=IDF[0:rows, 0:rows]), r=[key_src, 'idf'], w=['pc'])
        V(eng, lambda e: e.tensor_copy(out=dst, in_=pc[:, 0:rows]), r=['pc'], w=[key_dst])

    wcount = [0]

    def dense(wsrc, nK, cols, rhs_fn, rhs_keys, consume, m_tiles=None):
        wv = wsrc.rearrange("(k p) n -> p k n", p=128)
        kmax = 8192 // 512
        for (c0, ncol) in cols:
            kper = min(nK, 8192 // ncol)
            assert nK % kper == 0
            nkb = nK // kper
            bufs = []
            for kb in range(nkb):
                bi = wcount[0] % 2
                wcount[0] += 1
                wt = Wb[bi][:, 0:kper * ncol].rearrange("p (k n) -> p k n", n=ncol)
                s.dma('pool', wt, wv[:, kb * kper:(kb + 1) * kper, c0:c0 + ncol], writes=['W%d' % bi], fence=False)
                bufs.append((bi, wt))
                if nkb > 1 and kb < nkb - 1:
                    pass
            assert nkb <= 2
            for mt in range(ncol // 128):
                pi = dense.pcount % 2
                dense.pcount += 1
                ps = pa[pi]
                for half in range(2):
                    for kc in range(nK):
                        bi, wt = bufs[kc // kper]
                        V('pe', lambda e: e.matmul(ps[:, half * 512:(half + 1) * 512], lhsT=wt[:, kc % kper, mt * 128:(mt + 1) * 128],
                                                   rhs=rhs_fn(kc, half), start=(kc == 0), stop=(kc == nK - 1)),
                          r=['W%d' % bi] + rhs_keys(kc), w=['pa%d' % pi])
                consume(c0 + mt * 128, ps, 'pa%d' % pi)
    dense.pcount = 0

    def blocks(c0, n, bs=512):
        return [(c0 + i, min(bs, n - i)) for i in range(0, n, bs)]

    s.dma('sp', IDF, ident_d, writes=['idf'])
    s.dma('sp', IOTA, iota_d, writes=['iota'])
    s.dma('sp', SM[:, 0:4], flags, writes=['flags'])
    s.dma('pool', MASK, amask.rearrange("i p k -> p i k"), writes=['mask'])
    V('dve', lambda e: e.memset(ONES, 1.0), w=['ones'])
    V('dve', lambda e: e.tensor_copy(out=IDB, in_=IDF), r=['idf'], w=['idb'])
    FLAG = SM[:, 0:1]
    CB = SM[:, 1:2]
    FLM1 = SM[:, 2:3]
    HPI = SM[:, 4:5]
    V('dve', lambda e: e.memset(HPI, math.pi / 2), w=['hpi'])
    s.dma('sp', STG[0:16, 0:128], cond, writes=['stg'])
    transpose_f32(CONDT, STG[0:16, 0:128], 16, 'stg', 'condt')
    SC = bf16v(oSMALL + 8, 8)
    V('act', lambda e: e.activation(out=SC, in_=CONDT, func=AF.Silu), r=['condt'], w=['sc'])

    xin_v = xin.rearrange("(a p) d -> p a d", p=128)
    for a in range(8):
        for hh in range(2):
            stage = Tf[:, 0:1024]
            s.dma('sp', stage, xin_v[:, a, hh * 1024:(hh + 1) * 1024], writes=['t_stage'])
            for cc in range(8):
                c = hh * 8 + cc
                V('pe', lambda e: e.transpose(out=pb[:, cc * 128:(cc + 1) * 128], in_=stage[:, cc * 128:(cc + 1) * 128], identity=IDF),
                  r=['t_stage', 'idf'], w=['pb'])
            outst = Tf[:, 1024:2048]
            V('act', lambda e: e.activation(out=outst, in_=pb[:, :], func=AF.Identity), r=['pb'], w=['t_out'])
            s.dma('sp', xs[hh * 8:(hh + 1) * 8, :, a * 128:(a + 1) * 128].rearrange("c p t -> p c t"),
                  outst.rearrange("p (c t) -> p c t", t=128), reads=['t_out'], writes=['xs'])
    s.barrier()

    def load_params(l):
        s.dma('sp', STG[0:96, 0:128], b_mod[l], writes=['stg'])
        transpose_f32(PAR[:, 0:96], STG[0:96, 0:128], 96, 'stg', 'par')
        s.dma('sp', STG[0:64, 0:128], gvec[l], writes=['stg'])
        transpose_f32(PAR[:, 96:160], STG[0:64, 0:128], 64, 'stg', 'par')
        s.dma('sp', STG[0:96, 0:128], rgv[l], writes=['stg'])
        transpose_f32(PAR[:, 160:256], STG[0:96, 0:128], 96, 'stg', 'par')

    def modulation(l):
        MODP = PAR[:, 256:352]
        wv = w_mod[l].rearrange("(k p) n -> p k n", p=128)
        for blk in range(24):
            bi = wcount[0] % 2
            wcount[0] += 1
            wt = Wb[bi].rearrange("p (k n) -> p k n", n=512)
            s.dma('pool', wt, wv[:, :, blk * 512:(blk + 1) * 512], writes=['W%d' % bi], fence=False)
            for mt in range(4):
                j = blk * 4 + mt
                for kc in range(16):
                    V('pe', lambda e: e.matmul(pc[:, j:j + 1], lhsT=wt[:, kc, mt * 128:(mt + 1) * 128], rhs=SC[:, kc:kc + 1],
                                               start=(kc == 0), stop=(kc == 15)), r=['W%d' % bi, 'sc'], w=['pc'])
        V('dve', lambda e: e.tensor_tensor(out=MODP, in0=pc[:, 0:96], in1=PAR[:, 0:96], op=ALU.add), r=['pc', 'par'], w=['mod'])
        dbg('par', PAR, ['par', 'mod'])
        V('dve', lambda e: e.scalar_tensor_tensor(out=PAR[:, 352:368], in0=MODP[:, 16:32], scalar=1.0, in1=PAR[:, 96:112], op0=ALU.add, op1=ALU.mult),
          r=['mod', 'par'], w=['mod2'])
        V('dve', lambda e: e.scalar_tensor_tensor(out=PAR[:, 368:384], in0=MODP[:, 64:80], scalar=1.0, in1=PAR[:, 128:144], op0=ALU.add, op1=ALU.mult),
          r=['mod', 'par'], w=['mod2'])
        V('dve', lambda e: e.tensor_tensor(out=PAR[:, 384:400], in0=MODP[:, 32:48], in1=PAR[:, 112:128], op=ALU.mult), r=['mod', 'par'], w=['mod2'])
        V('dve', lambda e: e.tensor_tensor(out=PAR[:, 400:416], in0=MODP[:, 80:96], in1=PAR[:, 144:160], op=ALU.mult), r=['mod', 'par'], w=['mod2'])

    def norm_to_xn(gs_col, shift_col):
        Xr = Tf.rearrange("p (c t) -> p c t", t=NT)
        SQ = Mb
        for c in range(16):
            s.dma('sp', Xr[:, c, :], xs[c], reads=['xs'], writes=['x%d' % c])
            V('act', lambda e: e.activation(out=SQ[:, c, :], in_=Xr[:, c, :], func=AF.Square), r=['x%d' % c], w=['sq%d' % c])
        onesb = bf16v(oONESB, 64)
        V('dve', lambda e: e.tensor_copy(out=onesb, in_=ONES[:, 0:128]), r=['ones'], w=['onesb'])
        for half in range(2):
            for c in range(16):
                V('pe', lambda e: e.matmul(pb[:, half * 512:(half + 1) * 512], lhsT=onesb, rhs=SQ[:, c, half * 512:(half + 1) * 512],
                                           start=(c == 0), stop=(c == 15)), r=['onesb', 'sq%d' % c], w=['pb'])
        V('act', lambda e: e.activation(out=RSTD, in_=pb[:, :], func=AF.Sqrt, scale=1.0 / D, bias=EPS), r=['pb', 'eps'], w=['rstd'])
        V('dve', lambda e: e.reciprocal(out=RSTD, in_=RSTD), r=['rstd'], w=['rstd'])
        for c in range(16):
            V('dve', lambda e: e.tensor_tensor(out=Xr[:, c, :], in0=Xr[:, c, :], in1=RSTD, op=ALU.mult), r=['x%d' % c, 'rstd'], w=['x%d' % c])
            V('act', lambda e: e.activation(out=XN[:, c, :], in_=Xr[:, c, :], func=AF.Identity,
                                            scale=PAR[:, gs_col + c:gs_col + c + 1], bias=PAR[:, shift_col + c:shift_col + c + 1]),
              r=['x%d' % c, 'mod', 'mod2'], w=['xn%d' % c])
        dbg('rstd', RSTD, ['rstd'])
        dbg('xn0', XN[:, 0, :], ['xn0'])
        dbg('par2', PAR, ['par', 'mod', 'mod2'])
        s.barrier()

    EPS = SM[:, 5:6]
    V('dve', lambda e: e.memset(EPS, 1e-6), w=['eps'])

    def post_norm_residual(gg_col):
        SQ = Mb
        for c in range(16):
            V('act', lambda e: e.activation(out=SQ[:, c, :], in_=XN[:, c, :], func=AF.Square), r=['xn%d' % c], w=['sq%d' % c])
        onesb = bf16v(oONESB, 64)
        for half in range(2):
            for c in range(16):
                V('pe', lambda e: e.matmul(pb[:, half * 512:(half + 1) * 512], lhsT=onesb, rhs=SQ[:, c, half * 512:(half + 1) * 512],
                                           start=(c == 0), stop=(c == 15)), r=['onesb', 'sq%d' % c], w=['pb'])
        V('act', lambda e: e.activation(out=RSTD, in_=pb[:, :], func=AF.Sqrt, scale=1.0 / D, bias=EPS), r=['pb', 'eps'], w=['rstd'])
        V('dve', lambda e: e.reciprocal(out=RSTD, in_=RSTD), r=['rstd'], w=['rstd'])
        Xr = Tf.rearrange("p (c t) -> p c t", t=NT)
        for c in range(16):
            s.dma('sp', Xr[:, c, :], xs[c], reads=['xs'], writes=['x%d' % c])
            tmp = XST
            V('dve', lambda e: e.tensor_tensor(out=tmp, in0=XN[:, c, :], in1=RSTD, op=ALU.mult), r=['xn%d' % c, 'rstd'], w=['xst'])
            V('dve', lambda e: e.scalar_tensor_tensor(out=Xr[:, c, :], in0=tmp, scalar=PAR[:, gg_col + c:gg_col + c + 1], in1=Xr[:, c, :],
                                                      op0=ALU.mult, op1=ALU.add), r=['xst', 'x%d' % c, 'mod2'], w=['x%d' % c])
            s.dma('sp', xs[c], Xr[:, c, :], reads=['x%d' % c], writes=['xs'])
        s.barrier()

    def xn_rhs(kc, half):
        return XN[:, kc, half * 512:(half + 1) * 512]

    def xn_keys(kc):
        return ['xn%d' % kc]

    def merge_branch(l, b, act_col_base):
        SG = Tf[:, 0:1024]
        for blk in range(4):
            SGs = Tf[:, 0:4096].rearrange("p (m t) -> p m t", t=NT)

            def cons_gate(col, ps, pk):
                mt = ((col - act_col_base) // 128) % 4
                V('act', lambda e: e.activation(out=SGs[:, mt, :], in_=ps[:, :], func=AF.Sigmoid), r=[pk], w=['sg%d' % mt])
            dense(w_in[l], 16, [(act_col_base + blk * 512, 512)], xn_rhs, xn_keys, cons_gate)

            def cons_proj(col, ps, pk):
                c = col // 128
                mt = c % 4
                if b == 0:
                    V('dve', lambda e: e.tensor_tensor(out=Mb[:, c, :], in0=ps[:, :], in1=SGs[:, mt, :], op=ALU.mult), r=[pk, 'sg%d' % mt], w=['m%d' % c])
                else:
                    V('dve', lambda e: e.tensor_tensor(out=SGs[:, mt, :], in0=ps[:, :], in1=SGs[:, mt, :], op=ALU.mult), r=[pk, 'sg%d' % mt], w=['sg%d' % mt])
                    V('dve', lambda e: e.tensor_tensor(out=Mb[:, c, :], in0=Mb[:, c, :], in1=SGs[:, mt, :], op=ALU.add), r=['m%d' % c, 'sg%d' % mt], w=['m%d' % c])
            dense(w_bo[l, b], 8, [(blk * 512, 512)], lambda kc, half: YB[:, kc, half * 512:(half + 1) * 512], lambda kc: ['yb%d' % kc], cons_proj)
        s.barrier()

    def rg_branch(l):
        OFFX, OFFG = 0, 1024
        XR = Tb[:, 0:8192].rearrange("p (c t) -> p c t", t=NT)
        XG = Tb[:, 8192:16384].rearrange("p (c t) -> p c t", t=NT)
        base = 8192
        def tf(i):
            return Tf[:, base + i * 1024: base + (i + 1) * 1024]
        XC, R_, IG, A_, SQ_, HF, HB = tf(0), tf(1), tf(2), tf(3), tf(4), tf(5), tf(6)
        XCB = Tb[:, 2 * (base + 7 * 1024): 2 * (base + 7 * 1024) + 1024]
        V('act', lambda e: e.activation(out=SM[:, 16:32], in_=PAR[:, 232:248], func=AF.Exp, scale=-1.0), r=['par'], w=['cneg'])
        V('act', lambda e: e.activation(out=SM[:, 16:32], in_=SM[:, 16:32], func=AF.Ln, bias=1.0), r=['cneg'], w=['cneg'])
        V('dve', lambda e: e.tensor_scalar(out=SM[:, 32:48], in0=SM[:, 16:32], scalar1=-16.0, scalar2=None, op0=ALU.mult), r=['cneg'], w=['cneg2'])
        V('dve', lambda e: e.tensor_scalar(out=SM[:, 16:32], in0=SM[:, 16:32], scalar1=-8.0, scalar2=None, op0=ALU.mult), r=['cneg'], w=['cneg'])
        V('dve', lambda e: e.tensor_scalar(out=SM[:, 48:80], in0=PAR[:, 160:192], scalar1=FLM1, scalar2=None, op0=ALU.mult), r=['par', 'flags'], w=['wneg'])
        s.dma('sp', STG[0:16, 0:128], st_rg[l], writes=['stg'])
        transpose_f32(SM[:, 80:96], STG[0:16, 0:128], 16, 'stg', 'h0rg')
        STC = SM[:, 96:160].rearrange("p (d q c) -> p d q c", d=2, q=4)

        def cons_x(col, ps, pk):
            c = (col % 1024) // 128
            dst = XR if col < 1024 else XG
            V('act', lambda e: e.activation(out=dst[:, c, :], in_=ps[:, :], func=AF.Identity), r=[pk], w=['xrg'])
        dense(w_in[l], 16, blocks(OFFX, 2048), xn_rhs, xn_keys, cons_x)
        s.dma('pool', GW, rg_gate_w[l].rearrange("a i j -> i a j"), writes=['W1'], fence=False)

        for c in range(8):
            wc = lambda k: PAR[:, 160 + k * 8 + c:160 + k * 8 + c + 1]
            wn = lambda k: SM[:, 48 + k * 8 + c:48 + k * 8 + c + 1]
            x = XR[:, c, :]
            V('act', lambda e: e.activation(out=XC, in_=x, func=AF.Identity, scale=wc(1), bias=PAR[:, 192 + c:193 + c]), r=['xrg', 'par'], w=['xc'])
            V('dve', lambda e: e.scalar_tensor_tensor(out=XC[:, 1:NT], in0=x[:, 0:NT - 1], scalar=wc(0), in1=XC[:, 1:NT], op0=ALU.mult, op1=ALU.add), r=['xrg', 'xc'], w=['xc'])
            V('dve', lambda e: e.scalar_tensor_tensor(out=XC[:, 0:NT - 1], in0=x[:, 1:NT], scalar=wc(2), in1=XC[:, 0:NT - 1], op0=ALU.mult, op1=ALU.add), r=['xrg', 'xc'], w=['xc'])
            V('dve', lambda e: e.scalar_tensor_tensor(out=XC[:, 0:NT - 2], in0=x[:, 2:NT], scalar=wc(3), in1=XC[:, 0:NT - 2], op0=ALU.mult, op1=ALU.add), r=['xrg', 'xc'], w=['xc'])
            def cols(off):
                return slice(off, off + 513, 256)
            V('dve', lambda e: e.scalar_tensor_tensor(out=XC[:, cols(256)], in0=x[:, cols(255)], scalar=wn(0), in1=XC[:, cols(256)], op0=ALU.mult, op1=ALU.add), r=['xrg', 'xc', 'wneg'], w=['xc'])
            V('dve', lambda e: e.scalar_tensor_tensor(out=XC[:, cols(255)], in0=x[:, cols(256)], scalar=wn(2), in1=XC[:, cols(255)], op0=ALU.mult, op1=ALU.add), r=['xrg', 'xc', 'wneg'], w=['xc'])
            V('dve', lambda e: e.scalar_tensor_tensor(out=XC[:, cols(255)], in0=x[:, cols(257)], scalar=wn(3), in1=XC[:, cols(255)], op0=ALU.mult, op1=ALU.add), r=['xrg', 'xc', 'wneg'], w=['xc'])
            V('dve', lambda e: e.scalar_tensor_tensor(out=XC[:, cols(254)], in0=x[:, cols(256)], scalar=wn(3), in1=XC[:, cols(254)], op0=ALU.mult, op1=ALU.add), r=['xrg', 'xc', 'wneg'], w=['xc'])
            V('act', lambda e: e.activation(out=XCB, in_=XC, func=AF.Identity), r=['xc'], w=['xcb'])
            for d in range(2):
                H = HF if d == 0 else HB
                for g, dst in ((0, R_), (1, IG)):
                    pi = dense.pcount % 2
                    dense.pcount += 1
                    ps = pa[pi]
                    for half in range(2):
                        V('pe', lambda e: e.matmul(ps[:, half * 512:(half + 1) * 512], lhsT=GW[:, (d * 2 + g) * 8 + c, :], rhs=XCB[:, half * 512:(half + 1) * 512],
                                                   start=True, stop=True), r=['W1', 'xcb'], w=['pa%d' % pi])
                    bcol = 200 + (d * 2 + g) * 8 + c
                    V('act', lambda e: e.activation(out=dst, in_=ps[:, :], func=AF.Sigmoid, bias=PAR[:, bcol:bcol + 1]), r=['pa%d' % pi, 'par'], w=['rg_t%d' % g])
                V('act', lambda e: e.activation(out=A_, in_=R_, func=AF.Exp, scale=SM[:, 16 + d * 8 + c:17 + d * 8 + c]), r=['rg_t0', 'cneg'], w=['rg_a'])
                V('act', lambda e: e.activation(out=SQ_, in_=R_, func=AF.Exp, scale=SM[:, 32 + d * 8 + c:33 + d * 8 + c]), r=['rg_t0', 'cneg2'], w=['rg_s'])
                V('act', lambda e: e.activation(out=SQ_, in_=SQ_, func=AF.Sqrt, scale=-1.0, bias=1.0), r=['rg_s'], w=['rg_s'])
                V('dve', lambda e: e.tensor_tensor(out=SQ_, in0=SQ_, in1=IG, op=ALU.mult), r=['rg_s', 'rg_t1'], w=['rg_s'])
                V('dve', lambda e: e.tensor_tensor(out=SQ_, in0=SQ_, in1=XC, op=ALU.mult), r=['rg_s', 'xc'], w=['rg_s'])
                bc = cols(256) if d == 0 else cols(255)
                V('dve', lambda e: e.tensor_scalar(out=A_[:, bc], in0=A_[:, bc], scalar1=FLAG, scalar2=None, op0=ALU.mult), r=['rg_a', 'flags'], w=['rg_a'])
                h0 = SM[:, 80 + d * 8 + c:81 + d * 8 + c]
                if d == 0:
                    V('dve', lambda e: e.tensor_tensor_scan(out=H, data0=A_, data1=SQ_, initial=h0, op0=ALU.mult, op1=ALU.add), r=['rg_a', 'rg_s', 'h0rg'], w=['rg_h%d' % d])
                    V('dve', lambda e: e.tensor_copy(out=STC[:, 0, :, c], in_=H[:, 255:NT:256]), r=['rg_h0'], w=['stc'])
                else:
                    V('dve', lambda e: e.tensor_tensor_scan(out=H[:, ::-1], data0=A_[:, ::-1], data1=SQ_[:, ::-1], initial=h0, op0=ALU.mult, op1=ALU.add),
                      r=['rg_a', 'rg_s', 'h0rg'], w=['rg_h%d' % d])
                    V('dve', lambda e: e.tensor_copy(out=STC[:, 1, :, c], in_=H[:, 0:NT:256]), r=['rg_h1'], w=['stc'])
            V('dve', lambda e: e.tensor_tensor(out=HF, in0=HF, in1=HB, op=ALU.add), r=['rg_h0', 'rg_h1'], w=['rg_h0'])
            V('act', lambda e: e.activation(out=R_, in_=XG[:, c, :], func=AF.Gelu_apprx_tanh), r=['xrg', 'rg_t0'], w=['rg_t0'])
            V('dve', lambda e: e.tensor_tensor(out=YB[:, c, :], in0=HF, in1=R_, op=ALU.mult), r=['rg_h0', 'rg_t0'], w=['yb%d' % c])
        for d in range(2):
            src = SM[:, 96 + d * 32:96 + (d + 1) * 32]
            V('pe', lambda e: e.transpose(out=pc[0:32, 0:128], in_=src, identity=IDF), r=['stc', 'idf'], w=['pc'])
            V('dve', lambda e: e.tensor_copy(out=STG[0:32, 0:128], in_=pc[0:32, 0:128]), r=['pc'], w=['stg'])
            s.dma('sp', nrg_out[l, d].rearrange("q c h -> (q c) h"), STG[0:32, 0:128], reads=['stg'], is_output=True)
        s.barrier()

    def s5_branch(l):
        OFFU = 2048
        U = Tb[:, 0:8192].rearrange("p (c t) -> p c t", t=NT)
        Z = Mb[:, 0:8, :]
        base = 4096
        def tf(i):
            return Tf[:, base + i * 1024: base + (i + 1) * 1024]
        COS, SIN, A_, T1, T2, T3, GR, GI = [tf(i) for i in range(8)]
        HRb = Tb[:, 2 * (base + 8 * 1024): 2 * (base + 8 * 1024) + 1024]
        HIb = Tb[:, 2 * (base + 8 * 1024) + 1024: 2 * (base + 8 * 1024) + 2048]
        YBb = bf16v(oYB, 4096)
        BT = Wb[0][:, 0:4096].rearrange("p (a c m) -> p a c m", a=4, c=8)
        CTfl = YBb[:, 0:4 * 8 * 192]
        CT = CTfl.rearrange("p (a c m) -> p a c m", a=4, c=8)
        CTOFF = [0, 32, 64, 160]
        U3 = Tb[:, 2 * (base + 9 * 1024 + 1536): 2 * (base + 9 * 1024 + 1536) + 1024]
        assert base + 9 * 1024 + 1536 + 512 <= 16384
        M2 = f32v(oM + 4096, 4096)
        BZf = M2[:, 0:1024]
        BZ = BZf.rearrange("p (g m) -> p g m", m=32)
        CZf = M2[:, 1024:2048]
        CZ = CZf.rearrange("p (c m) -> p c m", m=128)
        CTf = M2[:, 2048:4096].rearrange("p (r c m) -> p r c m", r=2, c=8)
        zoff = base + 9 * 1024 - 4096
        s5p = Tf[:, zoff + 4096: zoff + 4096 + 1024]
        LRE, LIM = s5p[:, 0:64], s5p[:, 64:128]
        DT = s5p[:, 128:192]
        RHO, TH = s5p[:, 192:256], s5p[:, 256:320]
        KR, KI = s5p[:, 320:384], s5p[:, 384:448]
        H0R, H0I = s5p[:, 448:512], s5p[:, 512:576]
        TMPA, TMPB, TMPC, TMPD = s5p[:, 576:640], s5p[:, 640:704], s5p[:, 704:768], s5p[:, 768:832]
        HTR, HTI = s5p[:, 832:896], s5p[:, 896:960]
        STRf = Tf[:, zoff + 5120: zoff + 5120 + 256]
        STIf = Tf[:, zoff + 5376: zoff + 5376 + 256]
        STR_ = STRf.rearrange("p (d g q) -> p d g q", d=2, q=4)
        STI_ = STIf.rearrange("p (d g q) -> p d g q", d=2, q=4)
        assert zoff + 5632 <= 16384

        def cons_u(col, ps, pk):
            c = (col - OFFU) // 128
            V('act', lambda e: e.activation(out=U[:, c, :], in_=ps[:, :], func=AF.Identity), r=[pk], w=['u'])
        dense(w_in[l], 16, blocks(OFFU, 1024), xn_rhs, xn_keys, cons_u)
        dbg('u0', U[:, 0, :], ['u'])
        s.barrier()
        V('dve', lambda e: e.memset(CTfl, 0.0), w=['ct'])

        s.dma('sp', STG[:, 0:128], s5_lam[l], writes=['stg'])
        V('pe', lambda e: e.transpose(out=pc[:, 0:128], in_=STG[:, 0:128], identity=IDF), r=['stg', 'idf'], w=['pc'])
        V('dve', lambda e: e.tensor_copy(out=s5p[:, 0:128], in_=pc[:, 0:128]), r=['pc'], w=['s5p'])
        s.dma('sp', STG[0:64, 128:130], s5_ldt[l], writes=['stg2'])
        for gpar in range(2):
            V('dve', lambda e: e.tensor_scalar(out=STG[0:64, gpar * 64:(gpar + 1) * 64], in0=ONES[0:64, 0:64], scalar1=STG[0:64, 128 + gpar:129 + gpar],
                                               scalar2=None, op0=ALU.mult), r=['stg2', 'ones', 'pc'], w=['stg'])
        V('pe', lambda e: e.transpose(out=pc[:, 0:64], in_=STG[0:64, 0:128], identity=IDF[0:64, 0:64]), r=['stg', 'idf'], w=['pc'])
        V('act', lambda e: e.activation(out=DT, in_=pc[:, 0:64], func=AF.Exp), r=['pc'], w=['s5p'])
        V('dve', lambda e: e.tensor_tensor(out=TH, in0=LIM, in1=DT, op=ALU.mult), r=['s5p'], w=['s5p'])
        V('dve', lambda e: e.tensor_tensor(out=TMPA, in0=LRE, in1=DT, op=ALU.mult), r=['s5p'], w=['s5p'])
        V('act', lambda e: e.activation(out=RHO, in_=TMPA, func=AF.Exp), r=['s5p'], w=['s5p'])
        KI64 = s5p[:, 960:1024].bitcast(mybir.dt.int32)
        V('dve', lambda e: e.tensor_scalar(out=KI64, in0=TH, scalar1=1.0 / TWO_PI, scalar2=None, op0=ALU.mult), r=['s5p'], w=['s5p'])
        V('dve', lambda e: e.scalar_tensor_tensor(out=TMPD, in0=KI64, scalar=-TWO_PI, in1=TH, op0=ALU.mult, op1=ALU.add), r=['s5p'], w=['s5p'])
        V('act', lambda e: e.activation(out=TMPB, in_=TMPD, func=AF.Sin, scale=0.99999), r=['s5p'], w=['s5p'])
        V('dve', lambda e: e.scalar_tensor_tensor(out=TMPD, in0=TMPD, scalar=-1.0, in1=TMPD, op0=ALU.mult, op1=ALU.max), r=['s5p'], w=['s5p'])
        V('act', lambda e: e.activation(out=TMPC, in_=TMPD, func=AF.Sin, scale=-0.99999, bias=HPI), r=['s5p', 'hpi'], w=['s5p'])
        V('dve', lambda e: e.tensor_tensor(out=TMPB, in0=TMPB, in1=RHO, op=ALU.mult), r=['s5p'], w=['s5p'])
        V('dve', lambda e: e.tensor_tensor(out=TMPC, in0=TMPC, in1=RHO, op=ALU.mult), r=['s5p'], w=['s5p'])
        V('dve', lambda e: e.tensor_scalar(out=TMPC, in0=TMPC, scalar1=-1.0, scalar2=None, op0=ALU.add), r=['s5p'], w=['s5p'])
        V('dve', lambda e: e.tensor_tensor(out=TMPA, in0=LRE, in1=LRE, op=ALU.mult), r=['s5p'], w=['s5p'])
        V('dve', lambda e: e.tensor_tensor(out=TMPD, in0=LIM, in1=LIM, op=ALU.mult), r=['s5p'], w=['s5p'])
        V('dve', lambda e: e.tensor_tensor(out=TMPA, in0=TMPA, in1=TMPD, op=ALU.add), r=['s5p'], w=['s5p'])
        V('dve', lambda e: e.reciprocal(out=TMPA, in_=TMPA), r=['s5p'], w=['s5p'])
        V('dve', lambda e: e.tensor_tensor(out=KR, in0=TMPC, in1=LRE, op=ALU.mult), r=['s5p'], w=['s5p'])
        V('dve', lambda e: e.tensor_tensor(out=TMPD, in0=TMPB, in1=LIM, op=ALU.mult), r=['s5p'], w=['s5p'])
        V('dve', lambda e: e.tensor_tensor(out=KR, in0=KR, in1=TMPD, op=ALU.add), r=['s5p'], w=['s5p'])
        V('dve', lambda e: e.tensor_tensor(out=KR, in0=KR, in1=TMPA, op=ALU.mult), r=['s5p'], w=['s5p'])
        V('dve', lambda e: e.tensor_tensor(out=KI, in0=TMPB, in1=LRE, op=ALU.mult), r=['s5p'], w=['s5p'])
        V('dve', lambda e: e.tensor_tensor(out=TMPD, in0=TMPC, in1=LIM, op=ALU.mult), r=['s5p'], w=['s5p'])
        V('dve', lambda e: e.tensor_tensor(out=KI, in0=KI, in1=TMPD, op=ALU.subtract), r=['s5p'], w=['s5p'])
        V('dve', lambda e: e.tensor_tensor(out=KI, in0=KI, in1=TMPA, op=ALU.mult), r=['s5p'], w=['s5p'])
        s.dma('sp', STG[0:64, 0:256], st_s5[l], writes=['stg'])
        for ri, dst in ((0, H0R), (1, H0I)):
            V('pe', lambda e: e.transpose(out=pc[:, 0:64], in_=STG[0:64, ri:256:2], identity=IDF[0:64, 0:64]), r=['stg', 'idf'], w=['pc'])
            V('dve', lambda e: e.tensor_copy(out=dst, in_=pc[:, 0:64]), r=['pc'], w=['s5p'])
        V('dve', lambda e: e.tensor_tensor(out=TMPA, in0=KR, in1=KR, op=ALU.mult), r=['s5p'], w=['s5p'])
        V('dve', lambda e: e.tensor_tensor(out=TMPD, in0=KI, in1=KI, op=ALU.mult), r=['s5p'], w=['s5p'])
        V('dve', lambda e: e.tensor_tensor(out=TMPA, in0=TMPA, in1=TMPD, op=ALU.add), r=['s5p'], w=['s5p'])
        V('dve', lambda e: e.reciprocal(out=TMPA, in_=TMPA), r=['s5p'], w=['s5p'])
        V('dve', lambda e: e.tensor_tensor(out=HTR, in0=H0R, in1=KR, op=ALU.mult), r=['s5p'], w=['s5p'])
        V('dve', lambda e: e.tensor_tensor(out=TMPD, in0=H0I, in1=KI, op=ALU.mult), r=['s5p'], w=['s5p'])
        V('dve', lambda e: e.tensor_tensor(out=HTR, in0=HTR, in1=TMPD, op=ALU.add), r=['s5p'], w=['s5p'])
        V('dve', lambda e: e.tensor_tensor(out=HTR, in0=HTR, in1=TMPA, op=ALU.mult), r=['s5p'], w=['s5p'])
        V('dve', lambda e: e.tensor_tensor(out=HTI, in0=H0I, in1=KR, op=ALU.mult), r=['s5p'], w=['s5p'])
        V('dve', lambda e: e.tensor_tensor(out=TMPD, in0=H0R, in1=KI, op=ALU.mult), r=['s5p'], w=['s5p'])
        V('dve', lambda e: e.tensor_tensor(out=HTI, in0=HTI, in1=TMPD, op=ALU.subtract), r=['s5p'], w=['s5p'])
        V('dve', lambda e: e.tensor_tensor(out=HTI, in0=HTI, in1=TMPA, op=ALU.mult), r=['s5p'], w=['s5p'])

        THP = TMPA
        V('dve', lambda e: e.tensor_scalar(out=THP, in0=TH, scalar1=1.0 / TWO_PI, scalar2=None, op0=ALU.mult), r=['s5p'], w=['s5p'])
        for ri in range(2):
            for d in range(2):
                V('dve', lambda e: e.memset(BZf, 0.0), w=['bz', 'bz0', 'bz1'])
                bsrc = s5_b[l, ri, d].rearrange("(gp gq) p m -> gq p gp m", gq=2)
                for gq in range(2):
                    s.dma('sp', BZ[gq * 64:(gq + 1) * 64, :, gq * 16:(gq + 1) * 16], bsrc[gq], writes=['bz%d' % gq])
                for ch in range(8):
                    V('pe', lambda e: e.transpose(out=pc[:, 0:128], in_=BZf[:, ch * 128:(ch + 1) * 128], identity=IDF), r=['bz', 'bz0', 'bz1', 'idf'], w=['pc'])
                    V('act', lambda e: e.activation(out=BT[:, ri * 2 + d, ch, :], in_=pc[:, 0:128], func=AF.Identity), r=['pc'], w=['bt'])
        for d in range(2):
            for ri in range(2):
                V('dve', lambda e: e.memset(CZf, 0.0), w=['cz'] + ['cz%d' % i for i in range(8)])
                for gl_ in range(4):
                    for gq in range(2):
                        csrc = s5_c[l, ri, d].rearrange("(c g) n p -> g n c p", g=8)[2 * gl_ + gq]
                        r0 = gl_ * 32 + gq * 16
                        s.dma('sp', CZ[r0:r0 + 16, :, gq * 64:(gq + 1) * 64], csrc, writes=['cz%d' % (gl_ * 2 + gq)])
                for ch in range(8):
                    V('pe', lambda e: e.transpose(out=pc[:, 0:128], in_=CZ[:, ch, :], identity=IDF), r=['cz'] + ['cz%d' % i for i in range(8)] + ['idf'], w=['pc'])
                    V('act', lambda e: e.activation(out=CTf[:, ri, ch, :], in_=pc[:, 0:128], func=AF.Identity), r=['pc'], w=['ctf'])
            for ch in range(8):
                for gl_ in range(4):
                    gp = ch * 4 + gl_
                    kr = KR[:, d * 32 + gp:d * 32 + gp + 1]
                    ki = KI[:, d * 32 + gp:d * 32 + gp + 1]
                    cre = CTf[:, 0, ch, gl_ * 32:(gl_ + 1) * 32]
                    cim = CTf[:, 1, ch, gl_ * 32:(gl_ + 1) * 32]
                    t32 = TMPD[:, 0:32]
                    V('dve', lambda e: e.tensor_scalar(out=t32, in0=cim, scalar1=ki, scalar2=None, op0=ALU.mult), r=['ctf', 's5p'], w=['t32'])
                    V('dve', lambda e: e.scalar_tensor_tensor(out=CT[:, 0 + d, ch, CTOFF[gl_]:CTOFF[gl_] + 32], in0=cre, scalar=kr, in1=t32, op0=ALU.mult, op1=ALU.subtract),
                      r=['ctf', 's5p', 't32'], w=['ct'])
                    V('dve', lambda e: e.tensor_scalar(out=t32, in0=cim, scalar1=kr, scalar2=-1.0, op0=ALU.mult, op1=ALU.mult), r=['ctf', 's5p'], w=['t32'])
                    V('dve', lambda e: e.tensor_scalar(out=cre, in0=cre, scalar1=ki, scalar2=None, op0=ALU.mult), r=['ctf', 's5p'], w=['ctf'])
                    V('dve', lambda e: e.tensor_tensor(out=CT[:, 2 + d, ch, CTOFF[gl_]:CTOFF[gl_] + 32], in0=t32, in1=cre, op=ALU.subtract), r=['ctf', 't32'], w=['ct'])

        dbg('s5p', s5p, ['s5p'])
        dbg('bt', BT[:, 0, 0, :], ['bt'])
        dbg('bt2', BT[:, 2, 0, :], ['bt'])
        dbg('ct', CT[:, 0, 0, :], ['ct'])
        dbg('ct2', CT[:, 2, 0, :], ['ct'])
        W1f = f32v(oW[1], 4096)
        W0h = f32v(oW[0] + 2048, 2048)
        GI2 = f32v(oYB + 3072, 1024)
        SETS = [dict(COS=COS, SIN=SIN, T1=T1, T2=T2, T3=T3, GR=GR, GI=GI),
                dict(COS=W1f[:, 0:1024], SIN=W1f[:, 1024:2048], T1=W1f[:, 2048:3072], T2=W1f[:, 3072:4096], T3=W0h[:, 0:1024], GR=W0h[:, 1024:2048], GI=GI2)]
        ENDRf = Tf[:, 15360:15616]
        ENDIf = Tf[:, 15616:15872]
        ENDR = ENDRf.rearrange("p (d g q) -> p d g q", d=2, q=4)
        ENDI = ENDIf.rearrange("p (d g q) -> p d g q", d=2, q=4)

        def geom(gl_):
            rows = slice(gl_ * 32, (gl_ + 1) * 32) if gl_ < 3 else slice(64, 128)
            orows = slice(gl_ * 32, (gl_ + 1) * 32) if gl_ < 2 else slice(64, 128)
            ccols = [slice(0, 32), slice(32, 64), slice(64, 128), slice(128, 192)][gl_]
            return rows, orows, ccols

        def bufs(it):
            q = it % 2
            S_ = SETS[q]
            return (S_['COS'], S_['SIN'], S_['T1'], S_['T2'], S_['T3'], S_['GR'], S_['GI'],
                    'cos%d' % q, 'sin%d' % q, 't1_%d' % q, 't2_%d' % q, 't3_%d' % q, 'gr%d' % q, 'gi%d' % q)

        def stageA(it, ch, gl_, d):
            cC, cS, c1, c2, c3, cG, cI, kC, kS, k1, k2, k3, kG, kI = bufs(it)
            rows, orows, ccols = geom(gl_)
            gp = ch * 4 + gl_
            col = d * 32 + gp
            if gl_ == 0 and d == 0:
                V('dve', lambda e: e.tensor_copy(out=U3[64:128, :], in_=U[64:128, ch, :]), r=['u'], w=['u3'])
                V('dve', lambda e: e.memset(U3[64:96, :], 0.0), r=['u3'], w=['u3'])
            for ri in range(2):
                for half in range(2):
                    V('pe', lambda e: e.matmul(pa[ri][:, half * 512:(half + 1) * 512], lhsT=BT[rows, ri * 2 + d, ch, :],
                                               rhs=(U[rows, ch, half * 512:(half + 1) * 512] if gl_ < 3 else U3[rows, half * 512:(half + 1) * 512]), start=True, stop=True), r=['bt', 'u', 'u3'], w=['pa%d' % ri])
            thp = THP[:, col:col + 1]
            KINT = cG.bitcast(mybir.dt.int32)
            V('pool', lambda e: e.tensor_scalar(out=A_, in0=IOTA, scalar1=0.0, scalar2=RHO[:, col:col + 1], op0=ALU.mult, op1=ALU.add), r=['iota', 's5p'], w=['a5'])
            V('pool', lambda e: e.tensor_scalar(out=A_[:, 256:NT:256], in0=A_[:, 256:NT:256], scalar1=FLAG, scalar2=None, op0=ALU.mult), r=['a5', 'flags'], w=['a5'])
            V('dve', lambda e: e.tensor_scalar(out=KINT, in0=IOTA, scalar1=thp, scalar2=None, op0=ALU.mult), r=['iota', 's5p'], w=[kG])
            V('dve', lambda e: e.scalar_tensor_tensor(out=c2, in0=IOTA, scalar=thp, in1=KINT, op0=ALU.mult, op1=ALU.subtract), r=['iota', 's5p', kG], w=[k2])
            V('act', lambda e: e.activation(out=cS, in_=c2, func=AF.Sin, scale=TWO_PI * 0.99999), r=[k2], w=[kS])
            V('dve', lambda e: e.scalar_tensor_tensor(out=c3, in0=c2, scalar=-1.0, in1=c2, op0=ALU.mult, op1=ALU.max), r=[k2], w=[k3])
            V('act', lambda e: e.activation(out=cC, in_=c3, func=AF.Sin, scale=-TWO_PI * 0.99999, bias=HPI), r=[k3, 'hpi'], w=[kC])
            if d == 0:
                br, bi = pa[0][:, :], pa[1][:, :]
            else:
                br, bi = pa[0][:, ::-1], pa[1][:, ::-1]
            V('dve', lambda e: e.tensor_tensor(out=c1, in0=cC, in1=br, op=ALU.mult), r=[kC, 'pa0'], w=[k1])
            V('dve', lambda e: e.tensor_tensor(out=c2, in0=cS, in1=bi, op=ALU.mult), r=[kS, 'pa1'], w=[k2])
            V('dve', lambda e: e.tensor_tensor(out=c1, in0=c1, in1=c2, op=ALU.add), r=[k1, k2], w=[k1])
            V('dve', lambda e: e.tensor_tensor(out=c2, in0=cC, in1=bi, op=ALU.mult), r=[kC, 'pa1'], w=[k2])
            V('dve', lambda e: e.tensor_tensor(out=c3, in0=cS, in1=br, op=ALU.mult), r=[kS, 'pa0'], w=[k3])
            V('dve', lambda e: e.tensor_tensor(out=c2, in0=c2, in1=c3, op=ALU.subtract), r=[k2, k3], w=[k2])
            V('dve', lambda e: e.tensor_tensor_scan(out=cG, data0=A_, data1=c1, initial=HTR[:, col:col + 1], op0=ALU.mult, op1=ALU.add), r=['a5', k1, 's5p'], w=[kG])
            V('dve', lambda e: e.tensor_tensor_scan(out=cI, data0=A_, data1=c2, initial=HTI[:, col:col + 1], op0=ALU.mult, op1=ALU.add), r=['a5', k2, 's5p'], w=[kI])

        def stageB(it, ch, gl_, d):
            cC, cS, c1, c2, c3, cG, cI, kC, kS, k1, k2, k3, kG, kI = bufs(it)
            rows, orows, ccols = geom(gl_)
            gp = ch * 4 + gl_
            V('pool', lambda e: e.tensor_tensor(out=c1, in0=cC, in1=cG, op=ALU.mult), r=[kC, kG, k1], w=[k1])
            V('pool', lambda e: e.tensor_tensor(out=c3, in0=cS, in1=cI, op=ALU.mult), r=[kS, kI, k3], w=[k3])
            V('pool', lambda e: e.tensor_tensor(out=c1, in0=c1, in1=c3, op=ALU.subtract), r=[k1, k3], w=[k1])
            V('pool', lambda e: e.tensor_tensor(out=c2, in0=cS, in1=cG, op=ALU.mult), r=[kS, kG, k2], w=[k2])
            V('pool', lambda e: e.tensor_tensor(out=c3, in0=cC, in1=cI, op=ALU.mult), r=[kC, kI, k3], w=[k3])
            V('pool', lambda e: e.tensor_tensor(out=c2, in0=c2, in1=c3, op=ALU.add), r=[k2, k3], w=[k2])
            so = slice(None) if d == 0 else slice(None, None, -1)
            if d == 0:
                V('act', lambda e: e.activation(out=HRb, in_=c1, func=AF.Identity), r=[k1], w=['hrb'])
                V('act', lambda e: e.activation(out=HIb, in_=c2, func=AF.Identity), r=[k2], w=['hib'])
            else:
                V('act', lambda e: e.activation(out=HRb[:, ::-1], in_=c1, func=AF.Identity), r=[k1], w=['hrb'])
                V('act', lambda e: e.activation(out=HIb[:, ::-1], in_=c2, func=AF.Identity), r=[k2], w=['hib'])
            V('act', lambda e: e.activation(out=ENDR[:, d, gp, so], in_=c1[:, 255:NT:256], func=AF.Identity), r=[k1], w=['endr'])
            V('act', lambda e: e.activation(out=ENDI[:, d, gp, so], in_=c2[:, 255:NT:256], func=AF.Identity), r=[k2], w=['endi'])
            for half in range(2):
                V('pe', lambda e: e.matmul(pb[orows, half * 512:(half + 1) * 512], lhsT=CT[:, 0 + d, ch, ccols], rhs=HRb[:, half * 512:(half + 1) * 512],
                                           start=(d == 0 and gl_ < 3), stop=False), r=['ct', 'hrb'], w=['pb'])
                V('pe', lambda e: e.matmul(pb[orows, half * 512:(half + 1) * 512], lhsT=CT[:, 2 + d, ch, ccols], rhs=HIb[:, half * 512:(half + 1) * 512],
                                           start=False, stop=(d == 1)), r=['ct', 'hib'], w=['pb'])
            if gl_ == 3 and d == 1:
                dcol = 248 + ch
                V('dve', lambda e: e.scalar_tensor_tensor(out=XST, in0=U[:, ch, :], scalar=PAR[:, dcol:dcol + 1], in1=pb[:, :], op0=ALU.mult, op1=ALU.add), r=['u', 'par', 'pb'], w=['xst'])
                V('act', lambda e: e.activation(out=Z[:, ch, :], in_=XST, func=AF.Gelu_apprx_tanh), r=['xst'], w=['z%d' % ch])

        iters = [(ch, gl_, d) for ch in range(8) for gl_ in range(4) for d in range(2)]
        for n, (ch, gl_, d) in enumerate(iters):
            stageA(n, ch, gl_, d)
            if n >= 1:
                stageB(n - 1, *iters[n - 1])
        stageB(len(iters) - 1, *iters[-1])
        t64 = TMPD
        for qq in range(4):
            er = ENDRf.rearrange("p (c q) -> p c q", q=4)[:, :, qq]
            ei = ENDIf.rearrange("p (c q) -> p c q", q=4)[:, :, qq]
            sr = STRf.rearrange("p (c q) -> p c q", q=4)[:, :, qq]
            si = STIf.rearrange("p (c q) -> p c q", q=4)[:, :, qq]
            V('dve', lambda e: e.tensor_tensor(out=t64, in0=ei, in1=KI, op=ALU.mult), r=['endi', 's5p'], w=['t64'])
            V('dve', lambda e: e.tensor_tensor(out=sr, in0=er, in1=KR, op=ALU.mult), r=['endr', 's5p'], w=['st5'])
            V('dve', lambda e: e.tensor_tensor(out=sr, in0=sr, in1=t64, op=ALU.subtract), r=['st5', 't64'], w=['st5'])
            V('dve', lambda e: e.tensor_tensor(out=t64, in0=ei, in1=KR, op=ALU.mult), r=['endi', 's5p'], w=['t64'])
            V('dve', lambda e: e.tensor_tensor(out=si, in0=er, in1=KI, op=ALU.mult), r=['endr', 's5p'], w=['st5'])
            V('dve', lambda e: e.tensor_tensor(out=si, in0=si, in1=t64, op=ALU.add), r=['st5', 't64'], w=['st5'])
        s.barrier()
        def cons_glu(col, ps, pk):
            c = col // 128
            V('act', lambda e: e.activation(out=T3, in_=ps[:, :], func=AF.Sigmoid), r=[pk], w=['t3_0'])
            V('dve', lambda e: e.tensor_tensor(out=YB[:, c, :], in0=T3, in1=Z[:, c, :], op=ALU.mult), r=['t3_0', 'z%d' % c], w=['yb%d' % c])
        dense(s5_w_glu[l], 8, blocks(0, 1024), lambda kc, half: Z[:, kc, half * 512:(half + 1) * 512], lambda kc: ['z%d' % kc], cons_glu)
        dbg('strf', STRf, ['st5'])
        for d in range(2):
            OUTT = Tf[:, base: base + 256].rearrange("p (x r) -> p x r", r=2)
            for ri, src in ((0, STRf), (1, STIf)):
                V('pe', lambda e: e.transpose(out=pc[:, 0:128], in_=src[:, d * 128:(d + 1) * 128], identity=IDF), r=['st5', 'idf'], w=['pc'])
                V('dve', lambda e: e.tensor_copy(out=OUTT[:, :, ri], in_=pc[:, 0:128]), r=['pc'], w=['outt'])
            dbg('outt', OUTT.rearrange("p x r -> p (x r)"), ['outt'])
            s.dma('sp', ns5_out[l, d].rearrange("g q x -> (g q) x"), OUTT.rearrange("p x r -> p (x r)"), reads=['outt'], is_output=True)
        s.barrier()

    def na_branch(l):
        OFFQ, OFFK, OFFV = 3072, 4096, 5120
        QT = Tb[:, 0:8192].rearrange("p (h t) -> p h t", t=NT)
        KT = Tb[:, 8192:16384].rearrange("p (h t) -> p h t", t=NT)
        VT = Tb[:, 16384:24576].rearrange("p (a c) -> p a c", c=1024)
        base = 12288
        BIAS = Tf[:, base: base + 640]
        SP = Tf[:, base + 640: base + 1280]
        Pb = Tb[:, 2 * (base + 1280): 2 * (base + 1280) + 1152]
        PT = Tb[:, 2 * (base + 1856): 2 * (base + 1856) + 1152]
        KCT = Tb[:, 2 * (base + 2432): 2 * (base + 2432) + 512]
        CKh = Tb[:, 2 * (base + 2688): 2 * (base + 2688) + 512].rearrange("p (a c) -> p a c", c=128)
        CVh = Tb[:, 2 * (base + 2944): 2 * (base + 2944) + 512].rearrange("p (a c) -> p a c", c=128)
        STAT = Tf[:, base + 3200: base + 3216]
        OST = Tf[:, base + 3216: base + 3216 + 512]
        assert base + 3728 <= 16384

        def cons_q(col, ps, pk):
            h = (col - OFFQ) // 128
            V('act', lambda e: e.activation(out=QT[:, h, :], in_=ps[:, :], func=AF.Identity, scale=128.0 ** -0.5), r=[pk], w=['qt'])

        def cons_k(col, ps, pk):
            h = (col - OFFK) // 128
            V('act', lambda e: e.activation(out=KT[:, h, :], in_=ps[:, :], func=AF.Identity), r=[pk], w=['kt'])
        dense(w_in[l], 16, blocks(OFFQ, 1024), xn_rhs, xn_keys, cons_q)
        dense(w_in[l], 16, blocks(OFFK, 1024), xn_rhs, xn_keys, cons_k)
        ckpt(7.1)
        wv = w_in[l].rearrange("(k p) n -> p k n", p=128)
        for which, off in ((0, OFFK), (1, OFFV)):
            for cb_ in range(2):
                bi = wcount[0] % 2
                wcount[0] += 1
                wt = Wb[bi].rearrange("p (k n) -> p k n", n=512)
                s.dma('pool', wt, wv[:, :, off + cb_ * 512: off + (cb_ + 1) * 512], writes=['W%d' % bi], fence=False)
                for a in range(8):
                    for kc in range(16):
                        V('pe', lambda e: e.matmul(pc[:, :], lhsT=XN[:, kc, a * 128:(a + 1) * 128], rhs=wt[:, kc, :], start=(kc == 0), stop=(kc == 15)),
                          r=['W%d' % bi, 'xn%d' % kc], w=['pc'])
                    V('act', lambda e: e.activation(out=OST, in_=pc[:, :], func=AF.Identity), r=['pc'], w=['ost'])
                    if which == 1:
                        V('dve', lambda e: e.tensor_copy(out=VT[:, a, cb_ * 512:(cb_ + 1) * 512], in_=OST), r=['ost'], w=['vt'])
                    dst = (nk_out if which == 0 else nv_out)[l, a * 128:(a + 1) * 128, cb_ * 512:(cb_ + 1) * 512]
                    s.dma('sp', dst, OST, reads=['ost'], is_output=True)
                    ckpt(7.15)
                    if which == 0 and cb_ == 1 and a == 7:
                        ckpt(7.16)
                    if which == 0 and cb_ == 0 and a == 3:
                        ckpt(7.155)
        ckpt(7.2)
        V('dve', lambda e: e.memset(BIAS, 0.0), w=['biasA0', 'biasB0'])
        ckv = ck[l].rearrange("(a p) c -> p a c", p=128)
        cvv = cv[l].rearrange("(a p) c -> p a c", p=128)
        W1f = f32v(oW[1], 4096)
        W1b = Wb[1]
        BIAS2 = W1f[:, 0:640]
        SP2 = W1f[:, 640:1280]
        Pb2 = W1b[:, 2 * 1280: 2 * 1280 + 1152]
        STAT2 = W1f[:, 1856:1872]
        CVh2 = W1b[:, 2 * 1872: 2 * 1872 + 512].rearrange("p (a c) -> p a c", c=128)
        V('dve', lambda e: e.memset(BIAS2, 0.0), w=['biasA1', 'biasB1'])
        ABUF = [(BIAS, SP, Pb, STAT), (BIAS2, SP2, Pb2, STAT2)]
        CVS = [CVh, CVh2]

        def head_prep(h):
            s.dma('pool', CKh, ckv[:, :, h * 128:(h + 1) * 128], writes=['ckh'])
            s.dma('pool', CVS[h % 2], cvv[:, :, h * 128:(h + 1) * 128], writes=['cvh%d' % (h % 2)])
            for a in range(4):
                V('pe', lambda e: e.transpose(out=pd[:, a * 128:(a + 1) * 128], in_=CKh[:, a, :], identity=IDB), r=['ckh', 'idb'], w=['pd'])
            V('act', lambda e: e.activation(out=KCT, in_=pd[:, 0:512], func=AF.Identity), r=['pd'], w=['kct'])

        def stageA(n, h, i):
            q = n % 2
            bB, bS, bP, bT = ABUF[q]
            kA, kBb, kS, kP, kT = 'biasA%d' % q, 'biasB%d' % q, 'sp%d' % q, 'p%d' % q, 'stat%d' % q
            lo = min(max(i - 2, 0), 3)
            qs = slice(i * 128, (i + 1) * 128)
            kr0, qr0 = 2 * lo, 2 * i
            for ql in range(2):
                r_first = kr0 - qr0 - ql + 7
                rl = max(0, -r_first)
                rh = min(10, 15 - r_first)
                src = tp[l, h, r_first + rl: r_first + rh].rearrange("r q k -> q r k")
                s.dma('sp', bB[ql * 64:(ql + 1) * 64, rl * 64: rh * 64].rearrange("p (r k) -> p r k", k=64), src, writes=[kA if ql == 0 else kBb])
            V('pe', lambda e: e.matmul(pb[:, 0:512], lhsT=QT[:, h, qs], rhs=KT[:, h, lo * 128: lo * 128 + 512], start=True, stop=True), r=['qt', 'kt'], w=['pb'])
            V('pe', lambda e: e.matmul(pb[:, 512:640], lhsT=QT[:, h, qs], rhs=KT[:, h, lo * 128 + 512: lo * 128 + 640], start=True, stop=True), r=['qt', 'kt'], w=['pb'])
            V('pe', lambda e: e.matmul(pc[:, :], lhsT=QT[:, h, qs], rhs=KCT, start=True, stop=True), r=['qt', 'kct'], w=['pc'])
            V('dve', lambda e: e.tensor_tensor(out=bS, in0=pb[:, 0:640], in1=bB, op=ALU.add), r=['pb', kA, kBb], w=[kS])
            V('dve', lambda e: e.tensor_tensor(out=bS, in0=bS, in1=MASK[:, i, :], op=ALU.add), r=[kS, 'mask'], w=[kS])
            V('dve', lambda e: e.tensor_reduce(out=bT[:, 0:1], in_=bS, axis=AX.X, op=ALU.max), r=[kS], w=[kT])
            V('dve', lambda e: e.tensor_reduce(out=bT[:, 1:2], in_=pc[:, :], axis=AX.X, op=ALU.max), r=['pc'], w=[kT])
            V('dve', lambda e: e.scalar_tensor_tensor(out=bT[:, 2:3], in0=bT[:, 1:2], scalar=CB, in1=bT[:, 0:1], op0=ALU.add, op1=ALU.max), r=[kT, 'flags'], w=[kT])
            V('dve', lambda e: e.tensor_scalar(out=bT[:, 3:4], in0=bT[:, 2:3], scalar1=-1.0, scalar2=None, op0=ALU.mult), r=[kT], w=[kT])
            V('dve', lambda e: e.tensor_tensor(out=bT[:, 4:5], in0=bT[:, 3:4], in1=CB, op=ALU.add), r=[kT, 'flags'], w=[kT])
            V('act', lambda e: e.activation(out=bP[:, 0:640], in_=bS, func=AF.Exp, bias=bT[:, 3:4], accum_out=bT[:, 5:6]), r=[kS, kT], w=[kP, kT])
            V('act', lambda e: e.activation(out=bP[:, 640:1152], in_=pc[:, :], func=AF.Exp, bias=bT[:, 4:5], accum_out=bT[:, 6:7]), r=['pc', kT], w=[kP, kT])
            V('dve', lambda e: e.tensor_tensor(out=bT[:, 7:8], in0=bT[:, 5:6], in1=bT[:, 6:7], op=ALU.add), r=[kT], w=[kT])
            V('dve', lambda e: e.reciprocal(out=bT[:, 7:8], in_=bT[:, 7:8]), r=[kT], w=[kT])
            V('dve', lambda e: e.tensor_scalar(out=bP, in0=bP, scalar1=bT[:, 7:8], scalar2=None, op0=ALU.mult), r=[kP, kT], w=[kP])

        def stageB(n, h, i):
            q = n % 2
            bB, bS, bP, bT = ABUF[q]
            kP = 'p%d' % q
            lo = min(max(i - 2, 0), 3)
            qs = slice(i * 128, (i + 1) * 128)
            for j0, nj in ((0, 5), (5, 4)):
                for j in range(nj):
                    V('pe', lambda e: e.transpose(out=pd[:, j * 128:(j + 1) * 128], in_=bP[:, (j0 + j) * 128:(j0 + j + 1) * 128], identity=IDB), r=[kP, 'idb'], w=['pd'])
                V('act', lambda e: e.activation(out=PT[:, j0 * 128:(j0 + nj) * 128], in_=pd[:, 0:nj * 128], func=AF.Identity), r=['pd'], w=['pt'])
            for j in range(9):
                if j < 5:
                    lhs = VT[:, lo + j, h * 128:(h + 1) * 128]
                    rk = ['vt']
                else:
                    lhs = CVS[h % 2][:, j - 5, :]
                    rk = ['cvh%d' % (h % 2)]
                V('pe', lambda e: e.matmul(pa[0][:, 0:128], lhsT=lhs, rhs=PT[:, j * 128:(j + 1) * 128], start=(j == 0), stop=(j == 8)), r=rk + ['pt'], w=['pa0'])
            V('act', lambda e: e.activation(out=YB[:, h, qs], in_=pa[0][:, 0:128], func=AF.Identity), r=['pa0'], w=['yb%d' % h])

        tiles = [(h, i) for h in range(8) for i in range(8)]
        for n, (h, i) in enumerate(tiles):
            if i == 0:
                head_prep(h)
            stageA(n, h, i)
            if n >= 1:
                stageB(n - 1, *tiles[n - 1])
        stageB(len(tiles) - 1, *tiles[-1])
        s.barrier()

    def ffn(l):
        def cons_a(col, ps, pk):
            c = col // 128
            V('act', lambda e: e.activation(out=HID[:, c, :], in_=ps[:, :], func=AF.Silu), r=[pk], w=['hid%d' % c])

        def cons_b(col, ps, pk):
            c = (col - FF) // 128
            V('dve', lambda e: e.tensor_tensor(out=HID[:, c, :], in0=ps[:, :], in1=HID[:, c, :], op=ALU.mult), r=[pk, 'hid%d' % c], w=['hid%d' % c])
        for blk in range(11):
            dense(w_ffn_in[l], 16, [(blk * 512, 512)], xn_rhs, xn_keys, cons_a)
            dense(w_ffn_in[l], 16, [(FF + blk * 512, 512)], xn_rhs, xn_keys, cons_b)
        s.barrier()
        wv = w_ffn_out[l].rearrange("(k p) n -> p k n", p=128)
        for c in range(16):
            bi = wcount[0] % 2
            wcount[0] += 1
            wt = Wb[bi][:, 0:44 * 128].rearrange("p (k n) -> p k n", n=128)
            s.dma('pool', wt, wv[:, :, c * 128:(c + 1) * 128], writes=['W%d' % bi], fence=False)
            pi = dense.pcount % 2
            dense.pcount += 1
            ps = pa[pi]
            for half in range(2):
                for kc in range(44):
                    V('pe', lambda e: e.matmul(ps[:, half * 512:(half + 1) * 512], lhsT=wt[:, kc, :],
                                               rhs=HID[:, kc, half * 512:(half + 1) * 512], start=(kc == 0), stop=(kc == 43)),
                      r=['W%d' % bi, 'hid%d' % kc], w=['pa%d' % pi])
            V('act', lambda e: e.activation(out=XN[:, c, :], in_=ps[:, :], func=AF.Identity), r=['pa%d' % pi], w=['xn%d' % c])
        s.barrier()

    try:
      ckpt(0)
      for l in range(NL):
        load_params(l)
        ckpt(1)
        modulation(l)
        s.barrier()
        ckpt(2)
        norm_to_xn(352, 256)
        ckpt(3)
        s5_branch(l)
        ckpt(4)
        merge_branch(l, 0, 6144 + 1 * 2048)
        ckpt(5)
        rg_branch(l)
        ckpt(6)
        merge_branch(l, 1, 6144 + 0 * 2048)
        ckpt(7)
        na_branch(l)
        ckpt(8)
        merge_branch(l, 2, 6144 + 2 * 2048)
        ckpt(9)
        def cons_o(col, ps, pk):
            c = col // 128
            V('act', lambda e: e.activation(out=XN[:, c, :], in_=ps[:, :], func=AF.Identity), r=[pk], w=['xn%d' % c])
        dense(w_o[l], 16, blocks(0, D), lambda kc, half: Mb[:, kc, half * 512:(half + 1) * 512], lambda kc: ['m%d' % kc], cons_o)
        s.barrier()
        ckpt(10)
        post_norm_residual(384)
        ckpt(11)
        norm_to_xn(368, 256 + 48)
        ffn(l)
        ckpt(12)
        post_norm_residual(400)
    except _Stop:
        pass

    for a in range(8):
        for hh in range(2):
            stage = Tf[:, 0:1024].rearrange("p (c t) -> p c t", t=128)
            s.dma('sp', stage, xs[hh * 8:(hh + 1) * 8, :, a * 128:(a + 1) * 128].rearrange("c p t -> p c t"), reads=['xs'], writes=['t_stage'])
            for cc in range(8):
                V('pe', lambda e: e.transpose(out=pb[:, cc * 128:(cc + 1) * 128], in_=stage[:, cc, :], identity=IDF), r=['t_stage', 'idf'], w=['pb'])
            outst = Tf[:, 1024:2048]
            V('act', lambda e: e.activation(out=outst, in_=pb[:, :], func=AF.Identity), r=['pb'], w=['t_out'])
            s.dma('sp', y_out[a * 128:(a + 1) * 128, hh * 1024:(hh + 1) * 1024], outst, reads=['t_out'], is_output=True)
    s.finish()
    return nc, s


def _na_mask_sample():
    m = np.full((8, 128, 640), NEG, np.float32)
    for i in range(8):
        lo = min(max(i - 2, 0), 3)
        for ql in range(2):
            qr = 2 * i + ql
            rs = min(max(qr - 4, 0), 8)
            for qc in range(64):
                ws = min(max(qc - 8, 0), 48)
                for krel in range(10):
                    kr = 2 * lo + krel
                    if rs <= kr < rs + 8:
                        m[i, ql * 64 + qc, krel * 64 + ws: krel * 64 + ws + 16] = 0.0
    return m


def _na_mask_prompt():
    m = np.full((8, 128, 640), NEG, np.float32)
    for i in range(8):
        lo = min(max(i - 2, 0), 3)
        seq = i // 2
        for j in range(5):
            if (lo + j) // 2 == seq:
                m[i, :, j * 128:(j + 1) * 128] = 0.0
    return m


def make_in_maps(inp, cores=range(8), L=DEPTH):
    f = lambda a: np.ascontiguousarray(np.asarray(a, dtype=np.float32))
    I = {k: f(v) for k, v in inp.items()}
    shared = {}
    shared["w_mod"] = I["w_mod"]
    shared["b_mod"] = I["b_mod"].reshape(L, 96, 128)
    shared["gvec"] = f(np.concatenate([I["g_mix_pre"].reshape(L, 16, 128), I["g_mix_post"].reshape(L, 16, 128),
                                       I["g_ffn_pre"].reshape(L, 16, 128), I["g_ffn_post"].reshape(L, 16, 128)], axis=1))
    shared["w_in"] = I["w_in"]
    shared["rgv"] = f(np.concatenate([I["rg_conv_w"].reshape(L, 32, 128), I["rg_conv_b"].reshape(L, 8, 128),
                                      I["rg_gate_b"].reshape(L, 32, 128), I["rg_lambda"].reshape(L, 16, 128),
                                      I["s5_d"].reshape(L, 8, 128)], axis=1))
    shared["rg_gate_w"] = I["rg_gate_w"].reshape(L, 32, 128, 128)
    shared["s5_lam"] = f(np.concatenate([I["s5_lambda_re"].reshape(L, 64, 128), I["s5_lambda_im"].reshape(L, 64, 128)], axis=1))
    shared["s5_ldt"] = I["s5_log_dt"].reshape(L, 64, 2)
    shared["s5_b"] = f(np.stack([I["s5_b_re"], I["s5_b_im"]], axis=1))
    shared["s5_c"] = f(np.stack([I["s5_c_re"], I["s5_c_im"]], axis=1))
    shared["s5_w_glu"] = I["s5_w_glu"]
    shared["w_bo"] = f(np.stack([I["w_s5_out"], I["w_rg_out"], I["w_na_out"]], axis=1))
    shared["w_o"] = I["w_o"]
    shared["w_ffn_in"] = I["w_ffn_in"]
    shared["w_ffn_out"] = I["w_ffn_out"]
    shared["ident"] = np.eye(128, dtype=np.float32)
    shared["iota1"] = f(np.broadcast_to(np.arange(1, NT + 1, dtype=np.float32), (128, NT)))
    rpb = I["na_rpb"]
    qc = np.arange(64)[:, None]
    kc = np.arange(64)[None, :]
    cidx = np.clip(kc - qc, -15, 15) + 15
    tp_s = f(rpb[:, :, :, cidx])
    tp_p = np.zeros_like(tp_s)
    mask_s = _na_mask_sample()
    mask_p = _na_mask_prompt()
    maps = []
    for core in cores:
        m = dict(shared)
        if core < 4:
            m["xin"] = f(I["x_prompt"][4 * core:4 * core + 4].reshape(NT, D))
            m["cond"] = I["c_ctx"].reshape(16, 128)
            m["tp"] = tp_p
            m["ck"] = np.zeros((L, 512, 1024), np.float32)
            m["cv"] = np.zeros((L, 512, 1024), np.float32)
            m["st_rg"] = np.zeros((L, 16, 128), np.float32)
            m["st_s5"] = np.zeros((L, 64, 256), np.float32)
            m["amask"] = mask_p
            fl = np.zeros((128, 4), np.float32)
            fl[:, 1] = NEG
            fl[:, 2] = -1.0
        else:
            b = core - 4
            m["xin"] = f(I["x_sample"][b])
            m["cond"] = I["c"][b].reshape(16, 128)
            m["tp"] = tp_s
            m["ck"] = f(I["cache_na_k"][b].reshape(L, 512, 1024))
            m["cv"] = f(I["cache_na_v"][b].reshape(L, 512, 1024))
            m["st_rg"] = f(I["state_rglru"][b].reshape(L, 16, 128))
            m["st_s5"] = f(I["state_s5"][b].reshape(L, 64, 256))
            m["amask"] = mask_s
            fl = np.zeros((128, 4), np.float32)
            fl[:, 0] = 1.0
        m["flags"] = fl
        maps.append(m)
    return maps


_CACHE = {}


def kernel(**inputs):
    if "nc" not in _CACHE:
        _CACHE["nc"] = build(DEPTH)[0]
    nc = _CACHE["nc"]
    maps = make_in_maps(inputs)
    res = run_bass_kernel_spmd(nc, maps, core_ids=list(range(8)))
    R = res.results
    L = DEPTH
    y_prompt = np.concatenate([R[c]["y"].reshape(4, 256, D) for c in range(4)], axis=0)
    y_sample = np.stack([R[c]["y"] for c in range(4, 8)], axis=0)
    nk = np.concatenate([R[c]["nk"].reshape(L, 4, 256, 8, 128).transpose(1, 0, 2, 3, 4) for c in range(4)], axis=0)
    nv = np.concatenate([R[c]["nv"].reshape(L, 4, 256, 8, 128).transpose(1, 0, 2, 3, 4) for c in range(4)], axis=0)
    nrg = np.concatenate([R[c]["nrg"].reshape(L, 2, 4, 1024).transpose(2, 0, 1, 3) for c in range(4)], axis=0)
    ns5 = np.concatenate([R[c]["ns5"].reshape(L, 2, 32, 4, 2, 64, 2).transpose(3, 0, 1, 2, 4, 5, 6).reshape(4, L, 2, 64, 64, 2)
                          for c in range(4)], axis=0)
    f = lambda a: np.ascontiguousarray(a, dtype=np.float32)
    return (f(y_prompt), f(y_sample), f(nk), f(nv), f(nrg), f(ns5))
```

```python
import math
import numpy as np
import concourse.bass as bass
import concourse.mybir as mybir
from concourse.bass_utils import run_bass_kernel_spmd

F32 = mybir.dt.float32
BF16 = mybir.dt.bfloat16
AF = mybir.ActivationFunctionType
ALU = mybir.AluOpType
AX = mybir.AxisListType

D = 2048
DEPTH = 4
NT = 1024
FF = 5632
NEG = -1e30
TWO_PI = 2.0 * math.pi


class Sched:
    ENG = ['pe', 'act', 'dve', 'pool', 'sp']

    def __init__(self, nc, n_dma_sems=32):
        self.nc = nc
        self.eng = dict(pe=nc.tensor, act=nc.scalar, dve=nc.vector, pool=nc.gpsimd, sp=nc.sync)
        self.sem = {e: nc.alloc_semaphore("cnt_" + e) for e in self.ENG}
        self.cnt = {e: 0 for e in self.ENG}
        self.seen = {e: {} for e in self.ENG}
        self.last_w = {}
        self.readers = {}
        self.dma_sems = [nc.alloc_semaphore("dma%d" % i) for i in range(n_dma_sems)]
        self.dma_tot = [0] * n_dma_sems
        self.dma_rr = 0
        self.out_tokens = []
        self.fence_tokens = []
        self.n_ins = 0

    def _wait(self, e, tok):
        sem, val = tok
        key = id(sem)
        if self.seen[e].get(key, 0) >= val:
            return
        self.seen[e][key] = val
        self.eng[e].wait_ge(sem, val)
        self.n_ins += 1

    def _deps(self, e, reads, writes):
        toks = []
        for k in reads:
            t = self.last_w.get(k)
            if t is not None:
                toks.append(t)
        for k in writes:
            t = self.last_w.get(k)
            if t is not None:
                toks.append(t)
            toks.extend(self.readers.get(k, ()))
        for t in toks:
            if e == 'pe' and t[0] is self.sem['pe']:
                continue
            self._wait(e, t)

    def _commit(self, tok, reads, writes):
        for k in reads:
            self.readers.setdefault(k, []).append(tok)
        for k in writes:
            self.last_w[k] = tok
            self.readers[k] = []

    def op(self, e, fn, reads=(), writes=()):
        self._deps(e, reads, writes)
        ins = fn(self.eng[e])
        self.cnt[e] += 1
        ins.then_inc(self.sem[e], 1)
        tok = (self.sem[e], self.cnt[e])
        self._commit(tok, reads, writes)
        self.n_ins += 1
        return tok

    def dma(self, q, out, in_, reads=(), writes=(), is_output=False, fence=True, **kw):
        k = self.dma_rr
        self.dma_rr = (self.dma_rr + 1) % len(self.dma_sems)
        sem = self.dma_sems[k]
        if self.dma_tot[k] > 0:
            self._wait(q, (sem, self.dma_tot[k]))
        self._deps(q, reads, writes)
        ins = self.eng[q].dma_start(out=out, in_=in_, **kw)
        self.dma_tot[k] += 16
        ins.then_inc(sem, 16)
        tok = (sem, self.dma_tot[k])
        self._commit(tok, reads, writes)
        if is_output:
            self.out_tokens.append(tok)
        if fence:
            self.fence_tokens.append(tok)
        self.n_ins += 1
        return tok

    def barrier(self, skip=()):
        for e in self.ENG:
            if e in skip:
                continue
            for f in self.ENG:
                if f != e and self.cnt[f] > 0:
                    self._wait(e, (self.sem[f], self.cnt[f]))
            for t in self.fence_tokens:
                self._wait(e, t)
        self.fence_tokens = []

    def finish(self, e='sp'):
        for t in self.out_tokens:
            self._wait(e, t)
        for f in self.ENG:
            if self.cnt[f] > 0 and f != e:
                self._wait(e, (self.sem[f], self.cnt[f]))


class _Stop(Exception):
    pass


def build(NL=DEPTH, STOP=None, LW=DEPTH, DBG=False):
    nc = bass.Bass("TRN2", target_bir_lowering=False)

    def ckpt(k):
        if STOP is not None and k == STOP:
            raise _Stop()

    def din(name, shape):
        return nc.dram_tensor(name, list(shape), F32, kind="ExternalInput").ap()

    def dout(name, shape):
        return nc.dram_tensor(name, list(shape), F32, kind="ExternalOutput").ap()

    xin = din("xin", [NT, D])
    cond = din("cond", [16, 128])
    w_mod = din("w_mod", [LW, D, 6 * D])
    b_mod = din("b_mod", [LW, 96, 128])
    gvec = din("gvec", [LW, 64, 128])
    w_in = din("w_in", [LW, D, 12288])
    rgv = din("rgv", [LW, 96, 128])
    rg_gate_w = din("rg_gate_w", [LW, 32, 128, 128])
    s5_lam = din("s5_lam", [LW, 128, 128])
    s5_ldt = din("s5_ldt", [LW, 64, 2])
    s5_b = din("s5_b", [LW, 2, 2, 64, 64, 16])
    s5_c = din("s5_c", [LW, 2, 2, 64, 16, 64])
    s5_w_glu = din("s5_w_glu", [LW, 1024, 1024])
    tp = din("tp", [LW, 8, 15, 64, 64])
    w_bo = din("w_bo", [LW, 3, 1024, D])
    w_o = din("w_o", [LW, D, D])
    w_ffn_in = din("w_ffn_in", [LW, D, 2 * FF])
    w_ffn_out = din("w_ffn_out", [LW, FF, D])
    ck = din("ck", [LW, 512, 1024])
    cv = din("cv", [LW, 512, 1024])
    st_rg = din("st_rg", [LW, 16, 128])
    st_s5 = din("st_s5", [LW, 64, 256])
    amask = din("amask", [8, 128, 640])
    flags = din("flags", [128, 4])
    ident_d = din("ident", [128, 128])
    iota_d = din("iota1", [128, NT])

    y_out = dout("y", [NT, D])
    nk_out = dout("nk", [LW, NT, 1024])
    nv_out = dout("nv", [LW, NT, 1024])
    nrg_out = dout("nrg", [LW, 2, 4, 8, 128])
    ns5_out = dout("ns5", [LW, 2, 32, 4, 256])
    xs = nc.dram_tensor("xs", [16, 128, NT], F32).ap()

    s = Sched(nc)
    dbg_seen = set()

    def dbg(name, ap, keys):
        if not DBG or name in dbg_seen:
            return
        dbg_seen.add(name)
        dt_ = nc.dram_tensor("dbg_" + name, list(ap.shape), ap.dtype, kind="ExternalOutput").ap()
        s.dma('sp', dt_, ap, reads=keys, is_output=True)

    TOT = 52400
    big = nc.alloc_sbuf_tensor("big", [128, TOT], F32)
    cur = [0]

    def carve(nwords):
        a = cur[0]
        cur[0] += nwords
        assert cur[0] <= TOT, cur[0]
        return a

    def f32v(off, n):
        return big[:, off:off + n]

    def bf16v(off, nwords):
        return big[:, off:off + nwords].bitcast(BF16)

    oW = [carve(4096), carve(4096)]
    oXN = carve(8192)
    oT = carve(16384)
    oM = carve(8192)
    oYB = carve(4096)
    oMASK = carve(2560)
    oPAR = carve(512)
    oRSTD = carve(1024)
    oONES = carve(128)
    oIOTA = carve(1024)
    oIDF = carve(128)
    oIDB = carve(64)
    oSMALL = carve(512)
    oSTG = carve(256)
    oXST = carve(1024)
    oCOND = carve(16)
    oONESB = carve(64)

    Wb = [bf16v(o, 4096) for o in oW]
    XN = bf16v(oXN, 8192).rearrange("p (c t) -> p c t", t=NT)
    Tf = f32v(oT, 16384)
    Tb = bf16v(oT, 16384)
    Mb = bf16v(oM, 8192).rearrange("p (c t) -> p c t", t=NT)
    HID = bf16v(oT, 16384 + 8192).rearrange("p (c t) -> p c t", t=NT)[:, 0:44, :]
    YB = bf16v(oYB, 4096).rearrange("p (c t) -> p c t", t=NT)
    MASK = bf16v(oMASK, 2560).rearrange("p (i k) -> p i k", k=640)
    PAR = f32v(oPAR, 512)
    RSTD = f32v(oRSTD, 1024)
    ONES = f32v(oONES, 128)
    IOTA = f32v(oIOTA, 1024)
    IDF = f32v(oIDF, 128)
    IDB = bf16v(oIDB, 64)
    SM = f32v(oSMALL, 512)
    GW = Wb[1][:, 0:4096].rearrange("p (a j) -> p a j", j=128)
    STG = f32v(oSTG, 256)
    XST = f32v(oXST, 1024)
    CONDT = f32v(oCOND, 16)

    pa = [nc.alloc_psum_tensor("pa0", [128, 1024], F32), nc.alloc_psum_tensor("pa1", [128, 1024], F32)]
    pb = nc.alloc_psum_tensor("pb", [128, 1024], F32)
    pc = nc.alloc_psum_tensor("pc", [128, 512], F32)
    pd = nc.alloc_psum_tensor("pd", [128, 1024], BF16)

    def V(e, fn, r=(), w=()):
        return s.op(e, fn, reads=r, writes=w)

    def transpose_f32(dst, src, rows, key_src, key_dst, eng='dve'):
        V('pe', lambda e: e.transpose(out=pc[:, 0:rows], in_=src, identity=IDF[0:rows, 0:rows]), r=[key_src, 'idf'], w=['pc'])
        V(eng, lambda e: e.tensor_copy(out=dst, in_=pc[:, 0:rows]), r=['pc'], w=[key_dst])

    wcount = [0]

    def dense(wsrc, nK, cols, rhs_fn, rhs_keys, consume, m_tiles=None):
        wv = wsrc.rearrange("(k p) n -> p k n", p=128)
        kmax = 8192 // 512
        for (c0, ncol) in cols:
            kper = min(nK, 8192 // ncol)
            assert nK % kper == 0
            nkb = nK // kper
            bufs = []
            for kb in range(nkb):
                bi = wcount[0] % 2
                wcount[0] += 1
                wt = Wb[bi][:, 0:kper * ncol].rearrange("p (k n) -> p k n", n=ncol)
                s.dma('pool', wt, wv[:, kb * kper:(kb + 1) * kper, c0:c0 + ncol], writes=['W%d' % bi], fence=False)
                bufs.append((bi, wt))
                if nkb > 1 and kb < nkb - 1:
                    pass
            assert nkb <= 2
            for mt in range(ncol // 128):
                pi = dense.pcount % 2
                dense.pcount += 1
                ps = pa[pi]
                for half in range(2):
                    for kc in range(nK):
                        bi, wt = bufs[kc // kper]
                        V('pe', lambda e: e.matmul(ps[:, half * 512:(half + 1) * 512], lhsT=wt[:, kc % kper, mt * 128:(mt + 1) * 128],
                                                   rhs=rhs_fn(kc, half), start=(kc == 0), stop=(kc == nK - 1)),
                          r=['W%d' % bi] + rhs_keys(kc), w=['pa%d' % pi])
                consume(c0 + mt * 128, ps, 'pa%d' % pi)
    dense.pcount = 0

    def blocks(c0, n, bs=512):
        return [(c0 + i, min(bs, n - i)) for i in range(0, n, bs)]

    s.dma('sp', IDF, ident_d, writes=['idf'])
    s.dma('sp', IOTA, iota_d, writes=['iota'])
    s.dma('sp', SM[:, 0:4], flags, writes=['flags'])
    s.dma('pool', MASK, amask.rearrange("i p k -> p i k"), writes=['mask'])
    V('dve', lambda e: e.memset(ONES, 1.0), w=['ones'])
    V('dve', lambda e: e.tensor_copy(out=IDB, in_=IDF), r=['idf'], w=['idb'])
    FLAG = SM[:, 0:1]
    CB = SM[:, 1:2]
    FLM1 = SM[:, 2:3]
    HPI = SM[:, 4:5]
    V('dve', lambda e: e.memset(HPI, math.pi / 2), w=['hpi'])
    s.dma('sp', STG[0:16, 0:128], cond, writes=['stg'])
    transpose_f32(CONDT, STG[0:16, 0:128], 16, 'stg', 'condt')
    SC = bf16v(oSMALL + 8, 8)
    V('act', lambda e: e.activation(out=SC, in_=CONDT, func=AF.Silu), r=['condt'], w=['sc'])

    xin_v = xin.rearrange("(a p) d -> p a d", p=128)
    for a in range(8):
        for hh in range(2):
            stage = Tf[:, 0:1024]
            s.dma('sp', stage, xin_v[:, a, hh * 1024:(hh + 1) * 1024], writes=['t_stage'])
            for cc in range(8):
                c = hh * 8 + cc
                V('pe', lambda e: e.transpose(out=pb[:, cc * 128:(cc + 1) * 128], in_=stage[:, cc * 128:(cc + 1) * 128], identity=IDF),
                  r=['t_stage', 'idf'], w=['pb'])
            outst = Tf[:, 1024:2048]
            V('act', lambda e: e.activation(out=outst, in_=pb[:, :], func=AF.Identity), r=['pb'], w=['t_out'])
            s.dma('sp', xs[hh * 8:(hh + 1) * 8, :, a * 128:(a + 1) * 128].rearrange("c p t -> p c t"),
                  outst.rearrange("p (c t) -> p c t", t=128), reads=['t_out'], writes=['xs'])
    s.barrier()

    def load_params(l):
        s.dma('sp', STG[0:96, 0:128], b_mod[l], writes=['stg'])
        transpose_f32(PAR[:, 0:96], STG[0:96, 0:128], 96, 'stg', 'par')
        s.dma('sp', STG[0:64, 0:128], gvec[l], writes=['stg'])
        transpose_f32(PAR[:, 96:160], STG[0:64, 0:128], 64, 'stg', 'par')
        s.dma('sp', STG[0:96, 0:128], rgv[l], writes=['stg'])
        transpose_f32(PAR[:, 160:256], STG[0:96, 0:128], 96, 'stg', 'par')

    def modulation(l):
        MODP = PAR[:, 256:352]
        wv = w_mod[l].rearrange("(k p) n -> p k n", p=128)
        for blk in range(24):
            bi = wcount[0] % 2
            wcount[0] += 1
            wt = Wb[bi].rearrange("p (k n) -> p k n", n=512)
            s.dma('pool', wt, wv[:, :, blk * 512:(blk + 1) * 512], writes=['W%d' % bi], fence=False)
            for mt in range(4):
                j = blk * 4 + mt
                for kc in range(16):
                    V('pe', lambda e: e.matmul(pc[:, j:j + 1], lhsT=wt[:, kc, mt * 128:(mt + 1) * 128], rhs=SC[:, kc:kc + 1],
                                               start=(kc == 0), stop=(kc == 15)), r=['W%d' % bi, 'sc'], w=['pc'])
        V('dve', lambda e: e.tensor_tensor(out=MODP, in0=pc[:, 0:96], in1=PAR[:, 0:96], op=ALU.add), r=['pc', 'par'], w=['mod'])
        dbg('par', PAR, ['par', 'mod'])
        V('dve', lambda e: e.scalar_tensor_tensor(out=PAR[:, 352:368], in0=MODP[:, 16:32], scalar=1.0, in1=PAR[:, 96:112], op0=ALU.add, op1=ALU.mult),
          r=['mod', 'par'], w=['mod2'])
        V('dve', lambda e: e.scalar_tensor_tensor(out=PAR[:, 368:384], in0=MODP[:, 64:80], scalar=1.0, in1=PAR[:, 128:144], op0=ALU.add, op1=ALU.mult),
          r=['mod', 'par'], w=['mod2'])
        V('dve', lambda e: e.tensor_tensor(out=PAR[:, 384:400], in0=MODP[:, 32:48], in1=PAR[:, 112:128], op=ALU.mult), r=['mod', 'par'], w=['mod2'])
        V('dve', lambda e: e.tensor_tensor(out=PAR[:, 400:416], in0=MODP[:, 80:96], in1=PAR[:, 144:160], op=ALU.mult), r=['mod', 'par'], w=['mod2'])

    def norm_to_xn(gs_col, shift_col):
        Xr = Tf.rearrange("p (c t) -> p c t", t=NT)
        SQ = Mb
        for c in range(16):
            s.dma('sp', Xr[:, c, :], xs[c], reads=['xs'], writes=['x%d' % c])
            V('act', lambda e: e.activation(out=SQ[:, c, :], in_=Xr[:, c, :], func=AF.Square), r=['x%d' % c], w=['sq%d' % c])
        onesb = bf16v(oONESB, 64)
        V('dve', lambda e: e.tensor_copy(out=onesb, in_=ONES[:, 0:128]), r=['ones'], w=['onesb'])
        for half in range(2):
            for c in range(16):
                V('pe', lambda e: e.matmul(pb[:, half * 512:(half + 1) * 512], lhsT=onesb, rhs=SQ[:, c, half * 512:(half + 1) * 512],
                                           start=(c == 0), stop=(c == 15)), r=['onesb', 'sq%d' % c], w=['pb'])
        V('act', lambda e: e.activation(out=RSTD, in_=pb[:, :], func=AF.Sqrt, scale=1.0 / D, bias=EPS), r=['pb', 'eps'], w=['rstd'])
        V('dve', lambda e: e.reciprocal(out=RSTD, in_=RSTD), r=['rstd'], w=['rstd'])
        for c in range(16):
            V('dve', lambda e: e.tensor_tensor(out=Xr[:, c, :], in0=Xr[:, c, :], in1=RSTD, op=ALU.mult), r=['x%d' % c, 'rstd'], w=['x%d' % c])
            V('act', lambda e: e.activation(out=XN[:, c, :], in_=Xr[:, c, :], func=AF.Identity,
                                            scale=PAR[:, gs_col + c:gs_col + c + 1], bias=PAR[:, shift_col + c:shift_col + c + 1]),
              r=['x%d' % c, 'mod', 'mod2'], w=['xn%d' % c])
        dbg('rstd', RSTD, ['rstd'])
        dbg('xn0', XN[:, 0, :], ['xn0'])
        dbg('par2', PAR, ['par', 'mod', 'mod2'])
        s.barrier(skip=('pool',))

    EPS = SM[:, 5:6]
    V('dve', lambda e: e.memset(EPS, 1e-6), w=['eps'])

    def post_norm_residual(gg_col):
        SQ = Mb
        for c in range(16):
            V('act', lambda e: e.activation(out=SQ[:, c, :], in_=XN[:, c, :], func=AF.Square), r=['xn%d' % c], w=['sq%d' % c])
        onesb = bf16v(oONESB, 64)
        for half in range(2):
            for c in range(16):
                V('pe', lambda e: e.matmul(pb[:, half * 512:(half + 1) * 512], lhsT=onesb, rhs=SQ[:, c, half * 512:(half + 1) * 512],
                                           start=(c == 0), stop=(c == 15)), r=['onesb', 'sq%d' % c], w=['pb'])
        V('act', lambda e: e.activation(out=RSTD, in_=pb[:, :], func=AF.Sqrt, scale=1.0 / D, bias=EPS), r=['pb', 'eps'], w=['rstd'])
        V('dve', lambda e: e.reciprocal(out=RSTD, in_=RSTD), r=['rstd'], w=['rstd'])
        Xr = Tf.rearrange("p (c t) -> p c t", t=NT)
        for c in range(16):
            s.dma('sp', Xr[:, c, :], xs[c], reads=['xs'], writes=['x%d' % c])
            tmp = XST
            V('dve', lambda e: e.tensor_tensor(out=tmp, in0=XN[:, c, :], in1=RSTD, op=ALU.mult), r=['xn%d' % c, 'rstd'], w=['xst'])
            V('dve', lambda e: e.scalar_tensor_tensor(out=Xr[:, c, :], in0=tmp, scalar=PAR[:, gg_col + c:gg_col + c + 1], in1=Xr[:, c, :],
                                                      op0=ALU.mult, op1=ALU.add), r=['xst', 'x%d' % c, 'mod2'], w=['x%d' % c])
            s.dma('sp', xs[c], Xr[:, c, :], reads=['x%d' % c], writes=['xs'])
        s.barrier(skip=('pool',))

    def xn_rhs(kc, half):
        return XN[:, kc, half * 512:(half + 1) * 512]

    def xn_keys(kc):
        return ['xn%d' % kc]

    def merge_branch(l, b, act_col_base):
        SG = Tf[:, 0:1024]
        for blk in range(4):
            SGs = Tf[:, 0:4096].rearrange("p (m t) -> p m t", t=NT)

            def cons_gate(col, ps, pk):
                mt = ((col - act_col_base) // 128) % 4
                V('act', lambda e: e.activation(out=SGs[:, mt, :], in_=ps[:, :], func=AF.Sigmoid), r=[pk], w=['sg%d' % mt])
            dense(w_in[l], 16, [(act_col_base + blk * 512, 512)], xn_rhs, xn_keys, cons_gate)

            def cons_proj(col, ps, pk):
                c = col // 128
                mt = c % 4
                if b == 0:
                    V('dve', lambda e: e.tensor_tensor(out=Mb[:, c, :], in0=ps[:, :], in1=SGs[:, mt, :], op=ALU.mult), r=[pk, 'sg%d' % mt], w=['m%d' % c])
                else:
                    V('dve', lambda e: e.tensor_tensor(out=SGs[:, mt, :], in0=ps[:, :], in1=SGs[:, mt, :], op=ALU.mult), r=[pk, 'sg%d' % mt], w=['sg%d' % mt])
                    V('dve', lambda e: e.tensor_tensor(out=Mb[:, c, :], in0=Mb[:, c, :], in1=SGs[:, mt, :], op=ALU.add), r=['m%d' % c, 'sg%d' % mt], w=['m%d' % c])
            dense(w_bo[l, b], 8, [(blk * 512, 512)], lambda kc, half: YB[:, kc, half * 512:(half + 1) * 512], lambda kc: ['yb%d' % kc], cons_proj)
        s.barrier(skip=('pool',))

    def rg_branch(l):
        OFFX, OFFG = 0, 1024
        XR = Tb[:, 0:8192].rearrange("p (c t) -> p c t", t=NT)
        XG = Tb[:, 8192:16384].rearrange("p (c t) -> p c t", t=NT)
        base = 8192
        def tf(i):
            return Tf[:, base + i * 1024: base + (i + 1) * 1024]
        XC, R_, IG, A_, SQ_, HF, HB = tf(0), tf(1), tf(2), tf(3), tf(4), tf(5), tf(6)
        XCB = Tb[:, 2 * (base + 7 * 1024): 2 * (base + 7 * 1024) + 1024]
        V('act', lambda e: e.activation(out=SM[:, 16:32], in_=PAR[:, 232:248], func=AF.Exp, scale=-1.0), r=['par'], w=['cneg'])
        V('act', lambda e: e.activation(out=SM[:, 16:32], in_=SM[:, 16:32], func=AF.Ln, bias=1.0), r=['cneg'], w=['cneg'])
        V('dve', lambda e: e.tensor_scalar(out=SM[:, 32:48], in0=SM[:, 16:32], scalar1=-16.0, scalar2=None, op0=ALU.mult), r=['cneg'], w=['cneg2'])
        V('dve', lambda e: e.tensor_scalar(out=SM[:, 16:32], in0=SM[:, 16:32], scalar1=-8.0, scalar2=None, op0=ALU.mult), r=['cneg'], w=['cneg'])
        V('dve', lambda e: e.tensor_scalar(out=SM[:, 48:80], in0=PAR[:, 160:192], scalar1=FLM1, scalar2=None, op0=ALU.mult), r=['par', 'flags'], w=['wneg'])
        s.dma('sp', STG[0:16, 0:128], st_rg[l], writes=['stg'])
        transpose_f32(SM[:, 80:96], STG[0:16, 0:128], 16, 'stg', 'h0rg')
        STC = SM[:, 96:160].rearrange("p (d q c) -> p d q c", d=2, q=4)

        def cons_x(col, ps, pk):
            c = (col % 1024) // 128
            dst = XR if col < 1024 else XG
            V('act', lambda e: e.activation(out=dst[:, c, :], in_=ps[:, :], func=AF.Identity), r=[pk], w=['xrg'])
        dense(w_in[l], 16, blocks(OFFX, 2048), xn_rhs, xn_keys, cons_x)
        s.dma('pool', GW, rg_gate_w[l].rearrange("a i j -> i a j"), writes=['W1'], fence=False)

        for c in range(8):
            wc = lambda k: PAR[:, 160 + k * 8 + c:160 + k * 8 + c + 1]
            wn = lambda k: SM[:, 48 + k * 8 + c:48 + k * 8 + c + 1]
            x = XR[:, c, :]
            V('act', lambda e: e.activation(out=XC, in_=x, func=AF.Identity, scale=wc(1), bias=PAR[:, 192 + c:193 + c]), r=['xrg', 'par'], w=['xc'])
            V('dve', lambda e: e.scalar_tensor_tensor(out=XC[:, 1:NT], in0=x[:, 0:NT - 1], scalar=wc(0), in1=XC[:, 1:NT], op0=ALU.mult, op1=ALU.add), r=['xrg', 'xc'], w=['xc'])
            V('dve', lambda e: e.scalar_tensor_tensor(out=XC[:, 0:NT - 1], in0=x[:, 1:NT], scalar=wc(2), in1=XC[:, 0:NT - 1], op0=ALU.mult, op1=ALU.add), r=['xrg', 'xc'], w=['xc'])
            V('dve', lambda e: e.scalar_tensor_tensor(out=XC[:, 0:NT - 2], in0=x[:, 2:NT], scalar=wc(3), in1=XC[:, 0:NT - 2], op0=ALU.mult, op1=ALU.add), r=['xrg', 'xc'], w=['xc'])
            def cols(off):
                return slice(off, off + 513, 256)
            V('dve', lambda e: e.scalar_tensor_tensor(out=XC[:, cols(256)], in0=x[:, cols(255)], scalar=wn(0), in1=XC[:, cols(256)], op0=ALU.mult, op1=ALU.add), r=['xrg', 'xc', 'wneg'], w=['xc'])
            V('dve', lambda e: e.scalar_tensor_tensor(out=XC[:, cols(255)], in0=x[:, cols(256)], scalar=wn(2), in1=XC[:, cols(255)], op0=ALU.mult, op1=ALU.add), r=['xrg', 'xc', 'wneg'], w=['xc'])
            V('dve', lambda e: e.scalar_tensor_tensor(out=XC[:, cols(255)], in0=x[:, cols(257)], scalar=wn(3), in1=XC[:, cols(255)], op0=ALU.mult, op1=ALU.add), r=['xrg', 'xc', 'wneg'], w=['xc'])
            V('dve', lambda e: e.scalar_tensor_tensor(out=XC[:, cols(254)], in0=x[:, cols(256)], scalar=wn(3), in1=XC[:, cols(254)], op0=ALU.mult, op1=ALU.add), r=['xrg', 'xc', 'wneg'], w=['xc'])
            V('act', lambda e: e.activation(out=XCB, in_=XC, func=AF.Identity), r=['xc'], w=['xcb'])
            for d in range(2):
                H = HF if d == 0 else HB
                for g, dst in ((0, R_), (1, IG)):
                    pi = dense.pcount % 2
                    dense.pcount += 1
                    ps = pa[pi]
                    for half in range(2):
                        V('pe', lambda e: e.matmul(ps[:, half * 512:(half + 1) * 512], lhsT=GW[:, (d * 2 + g) * 8 + c, :], rhs=XCB[:, half * 512:(half + 1) * 512],
                                                   start=True, stop=True), r=['W1', 'xcb'], w=['pa%d' % pi])
                    bcol = 200 + (d * 2 + g) * 8 + c
                    V('act', lambda e: e.activation(out=dst, in_=ps[:, :], func=AF.Sigmoid, bias=PAR[:, bcol:bcol + 1]), r=['pa%d' % pi, 'par'], w=['rg_t%d' % g])
                V('act', lambda e: e.activation(out=A_, in_=R_, func=AF.Exp, scale=SM[:, 16 + d * 8 + c:17 + d * 8 + c]), r=['rg_t0', 'cneg'], w=['rg_a'])
                V('act', lambda e: e.activation(out=SQ_, in_=R_, func=AF.Exp, scale=SM[:, 32 + d * 8 + c:33 + d * 8 + c]), r=['rg_t0', 'cneg2'], w=['rg_s'])
                V('act', lambda e: e.activation(out=SQ_, in_=SQ_, func=AF.Sqrt, scale=-1.0, bias=1.0), r=['rg_s'], w=['rg_s'])
                V('dve', lambda e: e.tensor_tensor(out=SQ_, in0=SQ_, in1=IG, op=ALU.mult), r=['rg_s', 'rg_t1'], w=['rg_s'])
                V('dve', lambda e: e.tensor_tensor(out=SQ_, in0=SQ_, in1=XC, op=ALU.mult), r=['rg_s', 'xc'], w=['rg_s'])
                bc = cols(256) if d == 0 else cols(255)
                V('dve', lambda e: e.tensor_scalar(out=A_[:, bc], in0=A_[:, bc], scalar1=FLAG, scalar2=None, op0=ALU.mult), r=['rg_a', 'flags'], w=['rg_a'])
                h0 = SM[:, 80 + d * 8 + c:81 + d * 8 + c]
                if d == 0:
                    V('dve', lambda e: e.tensor_tensor_scan(out=H, data0=A_, data1=SQ_, initial=h0, op0=ALU.mult, op1=ALU.add), r=['rg_a', 'rg_s', 'h0rg'], w=['rg_h%d' % d])
                    V('dve', lambda e: e.tensor_copy(out=STC[:, 0, :, c], in_=H[:, 255:NT:256]), r=['rg_h0'], w=['stc'])
                else:
                    V('dve', lambda e: e.tensor_tensor_scan(out=H[:, ::-1], data0=A_[:, ::-1], data1=SQ_[:, ::-1], initial=h0, op0=ALU.mult, op1=ALU.add),
                      r=['rg_a', 'rg_s', 'h0rg'], w=['rg_h%d' % d])
                    V('dve', lambda e: e.tensor_copy(out=STC[:, 1, :, c], in_=H[:, 0:NT:256]), r=['rg_h1'], w=['stc'])
            V('dve', lambda e: e.tensor_tensor(out=HF, in0=HF, in1=HB, op=ALU.add), r=['rg_h0', 'rg_h1'], w=['rg_h0'])
            V('act', lambda e: e.activation(out=R_, in_=XG[:, c, :], func=AF.Gelu_apprx_tanh), r=['xrg', 'rg_t0'], w=['rg_t0'])
            V('dve', lambda e: e.tensor_tensor(out=YB[:, c, :], in0=HF, in1=R_, op=ALU.mult), r=['rg_h0', 'rg_t0'], w=['yb%d' % c])
        for d in range(2):
            src = SM[:, 96 + d * 32:96 + (d + 1) * 32]
            V('pe', lambda e: e.transpose(out=pc[0:32, 0:128], in_=src, identity=IDF), r=['stc', 'idf'], w=['pc'])
            V('dve', lambda e: e.tensor_copy(out=STG[0:32, 0:128], in_=pc[0:32, 0:128]), r=['pc'], w=['stg'])
            s.dma('sp', nrg_out[l, d].rearrange("q c h -> (q c) h"), STG[0:32, 0:128], reads=['stg'], is_output=True)
        s.barrier()

    def s5_branch(l):
        OFFU = 2048
        U = Tb[:, 0:8192].rearrange("p (c t) -> p c t", t=NT)
        Z = Mb[:, 0:8, :]
        base = 4096
        def tf(i):
            return Tf[:, base + i * 1024: base + (i + 1) * 1024]
        COS, SIN, A_, T1, T2, T3, GR, GI = [tf(i) for i in range(8)]
        HRb = Tb[:, 2 * (base + 8 * 1024): 2 * (base + 8 * 1024) + 1024]
        HIb = Tb[:, 2 * (base + 8 * 1024) + 1024: 2 * (base + 8 * 1024) + 2048]
        YBb = bf16v(oYB, 4096)
        BT = Wb[0][:, 0:4096].rearrange("p (a c m) -> p a c m", a=4, c=8)
        CTfl = YBb[:, 0:4 * 8 * 192]
        CT = CTfl.rearrange("p (a c m) -> p a c m", a=4, c=8)
        CTOFF = [0, 32, 64, 160]
        U3 = Tb[:, 2 * (base + 9 * 1024 + 1536): 2 * (base + 9 * 1024 + 1536) + 1024]
        assert base + 9 * 1024 + 1536 + 512 <= 16384
        M2 = f32v(oM + 4096, 4096)
        BZf = M2[:, 0:1024]
        BZ = BZf.rearrange("p (g m) -> p g m", m=32)
        CZf = M2[:, 1024:2048]
        CZ = CZf.rearrange("p (c m) -> p c m", m=128)
        CTf = M2[:, 2048:4096].rearrange("p (r c m) -> p r c m", r=2, c=8)
        zoff = base + 9 * 1024 - 4096
        s5p = Tf[:, zoff + 4096: zoff + 4096 + 1024]
        LRE, LIM = s5p[:, 0:64], s5p[:, 64:128]
        DT = s5p[:, 128:192]
        RHO, TH = s5p[:, 192:256], s5p[:, 256:320]
        KR, KI = s5p[:, 320:384], s5p[:, 384:448]
        H0R, H0I = s5p[:, 448:512], s5p[:, 512:576]
        TMPA, TMPB, TMPC, TMPD = s5p[:, 576:640], s5p[:, 640:704], s5p[:, 704:768], s5p[:, 768:832]
        HTR, HTI = s5p[:, 832:896], s5p[:, 896:960]
        STRf = Tf[:, zoff + 5120: zoff + 5120 + 256]
        STIf = Tf[:, zoff + 5376: zoff + 5376 + 256]
        STR_ = STRf.rearrange("p (d g q) -> p d g q", d=2, q=4)
        STI_ = STIf.rearrange("p (d g q) -> p d g q", d=2, q=4)
        assert zoff + 5632 <= 16384

        def cons_u(col, ps, pk):
            c = (col - OFFU) // 128
            V('act', lambda e: e.activation(out=U[:, c, :], in_=ps[:, :], func=AF.Identity), r=[pk], w=['u'])
        dense(w_in[l], 16, blocks(OFFU, 1024), xn_rhs, xn_keys, cons_u)
        dbg('u0', U[:, 0, :], ['u'])
        s.barrier()
        V('dve', lambda e: e.memset(CTfl, 0.0), w=['ct'])

        s.dma('sp', STG[:, 0:128], s5_lam[l], writes=['stg'])
        V('pe', lambda e: e.transpose(out=pc[:, 0:128], in_=STG[:, 0:128], identity=IDF), r=['stg', 'idf'], w=['pc'])
        V('dve', lambda e: e.tensor_copy(out=s5p[:, 0:128], in_=pc[:, 0:128]), r=['pc'], w=['s5p'])
        s.dma('sp', STG[0:64, 128:130], s5_ldt[l], writes=['stg2'])
        for gpar in range(2):
            V('dve', lambda e: e.tensor_scalar(out=STG[0:64, gpar * 64:(gpar + 1) * 64], in0=ONES[0:64, 0:64], scalar1=STG[0:64, 128 + gpar:129 + gpar],
                                               scalar2=None, op0=ALU.mult), r=['stg2', 'ones', 'pc'], w=['stg'])
        V('pe', lambda e: e.transpose(out=pc[:, 0:64], in_=STG[0:64, 0:128], identity=IDF[0:64, 0:64]), r=['stg', 'idf'], w=['pc'])
        V('act', lambda e: e.activation(out=DT, in_=pc[:, 0:64], func=AF.Exp), r=['pc'], w=['s5p'])
        V('dve', lambda e: e.tensor_tensor(out=TH, in0=LIM, in1=DT, op=ALU.mult), r=['s5p'], w=['s5p'])
        V('dve', lambda e: e.tensor_tensor(out=TMPA, in0=LRE, in1=DT, op=ALU.mult), r=['s5p'], w=['s5p'])
        V('act', lambda e: e.activation(out=RHO, in_=TMPA, func=AF.Exp), r=['s5p'], w=['s5p'])
        KI64 = s5p[:, 960:1024].bitcast(mybir.dt.int32)
        V('dve', lambda e: e.tensor_scalar(out=KI64, in0=TH, scalar1=1.0 / TWO_PI, scalar2=None, op0=ALU.mult), r=['s5p'], w=['s5p'])
        V('dve', lambda e: e.scalar_tensor_tensor(out=TMPD, in0=KI64, scalar=-TWO_PI, in1=TH, op0=ALU.mult, op1=ALU.add), r=['s5p'], w=['s5p'])
        V('act', lambda e: e.activation(out=TMPB, in_=TMPD, func=AF.Sin, scale=0.99999), r=['s5p'], w=['s5p'])
        V('dve', lambda e: e.scalar_tensor_tensor(out=TMPD, in0=TMPD, scalar=-1.0, in1=TMPD, op0=ALU.mult, op1=ALU.max), r=['s5p'], w=['s5p'])
        V('act', lambda e: e.activation(out=TMPC, in_=TMPD, func=AF.Sin, scale=-0.99999, bias=HPI), r=['s5p', 'hpi'], w=['s5p'])
        V('dve', lambda e: e.tensor_tensor(out=TMPB, in0=TMPB, in1=RHO, op=ALU.mult), r=['s5p'], w=['s5p'])
        V('dve', lambda e: e.tensor_tensor(out=TMPC, in0=TMPC, in1=RHO, op=ALU.mult), r=['s5p'], w=['s5p'])
        V('dve', lambda e: e.tensor_scalar(out=TMPC, in0=TMPC, scalar1=-1.0, scalar2=None, op0=ALU.add), r=['s5p'], w=['s5p'])
        V('dve', lambda e: e.tensor_tensor(out=TMPA, in0=LRE, in1=LRE, op=ALU.mult), r=['s5p'], w=['s5p'])
        V('dve', lambda e: e.tensor_tensor(out=TMPD, in0=LIM, in1=LIM, op=ALU.mult), r=['s5p'], w=['s5p'])
        V('dve', lambda e: e.tensor_tensor(out=TMPA, in0=TMPA, in1=TMPD, op=ALU.add), r=['s5p'], w=['s5p'])
        V('dve', lambda e: e.reciprocal(out=TMPA, in_=TMPA), r=['s5p'], w=['s5p'])
        V('dve', lambda e: e.tensor_tensor(out=KR, in0=TMPC, in1=LRE, op=ALU.mult), r=['s5p'], w=['s5p'])
        V('dve', lambda e: e.tensor_tensor(out=TMPD, in0=TMPB, in1=LIM, op=ALU.mult), r=['s5p'], w=['s5p'])
        V('dve', lambda e: e.tensor_tensor(out=KR, in0=KR, in1=TMPD, op=ALU.add), r=['s5p'], w=['s5p'])
        V('dve', lambda e: e.tensor_tensor(out=KR, in0=KR, in1=TMPA, op=ALU.mult), r=['s5p'], w=['s5p'])
        V('dve', lambda e: e.tensor_tensor(out=KI, in0=TMPB, in1=LRE, op=ALU.mult), r=['s5p'], w=['s5p'])
        V('dve', lambda e: e.tensor_tensor(out=TMPD, in0=TMPC, in1=LIM, op=ALU.mult), r=['s5p'], w=['s5p'])
        V('dve', lambda e: e.tensor_tensor(out=KI, in0=KI, in1=TMPD, op=ALU.subtract), r=['s5p'], w=['s5p'])
        V('dve', lambda e: e.tensor_tensor(out=KI, in0=KI, in1=TMPA, op=ALU.mult), r=['s5p'], w=['s5p'])
        s.dma('sp', STG[0:64, 0:256], st_s5[l], writes=['stg'])
        for ri, dst in ((0, H0R), (1, H0I)):
            V('pe', lambda e: e.transpose(out=pc[:, 0:64], in_=STG[0:64, ri:256:2], identity=IDF[0:64, 0:64]), r=['stg', 'idf'], w=['pc'])
            V('dve', lambda e: e.tensor_copy(out=dst, in_=pc[:, 0:64]), r=['pc'], w=['s5p'])
        V('dve', lambda e: e.tensor_tensor(out=TMPA, in0=KR, in1=KR, op=ALU.mult), r=['s5p'], w=['s5p'])
        V('dve', lambda e: e.tensor_tensor(out=TMPD, in0=KI, in1=KI, op=ALU.mult), r=['s5p'], w=['s5p'])
        V('dve', lambda e: e.tensor_tensor(out=TMPA, in0=TMPA, in1=TMPD, op=ALU.add), r=['s5p'], w=['s5p'])
        V('dve', lambda e: e.reciprocal(out=TMPA, in_=TMPA), r=['s5p'], w=['s5p'])
        V('dve', lambda e: e.tensor_tensor(out=HTR, in0=H0R, in1=KR, op=ALU.mult), r=['s5p'], w=['s5p'])
        V('dve', lambda e: e.tensor_tensor(out=TMPD, in0=H0I, in1=KI, op=ALU.mult), r=['s5p'], w=['s5p'])
        V('dve', lambda e: e.tensor_tensor(out=HTR, in0=HTR, in1=TMPD, op=ALU.add), r=['s5p'], w=['s5p'])
        V('dve', lambda e: e.tensor_tensor(out=HTR, in0=HTR, in1=TMPA, op=ALU.mult), r=['s5p'], w=['s5p'])
        V('dve', lambda e: e.tensor_tensor(out=HTI, in0=H0I, in1=KR, op=ALU.mult), r=['s5p'], w=['s5p'])
        V('dve', lambda e: e.tensor_tensor(out=TMPD, in0=H0R, in1=KI, op=ALU.mult), r=['s5p'], w=['s5p'])
        V('dve', lambda e: e.tensor_tensor(out=HTI, in0=HTI, in1=TMPD, op=ALU.subtract), r=['s5p'], w=['s5p'])
        V('dve', lambda e: e.tensor_tensor(out=HTI, in0=HTI, in1=TMPA, op=ALU.mult), r=['s5p'], w=['s5p'])

        THP = TMPA
        V('dve', lambda e: e.tensor_scalar(out=THP, in0=TH, scalar1=1.0 / TWO_PI, scalar2=None, op0=ALU.mult), r=['s5p'], w=['s5p'])
        for ri in range(2):
            for d in range(2):
                V('dve', lambda e: e.memset(BZf, 0.0), w=['bz', 'bz0', 'bz1'])
                bsrc = s5_b[l, ri, d].rearrange("(gp gq) p m -> gq p gp m", gq=2)
                for gq in range(2):
                    s.dma('sp', BZ[gq * 64:(gq + 1) * 64, :, gq * 16:(gq + 1) * 16], bsrc[gq], writes=['bz%d' % gq])
                for ch in range(8):
                    V('pe', lambda e: e.transpose(out=pc[:, 0:128], in_=BZf[:, ch * 128:(ch + 1) * 128], identity=IDF), r=['bz', 'bz0', 'bz1', 'idf'], w=['pc'])
                    V('act', lambda e: e.activation(out=BT[:, ri * 2 + d, ch, :], in_=pc[:, 0:128], func=AF.Identity), r=['pc'], w=['bt'])
        for d in range(2):
            for ri in range(2):
                V('dve', lambda e: e.memset(CZf, 0.0), w=['cz'] + ['cz%d' % i for i in range(8)])
                for gl_ in range(4):
                    for gq in range(2):
                        csrc = s5_c[l, ri, d].rearrange("(c g) n p -> g n c p", g=8)[2 * gl_ + gq]
                        r0 = gl_ * 32 + gq * 16
                        s.dma('sp', CZ[r0:r0 + 16, :, gq * 64:(gq + 1) * 64], csrc, writes=['cz%d' % (gl_ * 2 + gq)])
                for ch in range(8):
                    V('pe', lambda e: e.transpose(out=pc[:, 0:128], in_=CZ[:, ch, :], identity=IDF), r=['cz'] + ['cz%d' % i for i in range(8)] + ['idf'], w=['pc'])
                    V('act', lambda e: e.activation(out=CTf[:, ri, ch, :], in_=pc[:, 0:128], func=AF.Identity), r=['pc'], w=['ctf'])
            for ch in range(8):
                for gl_ in range(4):
                    gp = ch * 4 + gl_
                    kr = KR[:, d * 32 + gp:d * 32 + gp + 1]
                    ki = KI[:, d * 32 + gp:d * 32 + gp + 1]
                    cre = CTf[:, 0, ch, gl_ * 32:(gl_ + 1) * 32]
                    cim = CTf[:, 1, ch, gl_ * 32:(gl_ + 1) * 32]
                    t32 = TMPD[:, 0:32]
                    V('dve', lambda e: e.tensor_scalar(out=t32, in0=cim, scalar1=ki, scalar2=None, op0=ALU.mult), r=['ctf', 's5p'], w=['t32'])
                    V('dve', lambda e: e.scalar_tensor_tensor(out=CT[:, 0 + d, ch, CTOFF[gl_]:CTOFF[gl_] + 32], in0=cre, scalar=kr, in1=t32, op0=ALU.mult, op1=ALU.subtract),
                      r=['ctf', 's5p', 't32'], w=['ct'])
                    V('dve', lambda e: e.tensor_scalar(out=t32, in0=cim, scalar1=kr, scalar2=-1.0, op0=ALU.mult, op1=ALU.mult), r=['ctf', 's5p'], w=['t32'])
                    V('dve', lambda e: e.tensor_scalar(out=cre, in0=cre, scalar1=ki, scalar2=None, op0=ALU.mult), r=['ctf', 's5p'], w=['ctf'])
                    V('dve', lambda e: e.tensor_tensor(out=CT[:, 2 + d, ch, CTOFF[gl_]:CTOFF[gl_] + 32], in0=t32, in1=cre, op=ALU.subtract), r=['ctf', 't32'], w=['ct'])

        dbg('s5p', s5p, ['s5p'])
        dbg('bt', BT[:, 0, 0, :], ['bt'])
        dbg('bt2', BT[:, 2, 0, :], ['bt'])
        dbg('ct', CT[:, 0, 0, :], ['ct'])
        dbg('ct2', CT[:, 2, 0, :], ['ct'])
        W1f = f32v(oW[1], 4096)
        W0h = f32v(oW[0] + 2048, 2048)
        GI2 = f32v(oYB + 3072, 1024)
        SETS = [dict(COS=COS, SIN=SIN, T1=T1, T2=T2, T3=T3, GR=GR, GI=GI),
                dict(COS=W1f[:, 0:1024], SIN=W1f[:, 1024:2048], T1=W1f[:, 2048:3072], T2=W1f[:, 3072:4096], T3=W0h[:, 0:1024], GR=W0h[:, 1024:2048], GI=GI2)]
        ENDRf = Tf[:, 15360:15616]
        ENDIf = Tf[:, 15616:15872]
        ENDR = ENDRf.rearrange("p (d g q) -> p d g q", d=2, q=4)
        ENDI = ENDIf.rearrange("p (d g q) -> p d g q", d=2, q=4)

        def geom(gl_):
            rows = slice(gl_ * 32, (gl_ + 1) * 32) if gl_ < 3 else slice(64, 128)
            orows = slice(gl_ * 32, (gl_ + 1) * 32) if gl_ < 2 else slice(64, 128)
            ccols = [slice(0, 32), slice(32, 64), slice(64, 128), slice(128, 192)][gl_]
            return rows, orows, ccols

        def bufs(it):
            q = it % 2
            S_ = SETS[q]
            return (S_['COS'], S_['SIN'], S_['T1'], S_['T2'], S_['T3'], S_['GR'], S_['GI'],
                    'cos%d' % q, 'sin%d' % q, 't1_%d' % q, 't2_%d' % q, 't3_%d' % q, 'gr%d' % q, 'gi%d' % q)

        def stageA(it, ch, gl_, d):
            cC, cS, c1, c2, c3, cG, cI, kC, kS, k1, k2, k3, kG, kI = bufs(it)
            rows, orows, ccols = geom(gl_)
            gp = ch * 4 + gl_
            col = d * 32 + gp
            if gl_ == 0 and d == 0:
                V('dve', lambda e: e.tensor_copy(out=U3[64:128, :], in_=U[64:128, ch, :]), r=['u'], w=['u3'])
                V('dve', lambda e: e.memset(U3[64:96, :], 0.0), r=['u3'], w=['u3'])
            for ri in range(2):
                for half in range(2):
                    V('pe', lambda e: e.matmul(pa[ri][:, half * 512:(half + 1) * 512], lhsT=BT[rows, ri * 2 + d, ch, :],
                                               rhs=(U[rows, ch, half * 512:(half + 1) * 512] if gl_ < 3 else U3[rows, half * 512:(half + 1) * 512]), start=True, stop=True), r=['bt', 'u', 'u3'], w=['pa%d' % ri])
            thp = THP[:, col:col + 1]
            KINT = cG.bitcast(mybir.dt.int32)
            V('pool', lambda e: e.tensor_scalar(out=A_, in0=IOTA, scalar1=0.0, scalar2=RHO[:, col:col + 1], op0=ALU.mult, op1=ALU.add), r=['iota', 's5p'], w=['a5'])
            V('pool', lambda e: e.tensor_scalar(out=A_[:, 256:NT:256], in0=A_[:, 256:NT:256], scalar1=FLAG, scalar2=None, op0=ALU.mult), r=['a5', 'flags'], w=['a5'])
            V('dve', lambda e: e.tensor_scalar(out=KINT, in0=IOTA, scalar1=thp, scalar2=None, op0=ALU.mult), r=['iota', 's5p'], w=[kG])
            V('dve', lambda e: e.scalar_tensor_tensor(out=c2, in0=IOTA, scalar=thp, in1=KINT, op0=ALU.mult, op1=ALU.subtract), r=['iota', 's5p', kG], w=[k2])
            V('act', lambda e: e.activation(out=cS, in_=c2, func=AF.Sin, scale=TWO_PI * 0.99999), r=[k2], w=[kS])
            V('dve', lambda e: e.scalar_tensor_tensor(out=c3, in0=c2, scalar=-1.0, in1=c2, op0=ALU.mult, op1=ALU.max), r=[k2], w=[k3])
            V('act', lambda e: e.activation(out=cC, in_=c3, func=AF.Sin, scale=-TWO_PI * 0.99999, bias=HPI), r=[k3, 'hpi'], w=[kC])
            if d == 0:
                br, bi = pa[0][:, :], pa[1][:, :]
            else:
                br, bi = pa[0][:, ::-1], pa[1][:, ::-1]
            V('dve', lambda e: e.tensor_tensor(out=c1, in0=cC, in1=br, op=ALU.mult), r=[kC, 'pa0'], w=[k1])
            V('dve', lambda e: e.tensor_tensor(out=c2, in0=cS, in1=bi, op=ALU.mult), r=[kS, 'pa1'], w=[k2])
            V('dve', lambda e: e.tensor_tensor(out=c1, in0=c1, in1=c2, op=ALU.add), r=[k1, k2], w=[k1])
            V('dve', lambda e: e.tensor_tensor(out=c2, in0=cC, in1=bi, op=ALU.mult), r=[kC, 'pa1'], w=[k2])
            V('dve', lambda e: e.tensor_tensor(out=c3, in0=cS, in1=br, op=ALU.mult), r=[kS, 'pa0'], w=[k3])
            V('dve', lambda e: e.tensor_tensor(out=c2, in0=c2, in1=c3, op=ALU.subtract), r=[k2, k3], w=[k2])
            V('dve', lambda e: e.tensor_tensor_scan(out=cG, data0=A_, data1=c1, initial=HTR[:, col:col + 1], op0=ALU.mult, op1=ALU.add), r=['a5', k1, 's5p'], w=[kG])
            V('dve', lambda e: e.tensor_tensor_scan(out=cI, data0=A_, data1=c2, initial=HTI[:, col:col + 1], op0=ALU.mult, op1=ALU.add), r=['a5', k2, 's5p'], w=[kI])

        def stageB(it, ch, gl_, d):
            cC, cS, c1, c2, c3, cG, cI, kC, kS, k1, k2, k3, kG, kI = bufs(it)
            rows, orows, ccols = geom(gl_)
            gp = ch * 4 + gl_
            V('pool', lambda e: e.tensor_tensor(out=c1, in0=cC, in1=cG, op=ALU.mult), r=[kC, kG, k1], w=[k1])
            V('pool', lambda e: e.tensor_tensor(out=c3, in0=cS, in1=cI, op=ALU.mult), r=[kS, kI, k3], w=[k3])
            V('pool', lambda e: e.tensor_tensor(out=c1, in0=c1, in1=c3, op=ALU.subtract), r=[k1, k3], w=[k1])
            V('pool', lambda e: e.tensor_tensor(out=c2, in0=cS, in1=cG, op=ALU.mult), r=[kS, kG, k2], w=[k2])
            V('pool', lambda e: e.tensor_tensor(out=c3, in0=cC, in1=cI, op=ALU.mult), r=[kC, kI, k3], w=[k3])
            V('pool', lambda e: e.tensor_tensor(out=c2, in0=c2, in1=c3, op=ALU.add), r=[k2, k3], w=[k2])
            so = slice(None) if d == 0 else slice(None, None, -1)
            if d == 0:
                V('act', lambda e: e.activation(out=HRb, in_=c1, func=AF.Identity), r=[k1], w=['hrb'])
                V('act', lambda e: e.activation(out=HIb, in_=c2, func=AF.Identity), r=[k2], w=['hib'])
            else:
                V('act', lambda e: e.activation(out=HRb[:, ::-1], in_=c1, func=AF.Identity), r=[k1], w=['hrb'])
                V('act', lambda e: e.activation(out=HIb[:, ::-1], in_=c2, func=AF.Identity), r=[k2], w=['hib'])
            V('act', lambda e: e.activation(out=ENDR[:, d, gp, so], in_=c1[:, 255:NT:256], func=AF.Identity), r=[k1], w=['endr'])
            V('act', lambda e: e.activation(out=ENDI[:, d, gp, so], in_=c2[:, 255:NT:256], func=AF.Identity), r=[k2], w=['endi'])
            for half in range(2):
                V('pe', lambda e: e.matmul(pb[orows, half * 512:(half + 1) * 512], lhsT=CT[:, 0 + d, ch, ccols], rhs=HRb[:, half * 512:(half + 1) * 512],
                                           start=(d == 0 and gl_ < 3), stop=False), r=['ct', 'hrb'], w=['pb'])
                V('pe', lambda e: e.matmul(pb[orows, half * 512:(half + 1) * 512], lhsT=CT[:, 2 + d, ch, ccols], rhs=HIb[:, half * 512:(half + 1) * 512],
                                           start=False, stop=(d == 1)), r=['ct', 'hib'], w=['pb'])
            if gl_ == 3 and d == 1:
                dcol = 248 + ch
                V('dve', lambda e: e.scalar_tensor_tensor(out=XST, in0=U[:, ch, :], scalar=PAR[:, dcol:dcol + 1], in1=pb[:, :], op0=ALU.mult, op1=ALU.add), r=['u', 'par', 'pb'], w=['xst'])
                V('act', lambda e: e.activation(out=Z[:, ch, :], in_=XST, func=AF.Gelu_apprx_tanh), r=['xst'], w=['z%d' % ch])

        iters = [(ch, gl_, d) for ch in range(8) for gl_ in range(4) for d in range(2)]
        for n, (ch, gl_, d) in enumerate(iters):
            stageA(n, ch, gl_, d)
            if n >= 1:
                stageB(n - 1, *iters[n - 1])
        stageB(len(iters) - 1, *iters[-1])
        t64 = TMPD
        for qq in range(4):
            er = ENDRf.rearrange("p (c q) -> p c q", q=4)[:, :, qq]
            ei = ENDIf.rearrange("p (c q) -> p c q", q=4)[:, :, qq]
            sr = STRf.rearrange("p (c q) -> p c q", q=4)[:, :, qq]
            si = STIf.rearrange("p (c q) -> p c q", q=4)[:, :, qq]
            V('dve', lambda e: e.tensor_tensor(out=t64, in0=ei, in1=KI, op=ALU.mult), r=['endi', 's5p'], w=['t64'])
            V('dve', lambda e: e.tensor_tensor(out=sr, in0=er, in1=KR, op=ALU.mult), r=['endr', 's5p'], w=['st5'])
            V('dve', lambda e: e.tensor_tensor(out=sr, in0=sr, in1=t64, op=ALU.subtract), r=['st5', 't64'], w=['st5'])
            V('dve', lambda e: e.tensor_tensor(out=t64, in0=ei, in1=KR, op=ALU.mult), r=['endi', 's5p'], w=['t64'])
            V('dve', lambda e: e.tensor_tensor(out=si, in0=er, in1=KI, op=ALU.mult), r=['endr', 's5p'], w=['st5'])
            V('dve', lambda e: e.tensor_tensor(out=si, in0=si, in1=t64, op=ALU.add), r=['st5', 't64'], w=['st5'])
        s.barrier()
        def cons_glu(col, ps, pk):
            c = col // 128
            V('act', lambda e: e.activation(out=T3, in_=ps[:, :], func=AF.Sigmoid), r=[pk], w=['t3_0'])
            V('dve', lambda e: e.tensor_tensor(out=YB[:, c, :], in0=T3, in1=Z[:, c, :], op=ALU.mult), r=['t3_0', 'z%d' % c], w=['yb%d' % c])
        dense(s5_w_glu[l], 8, blocks(0, 1024), lambda kc, half: Z[:, kc, half * 512:(half + 1) * 512], lambda kc: ['z%d' % kc], cons_glu)
        dbg('strf', STRf, ['st5'])
        for d in range(2):
            OUTT = Tf[:, base: base + 256].rearrange("p (x r) -> p x r", r=2)
            for ri, src in ((0, STRf), (1, STIf)):
                V('pe', lambda e: e.transpose(out=pc[:, 0:128], in_=src[:, d * 128:(d + 1) * 128], identity=IDF), r=['st5', 'idf'], w=['pc'])
                V('dve', lambda e: e.tensor_copy(out=OUTT[:, :, ri], in_=pc[:, 0:128]), r=['pc'], w=['outt'])
            dbg('outt', OUTT.rearrange("p x r -> p (x r)"), ['outt'])
            s.dma('sp', ns5_out[l, d].rearrange("g q x -> (g q) x"), OUTT.rearrange("p x r -> p (x r)"), reads=['outt'], is_output=True)
        s.barrier()

    def na_branch(l):
        OFFQ, OFFK, OFFV = 3072, 4096, 5120
        QT = Tb[:, 0:8192].rearrange("p (h t) -> p h t", t=NT)
        KT = Tb[:, 8192:16384].rearrange("p (h t) -> p h t", t=NT)
        VT = Tb[:, 16384:24576].rearrange("p (a c) -> p a c", c=1024)
        base = 12288
        BIAS = Tf[:, base: base + 640]
        SP = Tf[:, base + 640: base + 1280]
        Pb = Tb[:, 2 * (base + 1280): 2 * (base + 1280) + 1152]
        PT = Tb[:, 2 * (base + 1856): 2 * (base + 1856) + 1152]
        KCT = Tb[:, 2 * (base + 2432): 2 * (base + 2432) + 512]
        CKh = Tb[:, 2 * (base + 2688): 2 * (base + 2688) + 512].rearrange("p (a c) -> p a c", c=128)
        CVh = Tb[:, 2 * (base + 2944): 2 * (base + 2944) + 512].rearrange("p (a c) -> p a c", c=128)
        STAT = Tf[:, base + 3200: base + 3216]
        OST = Tf[:, base + 3216: base + 3216 + 512]
        assert base + 3728 <= 16384

        def cons_q(col, ps, pk):
            h = (col - OFFQ) // 128
            V('act', lambda e: e.activation(out=QT[:, h, :], in_=ps[:, :], func=AF.Identity, scale=128.0 ** -0.5), r=[pk], w=['qt'])

        def cons_k(col, ps, pk):
            h = (col - OFFK) // 128
            V('act', lambda e: e.activation(out=KT[:, h, :], in_=ps[:, :], func=AF.Identity), r=[pk], w=['kt'])
        dense(w_in[l], 16, blocks(OFFQ, 1024), xn_rhs, xn_keys, cons_q)
        dense(w_in[l], 16, blocks(OFFK, 1024), xn_rhs, xn_keys, cons_k)
        ckpt(7.1)
        wv = w_in[l].rearrange("(k p) n -> p k n", p=128)
        for which, off in ((0, OFFK), (1, OFFV)):
            for cb_ in range(2):
                bi = wcount[0] % 2
                wcount[0] += 1
                wt = Wb[bi].rearrange("p (k n) -> p k n", n=512)
                s.dma('pool', wt, wv[:, :, off + cb_ * 512: off + (cb_ + 1) * 512], writes=['W%d' % bi], fence=False)
                for a in range(8):
                    for kc in range(16):
                        V('pe', lambda e: e.matmul(pc[:, :], lhsT=XN[:, kc, a * 128:(a + 1) * 128], rhs=wt[:, kc, :], start=(kc == 0), stop=(kc == 15)),
                          r=['W%d' % bi, 'xn%d' % kc], w=['pc'])
                    V('act', lambda e: e.activation(out=OST, in_=pc[:, :], func=AF.Identity), r=['pc'], w=['ost'])
                    if which == 1:
                        V('dve', lambda e: e.tensor_copy(out=VT[:, a, cb_ * 512:(cb_ + 1) * 512], in_=OST), r=['ost'], w=['vt'])
                    dst = (nk_out if which == 0 else nv_out)[l, a * 128:(a + 1) * 128, cb_ * 512:(cb_ + 1) * 512]
                    s.dma('sp', dst, OST, reads=['ost'], is_output=True)
                    ckpt(7.15)
                    if which == 0 and cb_ == 1 and a == 7:
                        ckpt(7.16)
                    if which == 0 and cb_ == 0 and a == 3:
                        ckpt(7.155)
        ckpt(7.2)
        V('dve', lambda e: e.memset(BIAS, 0.0), w=['biasA0', 'biasB0'])
        ckv = ck[l].rearrange("(a p) c -> p a c", p=128)
        cvv = cv[l].rearrange("(a p) c -> p a c", p=128)
        W1f = f32v(oW[1], 4096)
        W1b = Wb[1]
        BIAS2 = W1f[:, 0:640]
        SP2 = W1f[:, 640:1280]
        Pb2 = W1b[:, 2 * 1280: 2 * 1280 + 1152]
        STAT2 = W1f[:, 1856:1872]
        CVh2 = W1b[:, 2 * 1872: 2 * 1872 + 512].rearrange("p (a c) -> p a c", c=128)
        V('dve', lambda e: e.memset(BIAS2, 0.0), w=['biasA1', 'biasB1'])
        ABUF = [(BIAS, SP, Pb, STAT), (BIAS2, SP2, Pb2, STAT2)]
        CVS = [CVh, CVh2]

        def head_prep(h):
            s.dma('pool', CKh, ckv[:, :, h * 128:(h + 1) * 128], writes=['ckh'])
            s.dma('pool', CVS[h % 2], cvv[:, :, h * 128:(h + 1) * 128], writes=['cvh%d' % (h % 2)])
            for a in range(4):
                V('pe', lambda e: e.transpose(out=pd[:, a * 128:(a + 1) * 128], in_=CKh[:, a, :], identity=IDB), r=['ckh', 'idb'], w=['pd'])
            V('act', lambda e: e.activation(out=KCT, in_=pd[:, 0:512], func=AF.Identity), r=['pd'], w=['kct'])

        def stageA(n, h, i):
            q = n % 2
            bB, bS, bP, bT = ABUF[q]
            kA, kBb, kS, kP, kT = 'biasA%d' % q, 'biasB%d' % q, 'sp%d' % q, 'p%d' % q, 'stat%d' % q
            lo = min(max(i - 2, 0), 3)
            qs = slice(i * 128, (i + 1) * 128)
            kr0, qr0 = 2 * lo, 2 * i
            for ql in range(2):
                r_first = kr0 - qr0 - ql + 7
                rl = max(0, -r_first)
                rh = min(10, 15 - r_first)
                src = tp[l, h, r_first + rl: r_first + rh].rearrange("r q k -> q r k")
                s.dma('sp', bB[ql * 64:(ql + 1) * 64, rl * 64: rh * 64].rearrange("p (r k) -> p r k", k=64), src, writes=[kA if ql == 0 else kBb])
            V('pe', lambda e: e.matmul(pb[:, 0:512], lhsT=QT[:, h, qs], rhs=KT[:, h, lo * 128: lo * 128 + 512], start=True, stop=True), r=['qt', 'kt'], w=['pb'])
            V('pe', lambda e: e.matmul(pb[:, 512:640], lhsT=QT[:, h, qs], rhs=KT[:, h, lo * 128 + 512: lo * 128 + 640], start=True, stop=True), r=['qt', 'kt'], w=['pb'])
            V('pe', lambda e: e.matmul(pc[:, :], lhsT=QT[:, h, qs], rhs=KCT, start=True, stop=True), r=['qt', 'kct'], w=['pc'])
            V('dve', lambda e: e.tensor_tensor(out=bS, in0=pb[:, 0:640], in1=bB, op=ALU.add), r=['pb', kA, kBb], w=[kS])
            V('dve', lambda e: e.tensor_tensor(out=bS, in0=bS, in1=MASK[:, i, :], op=ALU.add), r=[kS, 'mask'], w=[kS])
            V('dve', lambda e: e.tensor_reduce(out=bT[:, 0:1], in_=bS, axis=AX.X, op=ALU.max), r=[kS], w=[kT])
            V('dve', lambda e: e.tensor_reduce(out=bT[:, 1:2], in_=pc[:, :], axis=AX.X, op=ALU.max), r=['pc'], w=[kT])
            V('dve', lambda e: e.scalar_tensor_tensor(out=bT[:, 2:3], in0=bT[:, 1:2], scalar=CB, in1=bT[:, 0:1], op0=ALU.add, op1=ALU.max), r=[kT, 'flags'], w=[kT])
            V('dve', lambda e: e.tensor_scalar(out=bT[:, 3:4], in0=bT[:, 2:3], scalar1=-1.0, scalar2=None, op0=ALU.mult), r=[kT], w=[kT])
            V('dve', lambda e: e.tensor_tensor(out=bT[:, 4:5], in0=bT[:, 3:4], in1=CB, op=ALU.add), r=[kT, 'flags'], w=[kT])
            V('act', lambda e: e.activation(out=bP[:, 0:640], in_=bS, func=AF.Exp, bias=bT[:, 3:4], accum_out=bT[:, 5:6]), r=[kS, kT], w=[kP, kT])
            V('act', lambda e: e.activation(out=bP[:, 640:1152], in_=pc[:, :], func=AF.Exp, bias=bT[:, 4:5], accum_out=bT[:, 6:7]), r=['pc', kT], w=[kP, kT])
            V('dve', lambda e: e.tensor_tensor(out=bT[:, 7:8], in0=bT[:, 5:6], in1=bT[:, 6:7], op=ALU.add), r=[kT], w=[kT])
            V('dve', lambda e: e.reciprocal(out=bT[:, 7:8], in_=bT[:, 7:8]), r=[kT], w=[kT])
            V('dve', lambda e: e.tensor_scalar(out=bP, in0=bP, scalar1=bT[:, 7:8], scalar2=None, op0=ALU.mult), r=[kP, kT], w=[kP])

        def stageB(n, h, i):
            q = n % 2
            bB, bS, bP, bT = ABUF[q]
            kP = 'p%d' % q
            lo = min(max(i - 2, 0), 3)
            qs = slice(i * 128, (i + 1) * 128)
            for j0, nj in ((0, 5), (5, 4)):
                for j in range(nj):
                    V('pe', lambda e: e.transpose(out=pd[:, j * 128:(j + 1) * 128], in_=bP[:, (j0 + j) * 128:(j0 + j + 1) * 128], identity=IDB), r=[kP, 'idb'], w=['pd'])
                V('act', lambda e: e.activation(out=PT[:, j0 * 128:(j0 + nj) * 128], in_=pd[:, 0:nj * 128], func=AF.Identity), r=['pd'], w=['pt'])
            for j in range(9):
                if j < 5:
                    lhs = VT[:, lo + j, h * 128:(h + 1) * 128]
                    rk = ['vt']
                else:
                    lhs = CVS[h % 2][:, j - 5, :]
                    rk = ['cvh%d' % (h % 2)]
                V('pe', lambda e: e.matmul(pa[0][:, 0:128], lhsT=lhs, rhs=PT[:, j * 128:(j + 1) * 128], start=(j == 0), stop=(j == 8)), r=rk + ['pt'], w=['pa0'])
            V('act', lambda e: e.activation(out=YB[:, h, qs], in_=pa[0][:, 0:128], func=AF.Identity), r=['pa0'], w=['yb%d' % h])

        tiles = [(h, i) for h in range(8) for i in range(8)]
        for n, (h, i) in enumerate(tiles):
            if i == 0:
                head_prep(h)
            stageA(n, h, i)
            if n >= 1:
                stageB(n - 1, *tiles[n - 1])
        stageB(len(tiles) - 1, *tiles[-1])
        s.barrier()

    def ffn(l):
        def cons_a(col, ps, pk):
            c = col // 128
            V('act', lambda e: e.activation(out=HID[:, c, :], in_=ps[:, :], func=AF.Silu), r=[pk], w=['hid%d' % c])

        def cons_b(col, ps, pk):
            c = (col - FF) // 128
            V('dve', lambda e: e.tensor_tensor(out=HID[:, c, :], in0=ps[:, :], in1=HID[:, c, :], op=ALU.mult), r=[pk, 'hid%d' % c], w=['hid%d' % c])
        for blk in range(11):
            dense(w_ffn_in[l], 16, [(blk * 512, 512)], xn_rhs, xn_keys, cons_a)
            dense(w_ffn_in[l], 16, [(FF + blk * 512, 512)], xn_rhs, xn_keys, cons_b)
        s.barrier(skip=('pool',))
        wv = w_ffn_out[l].rearrange("(k p) n -> p k n", p=128)
        for c in range(16):
            bi = wcount[0] % 2
            wcount[0] += 1
            wt = Wb[bi][:, 0:44 * 128].rearrange("p (k n) -> p k n", n=128)
            s.dma('pool', wt, wv[:, :, c * 128:(c + 1) * 128], writes=['W%d' % bi], fence=False)
            pi = dense.pcount % 2
            dense.pcount += 1
            ps = pa[pi]
            for half in range(2):
                for kc in range(44):
                    V('pe', lambda e: e.matmul(ps[:, half * 512:(half + 1) * 512], lhsT=wt[:, kc, :],
                                               rhs=HID[:, kc, half * 512:(half + 1) * 512], start=(kc == 0), stop=(kc == 43)),
                      r=['W%d' % bi, 'hid%d' % kc], w=['pa%d' % pi])
            V('act', lambda e: e.activation(out=XN[:, c, :], in_=ps[:, :], func=AF.Identity), r=['pa%d' % pi], w=['xn%d' % c])
        s.barrier(skip=('pool',))

    try:
      ckpt(0)
      for l in range(NL):
        load_params(l)
        ckpt(1)
        modulation(l)
        s.barrier(skip=('pool',))
        ckpt(2)
        norm_to_xn(352, 256)
        ckpt(3)
        s5_branch(l)
        ckpt(4)
        merge_branch(l, 0, 6144 + 1 * 2048)
        ckpt(5)
        rg_branch(l)
        ckpt(6)
        merge_branch(l, 1, 6144 + 0 * 2048)
        ckpt(7)
        na_branch(l)
        ckpt(8)
        merge_branch(l, 2, 6144 + 2 * 2048)
        ckpt(9)
        def cons_o(col, ps, pk):
            c = col // 128
            V('act', lambda e: e.activation(out=XN[:, c, :], in_=ps[:, :], func=AF.Identity), r=[pk], w=['xn%d' % c])
        dense(w_o[l], 16, blocks(0, D), lambda kc, half: Mb[:, kc, half * 512:(half + 1) * 512], lambda kc: ['m%d' % kc], cons_o)
        s.barrier(skip=('pool',))
        ckpt(10)
        post_norm_residual(384)
        ckpt(11)
        norm_to_xn(368, 256 + 48)
        ffn(l)
        ckpt(12)
        post_norm_residual(400)
    except _Stop:
        pass

    for a in range(8):
        for hh in range(2):
            stage = Tf[:, 0:1024].rearrange("p (c t) -> p c t", t=128)
            s.dma('sp', stage, xs[hh * 8:(hh + 1) * 8, :, a * 128:(a + 1) * 128].rearrange("c p t -> p c t"), reads=['xs'], writes=['t_stage'])
            for cc in range(8):
                V('pe', lambda e: e.transpose(out=pb[:, cc * 128:(cc + 1) * 128], in_=stage[:, cc, :], identity=IDF), r=['t_stage', 'idf'], w=['pb'])
            outst = Tf[:, 1024:2048]
            V('act', lambda e: e.activation(out=outst, in_=pb[:, :], func=AF.Identity), r=['pb'], w=['t_out'])
            s.dma('sp', y_out[a * 128:(a + 1) * 128, hh * 1024:(hh + 1) * 1024], outst, reads=['t_out'], is_output=True)
    s.finish()
    return nc, s


def _na_mask_sample():
    m = np.full((8, 128, 640), NEG, np.float32)
    for i in range(8):
        lo = min(max(i - 2, 0), 3)
        for ql in range(2):
            qr = 2 * i + ql
            rs = min(max(qr - 4, 0), 8)
            for qc in range(64):
                ws = min(max(qc - 8, 0), 48)
                for krel in range(10):
                    kr = 2 * lo + krel
                    if rs <= kr < rs + 8:
                        m[i, ql * 64 + qc, krel * 64 + ws: krel * 64 + ws + 16] = 0.0
    return m


def _na_mask_prompt():
    m = np.full((8, 128, 640), NEG, np.float32)
    for i in range(8):
        lo = min(max(i - 2, 0), 3)
        seq = i // 2
        for j in range(5):
            if (lo + j) // 2 == seq:
                m[i, :, j * 128:(j + 1) * 128] = 0.0
    return m


def make_in_maps(inp, cores=range(8), L=DEPTH):
    f = lambda a: np.ascontiguousarray(np.asarray(a, dtype=np.float32))
    I = {k: f(v) for k, v in inp.items()}
    shared = {}
    shared["w_mod"] = I["w_mod"]
    shared["b_mod"] = I["b_mod"].reshape(L, 96, 128)
    shared["gvec"] = f(np.concatenate([I["g_mix_pre"].reshape(L, 16, 128), I["g_mix_post"].reshape(L, 16, 128),
                                       I["g_ffn_pre"].reshape(L, 16, 128), I["g_ffn_post"].reshape(L, 16, 128)], axis=1))
    shared["w_in"] = I["w_in"]
    shared["rgv"] = f(np.concatenate([I["rg_conv_w"].reshape(L, 32, 128), I["rg_conv_b"].reshape(L, 8, 128),
                                      I["rg_gate_b"].reshape(L, 32, 128), I["rg_lambda"].reshape(L, 16, 128),
                                      I["s5_d"].reshape(L, 8, 128)], axis=1))
    shared["rg_gate_w"] = I["rg_gate_w"].reshape(L, 32, 128, 128)
    shared["s5_lam"] = f(np.concatenate([I["s5_lambda_re"].reshape(L, 64, 128), I["s5_lambda_im"].reshape(L, 64, 128)], axis=1))
    shared["s5_ldt"] = I["s5_log_dt"].reshape(L, 64, 2)
    shared["s5_b"] = f(np.stack([I["s5_b_re"], I["s5_b_im"]], axis=1))
    shared["s5_c"] = f(np.stack([I["s5_c_re"], I["s5_c_im"]], axis=1))
    shared["s5_w_glu"] = I["s5_w_glu"]
    shared["w_bo"] = f(np.stack([I["w_s5_out"], I["w_rg_out"], I["w_na_out"]], axis=1))
    shared["w_o"] = I["w_o"]
    shared["w_ffn_in"] = I["w_ffn_in"]
    shared["w_ffn_out"] = I["w_ffn_out"]
    shared["ident"] = np.eye(128, dtype=np.float32)
    shared["iota1"] = f(np.broadcast_to(np.arange(1, NT + 1, dtype=np.float32), (128, NT)))
    rpb = I["na_rpb"]
    qc = np.arange(64)[:, None]
    kc = np.arange(64)[None, :]
    cidx = np.clip(kc - qc, -15, 15) + 15
    tp_s = f(rpb[:, :, :, cidx])
    tp_p = np.zeros_like(tp_s)
    mask_s = _na_mask_sample()
    mask_p = _na_mask_prompt()
    maps = []
    for core in cores:
        m = dict(shared)
        if core < 4:
            m["xin"] = f(I["x_prompt"][4 * core:4 * core + 4].reshape(NT, D))
            m["cond"] = I["c_ctx"].reshape(16, 128)
            m["tp"] = tp_p
            m["ck"] = np.zeros((L, 512, 1024), np.float32)
            m["cv"] = np.zeros((L, 512, 1024), np.float32)
            m["st_rg"] = np.zeros((L, 16, 128), np.float32)
            m["st_s5"] = np.zeros((L, 64, 256), np.float32)
            m["amask"] = mask_p
            fl = np.zeros((128, 4), np.float32)
            fl[:, 1] = NEG
            fl[:, 2] = -1.0
        else:
            b = core - 4
            m["xin"] = f(I["x_sample"][b])
            m["cond"] = I["c"][b].reshape(16, 128)
            m["tp"] = tp_s
            m["ck"] = f(I["cache_na_k"][b].reshape(L, 512, 1024))
            m["cv"] = f(I["cache_na_v"][b].reshape(L, 512, 1024))
            m["st_rg"] = f(I["state_rglru"][b].reshape(L, 16, 128))
            m["st_s5"] = f(I["state_s5"][b].reshape(L, 64, 256))
            m["amask"] = mask_s
            fl = np.zeros((128, 4), np.float32)
            fl[:, 0] = 1.0
        m["flags"] = fl
        maps.append(m)
    return maps


_CACHE = {}


def kernel(**inputs):
    if "nc" not in _CACHE:
        _CACHE["nc"] = build(DEPTH)[0]
    nc = _CACHE["nc"]
    maps = make_in_maps(inputs)
    res = run_bass_kernel_spmd(nc, maps, core_ids=list(range(8)))
    R = res.results
    L = DEPTH
    y_prompt = np.concatenate([R[c]["y"].reshape(4, 256, D) for c in range(4)], axis=0)
    y_sample = np.stack([R[c]["y"] for c in range(4, 8)], axis=0)
    nk = np.concatenate([R[c]["nk"].reshape(L, 4, 256, 8, 128).transpose(1, 0, 2, 3, 4) for c in range(4)], axis=0)
    nv = np.concatenate([R[c]["nv"].reshape(L, 4, 256, 8, 128).transpose(1, 0, 2, 3, 4) for c in range(4)], axis=0)
    nrg = np.concatenate([R[c]["nrg"].reshape(L, 2, 4, 1024).transpose(2, 0, 1, 3) for c in range(4)], axis=0)
    ns5 = np.concatenate([R[c]["ns5"].reshape(L, 2, 32, 4, 2, 64, 2).transpose(3, 0, 1, 2, 4, 5, 6).reshape(4, L, 2, 64, 64, 2)
                          for c in range(4)], axis=0)
    f = lambda a: np.ascontiguousarray(a, dtype=np.float32)
    return (f(y_prompt), f(y_sample), f(nk), f(nv), f(nrg), f(ns5))
```
